# Optimizing a Trainium2 kernel written in Bass

```python
import math
import jax, jax.numpy as jnp
from jax import lax
import numpy as np

D_MODEL = 2048
BATCH = 4
SEQ = 2048
DEPTH = 1
DEC_BATCH = 128
DEC_SEQ = 4
PAST_LEN = 16384
PAGE_SIZE = 128

N_META = 16
D_FF = 256 * ((8 * D_MODEL // 3 + 255) // 256)
W_POOL = D_MODEL // 2
W_SSM = D_MODEL // 2
POOL_WINDOWS = (2, 4, 8, 16)
N_POOL_GROUPS = len(POOL_WINDOWS)
POOL_GW = W_POOL // N_POOL_GROUPS
POOL_HIST = max(POOL_WINDOWS) - 1
SSM_GS = 16
N_SSM_GROUPS = W_SSM // SSM_GS
SSM_STATE = 64
W_IN = W_POOL + W_SSM + 2 * D_MODEL
RMS_EPS = 1e-6

kernel_name = "gated_pool_s5_macaron_decoder_step"


def rmsnorm(x, g):
    xf = x.astype(jnp.float32)
    r = lax.rsqrt(jnp.mean(xf * xf, axis=-1, keepdims=True) + RMS_EPS)
    return (xf * r * g.astype(jnp.float32)).astype(x.dtype)


def swiglu(x, w_gate, w_up, w_down):
    return (jax.nn.silu(x @ w_gate) * (x @ w_up)) @ w_down


def pool_mix(u, hist, pos0, w_grp, scale):
    nb, L, _ = u.shape
    full = jnp.concatenate([hist.astype(u.dtype), u], axis=1)
    cs = jnp.cumsum(full.astype(jnp.float32), axis=1)
    cs = jnp.pad(cs, ((0, 0), (1, 0), (0, 0)))
    avail = jnp.arange(L, dtype=jnp.int32) + (pos0 + 1)
    means = []
    for g, w in enumerate(POOL_WINDOWS):
        sl = slice(g * POOL_GW, (g + 1) * POOL_GW)
        s = cs[:, POOL_HIST + 1:POOL_HIST + 1 + L, sl] - cs[:, POOL_HIST + 1 - w:POOL_HIST + 1 - w + L, sl]
        cnt = jnp.minimum(avail, w).astype(jnp.float32)[None, :, None]
        means.append(s / cnt)
    d = jnp.concatenate(means, axis=-1) - u.astype(jnp.float32)
    d = d.reshape(nb, L, N_POOL_GROUPS, POOL_GW)
    y = jnp.einsum('blgc,gcd->blgd', d, w_grp.astype(jnp.float32)).reshape(nb, L, W_POOL)
    y = y * scale.astype(jnp.float32)
    return y.astype(u.dtype), full[:, -POOL_HIST:]


def s5_scan(u, h0_re, h0_im, lam_re, lam_im, log_dt, b_re, b_im, c_re, c_im, d_skip):
    nb, L, _ = u.shape
    f32 = jnp.float32
    lam = lax.complex(lam_re.astype(f32), lam_im.astype(f32))
    dt = jnp.exp(log_dt.astype(f32))[:, None]
    lam_bar = jnp.exp(lam * dt)
    b_mat = lax.complex(b_re.astype(f32), b_im.astype(f32))
    b_bar = ((lam_bar - 1.0) / lam)[..., None] * b_mat
    c_mat = lax.complex(c_re.astype(f32), c_im.astype(f32))
    ug = u.astype(f32).reshape(nb, L, N_SSM_GROUPS, SSM_GS)
    bu = jnp.einsum('blgc,gpc->blgp', ug.astype(jnp.complex64), b_bar)
    h0 = lax.complex(h0_re.astype(f32), h0_im.astype(f32))
    bu = bu.at[:, 0].add(lam_bar[None] * h0)
    a = jnp.broadcast_to(lam_bar[None, None], (1, L, N_SSM_GROUPS, SSM_STATE))

    def combine(left, right):
        a1, b1 = left
        a2, b2 = right
        return a1 * a2, a2 * b1 + b2

    _, h = lax.associative_scan(combine, (a, bu), axis=1)
    y = jnp.einsum('blgp,gcp->blgc', h, c_mat).real
    y = y + d_skip.astype(f32).reshape(N_SSM_GROUPS, SSM_GS) * ug
    h_last = h[:, -1]
    return y.reshape(nb, L, W_SSM).astype(u.dtype), jnp.real(h_last), jnp.imag(h_last)


def setup_inputs(seed: int = 0) -> dict:
    key = jax.random.key(seed)
    ks = iter(jax.random.split(key, 40))
    f32 = jnp.float32

    def nrm(shape, scale):
        return jax.random.normal(next(ks), shape, f32) * scale

    def gain(shape):
        return 1.0 + 0.05 * jax.random.normal(next(ks), shape, f32)

    L_ = DEPTH
    n_idx = jnp.arange(SSM_STATE, dtype=f32)
    inp = {}
    inp["x_prompt"] = nrm((BATCH, SEQ, D_MODEL), 1.0)
    inp["x_sample"] = nrm((DEC_BATCH, DEC_SEQ, D_MODEL), 1.0)
    inp["state_pool"] = nrm((L_, DEC_BATCH, POOL_HIST, W_POOL), 1.0)
    inp["state_ssm_re"] = nrm((L_, DEC_BATCH, N_SSM_GROUPS, SSM_STATE), 0.3)
    inp["state_ssm_im"] = nrm((L_, DEC_BATCH, N_SSM_GROUPS, SSM_STATE), 0.3)
    inp["meta_tokens"] = nrm((N_META, D_MODEL), 1.0)
    inp["norm_ffn1"] = gain((L_, D_MODEL))
    inp["ffn1_w_gate"] = nrm((L_, D_MODEL, D_FF), D_MODEL ** -0.5)
    inp["ffn1_w_up"] = nrm((L_, D_MODEL, D_FF), D_MODEL ** -0.5)
    inp["ffn1_w_down"] = nrm((L_, D_FF, D_MODEL), D_FF ** -0.5)
    inp["norm_mix"] = gain((L_, D_MODEL))
    inp["w_in"] = nrm((L_, D_MODEL, W_IN), D_MODEL ** -0.5)
    inp["b_gate"] = nrm((L_, 2 * D_MODEL), 0.02)
    inp["pool_w"] = nrm((L_, N_POOL_GROUPS, POOL_GW, POOL_GW), POOL_GW ** -0.5)
    inp["pool_scale"] = gain((L_, W_POOL))
    inp["ssm_lambda_re"] = -0.5 + nrm((L_, N_SSM_GROUPS, SSM_STATE), 0.01)
    inp["ssm_lambda_im"] = math.pi * n_idx + nrm((L_, N_SSM_GROUPS, SSM_STATE), 0.01)
    inp["ssm_log_dt"] = jax.random.uniform(next(ks), (L_, N_SSM_GROUPS), f32,
                                           math.log(1e-3), math.log(1e-1))
    inp["ssm_b_re"] = nrm((L_, N_SSM_GROUPS, SSM_STATE, SSM_GS), (2 * SSM_GS) ** -0.5)
    inp["ssm_b_im"] = nrm((L_, N_SSM_GROUPS, SSM_STATE, SSM_GS), (2 * SSM_GS) ** -0.5)
    inp["ssm_c_re"] = nrm((L_, N_SSM_GROUPS, SSM_GS, SSM_STATE), (2 * SSM_STATE) ** -0.5)
    inp["ssm_c_im"] = nrm((L_, N_SSM_GROUPS, SSM_GS, SSM_STATE), (2 * SSM_STATE) ** -0.5)
    inp["ssm_d"] = nrm((L_, W_SSM), 1.0)
    inp["glu_w"] = nrm((L_, W_SSM, W_SSM), W_SSM ** -0.5)
    inp["glu_b"] = nrm((L_, W_SSM), 0.02)
    inp["w_branch_a"] = nrm((L_, W_POOL, D_MODEL), W_POOL ** -0.5)
    inp["w_branch_b"] = nrm((L_, W_SSM, D_MODEL), W_SSM ** -0.5)
    inp["w_out"] = nrm((L_, D_MODEL, D_MODEL), D_MODEL ** -0.5)
    inp["norm_ffn2"] = gain((L_, D_MODEL))
    inp["ffn2_w_gate"] = nrm((L_, D_MODEL, D_FF), D_MODEL ** -0.5)
    inp["ffn2_w_up"] = nrm((L_, D_MODEL, D_FF), D_MODEL ** -0.5)
    inp["ffn2_w_down"] = nrm((L_, D_FF, D_MODEL), D_FF ** -0.5)
    inp["final_norm"] = gain((D_MODEL,))
    return inp


def reference(x_prompt, x_sample, state_pool, state_ssm_re, state_ssm_im, meta_tokens,
              norm_ffn1, ffn1_w_gate, ffn1_w_up, ffn1_w_down,
              norm_mix, w_in, b_gate, pool_w, pool_scale,
              ssm_lambda_re, ssm_lambda_im, ssm_log_dt, ssm_b_re, ssm_b_im, ssm_c_re, ssm_c_im,
              ssm_d, glu_w, glu_b, w_branch_a, w_branch_b, w_out,
              norm_ffn2, ffn2_w_gate, ffn2_w_up, ffn2_w_down, final_norm):

    def mixer(xn, l, hist, h_re, h_im, pos0):
        z = xn @ w_in[l]
        u_pool = z[..., :W_POOL]
        u_ssm = z[..., W_POOL:W_POOL + W_SSM]
        gates = jax.nn.sigmoid(z[..., W_POOL + W_SSM:] + b_gate[l])
        gate_a = gates[..., :D_MODEL]
        gate_b = gates[..., D_MODEL:]
        a_out, hist_new = pool_mix(u_pool, hist, pos0, pool_w[l], pool_scale[l])
        s_out, hre_new, him_new = s5_scan(u_ssm, h_re, h_im, ssm_lambda_re[l], ssm_lambda_im[l],
                                          ssm_log_dt[l], ssm_b_re[l], ssm_b_im[l],
                                          ssm_c_re[l], ssm_c_im[l], ssm_d[l])
        s_out = jax.nn.gelu(s_out)
        s_out = s_out * jax.nn.sigmoid(s_out @ glu_w[l] + glu_b[l])
        merged = gate_a * (a_out @ w_branch_a[l]) + gate_b * (s_out @ w_branch_b[l])
        return merged @ w_out[l], hist_new, hre_new, him_new

    def run_group(x, hists, hres, hims, pos0):
        new_h, new_re, new_im = [], [], []
        for l in range(DEPTH):
            x = x + 0.5 * swiglu(rmsnorm(x, norm_ffn1[l]), ffn1_w_gate[l], ffn1_w_up[l], ffn1_w_down[l])
            m, hn, rn, im_ = mixer(rmsnorm(x, norm_mix[l]), l, hists[l], hres[l], hims[l], pos0)
            x = x + m
            x = x + 0.5 * swiglu(rmsnorm(x, norm_ffn2[l]), ffn2_w_gate[l], ffn2_w_up[l], ffn2_w_down[l])
            new_h.append(hn)
            new_re.append(rn)
            new_im.append(im_)
        y = rmsnorm(x, final_norm)
        return y, jnp.stack(new_h, 0), jnp.stack(new_re, 0), jnp.stack(new_im, 0)

    nbp = x_prompt.shape[0]
    meta = jnp.broadcast_to(meta_tokens.astype(x_prompt.dtype)[None], (nbp, N_META, D_MODEL))
    xp = jnp.concatenate([meta, x_prompt], axis=1)
    zero_hist = jnp.zeros((DEPTH, nbp, POOL_HIST, W_POOL), x_prompt.dtype)
    zero_h = jnp.zeros((DEPTH, nbp, N_SSM_GROUPS, SSM_STATE), jnp.float32)
    yp_full, pool_p, ssm_re_p, ssm_im_p = run_group(xp, zero_hist, zero_h, zero_h, 0)
    y_prompt = yp_full[:, N_META:]

    y_sample, pool_s, ssm_re_s, ssm_im_s = run_group(x_sample, state_pool, state_ssm_re,
                                                     state_ssm_im, PAST_LEN)
    return (y_prompt, y_sample, pool_p, pool_s, ssm_re_p, ssm_im_p, ssm_re_s, ssm_im_s)
```

```python
import math
from contextlib import ExitStack
import numpy as np
import concourse.bass as bass
import concourse.mybir as mybir
from concourse.bass_utils import run_bass_kernel_spmd

F32 = mybir.dt.float32
BF16 = mybir.dt.bfloat16
AF = mybir.ActivationFunctionType
ALU = mybir.AluOpType

D = 2048
DFF = 5632
NT = 383
OWN0 = 15
OWN = 344
SMP0 = 359
NSL = 6
NCH = 43
NCOL = NCH + NSL
TWO_PI = 2.0 * math.pi


class Buf:
    def __init__(self, name):
        self.name = name
        self.lw = None
        self.rd = {}


class Op:
    pass


class Sched:
    ENG = ['pe', 'act', 'dve', 'pool', 'sp']

    def __init__(self):
        self.q = {e: [] for e in self.ENG}
        self.dma_cnt = {}
        self.dma_ops = {}

    def add(self, eng, fn, reads=(), writes=(), dkey=None, inc=16):
        o = Op()
        o.eng = eng
        o.fn = fn
        o.dkey = dkey
        o.inc = inc
        o.sig = False
        o.deps = []
        ds = []
        for b in reads:
            if b.lw is not None:
                ds.append(b.lw)
        for b in writes:
            if b.lw is not None:
                ds.append(b.lw)
            ds.extend(b.rd.values())
        rk = eng if dkey is None else ('d', dkey)
        for b in reads:
            b.rd[rk] = o
        for b in writes:
            b.lw = o
            b.rd = {}
        seen = set()
        for d in ds:
            if d is o or id(d) in seen:
                continue
            seen.add(id(d))
            if d.dkey is None and d.eng == 'pe' and eng == 'pe' and dkey is None:
                continue
            o.deps.append(d)
            if d.dkey is None:
                d.sig = True
        if dkey is not None:
            self.dma_cnt[dkey] = self.dma_cnt.get(dkey, 0) + inc
            o.dval = self.dma_cnt[dkey]
            self.dma_ops.setdefault(dkey, []).append(o)
        self.q[eng].append(o)
        return o

    def finalize_key(self, key):
        for o in self.dma_ops.get(key, []):
            o.dval = self.dma_cnt[key]

    def emit(self, nc, tag, sem_es):
        with ExitStack() as es:
            sems = {}
            for e in ['pe', 'act', 'dve', 'pool', 'sp']:
                sems[('e', e)] = sem_es.enter_context(nc.semaphore(f"{tag}_e_{e}"))
            for k in self.dma_cnt:
                sems[('d', k)] = sem_es.enter_context(nc.semaphore(f"{tag}_d_{k}"))
            final = {}
            for e in self.ENG:
                c = 0
                for o in self.q[e]:
                    if o.dkey is None and o.sig:
                        c += 1
                        o.sval = c
                final[('e', e)] = c
            for k, v in self.dma_cnt.items():
                final[('d', k)] = v
            block = es.enter_context(nc.Block())

            def run(engname, eng):
                waited = {}
                for o in self.q[engname]:
                    need = {}
                    for d in o.deps:
                        if d.dkey is None:
                            key = ('e', d.eng)
                            val = d.sval
                        else:
                            key = ('d', d.dkey)
                            val = d.dval
                        if val > need.get(key, 0):
                            need[key] = val
                    for key, val in need.items():
                        if waited.get(key, 0) >= val:
                            continue
                        waited[key] = val
                        eng.wait_ge(sems[key], val)
                    ins = o.fn(eng)
                    if o.dkey is not None:
                        ins.then_inc(sems[('d', o.dkey)], o.inc)
                    elif o.sig:
                        ins.then_inc(sems[('e', o.eng)], 1)
                for key, val in final.items():
                    if val > 0 and waited.get(key, 0) < val:
                        eng.wait_ge(sems[key], val)

            @block.tensor
            def _(eng):
                run('pe', eng)

            @block.scalar
            def _(eng):
                run('act', eng)

            @block.vector
            def _(eng):
                run('dve', eng)

            @block.gpsimd
            def _(eng):
                run('pool', eng)

            @block.sync
            def _(eng):
                run('sp', eng)


def bc_last(ap, n):
    return ap.unsqueeze(2).broadcast_to([ap.shape[0], ap.shape[1], n])


def build_program(stage="full"):
    nc = bass.Bass("TRN2", target_bir_lowering=False)
    sem_es = ExitStack()

    def din(name, shape):
        return nc.dram_tensor(name, list(shape), F32, kind="ExternalInput").ap()

    def dout(name, shape):
        return nc.dram_tensor(name, list(shape), F32, kind="ExternalOutput").ap()

    xin = din("xin", [6, D, NT])
    hist_in = din("hist_in", [3, 1024, NSL, 15])
    sh0_in = din("sh0_in", [3, 128, 64, NSL])
    wh0_in = din("wh0_in", [3, 128, 64, NSL])
    w_g1 = din("w_g1", [44, 128, D]); w_u1 = din("w_u1", [44, 128, D]); w_d1 = din("w_d1", [16, 128, DFF])
    w_g2 = din("w_g2", [44, 128, D]); w_u2 = din("w_u2", [44, 128, D]); w_d2 = din("w_d2", [16, 128, DFF])
    w_in = din("w_in", [48, 128, D])
    w_pool = din("w_pool", [4, 256, 256])
    w_glu = din("w_glu", [1024, 1024])
    w_ba = din("w_ba", [16, 128, 1024]); w_bb = din("w_bb", [16, 128, 1024]); w_o = din("w_o", [D, D])
    gains = din("gains", [128, 4, 16])
    bgate = din("bgate", [128, 32])
    glub = din("glub", [128, 8])
    pscale = din("pscale", [128, 8])
    lr2 = din("lr2", [128, 64]); li2 = din("li2", [128, 64]); ldt = din("ldt", [128, 64])
    sb_in = din("sb_in", [128, 64, 16]); wb_in = din("wb_in", [128, 64, 16])
    n1_in = din("n1_in", [128, 64, 16]); n2_in = din("n2_in", [128, 64, 16])
    dd_in = din("dd_in", [128, 64])
    mask_in = din("mask_in", [128, 128])
    ident_in = din("ident_in", [128, 128])

    yT = dout("yT", [3, D, 368])
    pool_p = dout("pool_p", [1024, 15])
    pool_s = dout("pool_s", [3, 1024, NSL, 15])
    ssm_p = dout("ssm_p", [128, 64])
    ssm_s = dout("ssm_s", [3, 128, 64, NSL])

    ms = nc.dram_tensor("ms_scr", [4, 128, 8192], BF16, kind="ExternalOutput").ap()
    tb = nc.dram_tensor("tb_scr", [4, 128, 64], F32, kind="ExternalOutput").ap()
    XS = nc.dram_tensor("xs_scr", [3, 128, 16 * NT], F32).ap()
    US = nc.dram_tensor("us_scr", [3, 128, 64 * NCOL], BF16).ap()
    GS = nc.dram_tensor("gs_scr", [3, 128, 64 * NCOL], F32).ap()
    WS_ = nc.dram_tensor("ws_scr", [3, 128, 64 * NCOL], BF16).ap()
    WC = nc.dram_tensor("wc_scr", [256, 128, 4096], BF16).ap()
    cc_in = nc.dram_tensor("cc_in", [128, 512], F32)
    cc_out = nc.dram_tensor("cc_out", [128, 512], F32)

    with ExitStack() as es:
        S = Sched()

        def sb(name, shape, dt=F32):
            return es.enter_context(nc.sbuf_tensor(name, list(shape), dt))

        LR = sb("s_lr", [128, 64]); LI = sb("s_li", [128, 64]); LDT = sb("s_ldt", [128, 64])
        SB0 = sb("s_sb", [128, 64, 16]); WB0 = sb("s_wb", [128, 64, 16])
        N1 = sb("s_n1", [128, 64, 16]); N2 = sb("s_n2", [128, 64, 16])
        SBb = sb("s_sbb", [128, 64, 16]); WBb = sb("s_wbb", [128, 64, 16])
        DDt = sb("s_dd", [128, 64]); MASK = sb("s_mask", [128, 128]); IDF = sb("s_id", [128, 128])
        DT = sb("s_dt", [128, 64]); DLR = sb("s_dlr", [128, 64]); DLI = sb("s_dli", [128, 64])
        TR = sb("s_tr", [128, 9, 64]); TI = sb("s_ti", [128, 9, 64])
        TRn = sb("s_trn", [128, 9, 64]); TIn = sb("s_tin", [128, 9, 64])
        MAG = sb("s_mag", [128, 9, 64]); MAGn = sb("s_magn", [128, 9, 64])
        SN = sb("s_sn", [128, 9, 64]); CS = sb("s_cs", [128, 9, 64])
        R1 = sb("s_r1", [128, 9, 64]); R2 = sb("s_r2", [128, 9, 64])
        NI = sb("s_ni", [128, 64], mybir.dt.int32)
        QA = sb("s_qa", [128, 64]); QB = sb("s_qb", [128, 64]); QC = sb("s_qc", [128, 64])
        QR = sb("s_qr", [128, 64]); QI = sb("s_qi", [128, 64])
        T16a = sb("s_t16a", [128, 64, 16]); T16b = sb("s_t16b", [128, 64, 16])
        BNp = sb("s_bnp", [128, 64, 8, 16]); BPs = sb("s_bps", [128, 64, 8, 16])
        WBPs = sb("s_wbps", [128, 64, 8, 16]); M3f = sb("s_m3f", [128, 64, 8, 16])
        MO = [sb(f"s_mo{i}", [128, 16, 128], BF16) for i in range(4)]
        TMPM = [sb(f"s_tmpm{i}", [128, 128]) for i in range(2)]
        PSs = [es.enter_context(nc.psum_tensor(f"s_ps{i}", [128, 512], F32)) for i in range(8)]

        b_const = Buf("const")
        bT = {n: Buf(n) for n in ["DT", "DLR", "DLI", "TR", "TI", "TRn", "TIn", "MAG", "MAGn", "SN", "CS",
                                  "R1", "R2", "QA", "QB", "QC", "QR", "QI", "T16a", "T16b", "SBb", "WBb",
                                  "BNp", "BPs", "WBPs", "M3f", "N1", "N2", "WB0"]}
        bMO = [Buf(f"MO{i}") for i in range(4)]
        bTMPM = [Buf(f"TMPM{i}") for i in range(2)]
        bPS = [Buf(f"sps{i}") for i in range(8)]

        def ld(dst, src):
            b_const.lw = S.add('sp', lambda e, dst=dst, src=src: e.dma_start(out=dst, in_=src), dkey="const")

        ld(LR[:], lr2); ld(LI[:], li2); ld(LDT[:], ldt)
        ld(SB0[:], sb_in); ld(WB0[:], wb_in); ld(N1[:], n1_in); ld(N2[:], n2_in)
        ld(DDt[:], dd_in); ld(MASK[:], mask_in); ld(IDF[:], ident_in)
        S.finalize_key("const")
        C = [b_const]

        def dve(fn, reads, writes):
            S.add('dve', fn, reads=reads, writes=writes)

        def act(fn, reads, writes):
            S.add('act', fn, reads=reads, writes=writes)

        dve(lambda e: e.tensor_scalar(out=WB0[0:64], in0=WB0[0:64], scalar1=-1.0, scalar2=None, op0=ALU.mult),
            C, [bT["WB0"]])
        dve(lambda e: e.tensor_scalar(out=N1[64:128], in0=N1[64:128], scalar1=-1.0, scalar2=None, op0=ALU.mult),
            C, [bT["N1"]])
        dve(lambda e: e.tensor_scalar(out=N2[:], in0=N2[:], scalar1=-1.0, scalar2=None, op0=ALU.mult),
            C, [bT["N2"]])
        act(lambda e: e.activation(out=DT[:], in_=LDT[:], func=AF.Exp), C, [bT["DT"]])
        dve(lambda e: e.tensor_tensor(out=DLR[:], in0=DT[:], in1=LR[:], op=ALU.mult), C + [bT["DT"]], [bT["DLR"]])
        dve(lambda e: e.tensor_tensor(out=DLI[:], in0=DT[:], in1=LI[:], op=ALU.mult), C + [bT["DT"]], [bT["DLI"]])
        for k in range(1, 9):
            act(lambda e, k=k: e.activation(out=MAG[:, k], in_=DLR[:], func=AF.Exp, scale=float(k)),
                [bT["DLR"]], [bT["MAG"]])
            act(lambda e, k=k: e.activation(out=MAGn[:, k], in_=DLR[:], func=AF.Exp, scale=float(-k)),
                [bT["DLR"]], [bT["MAGn"]])
            for (RR, off) in ((R1, 0.0), (R2, math.pi / 2)):
                nm = "R1" if RR is R1 else "R2"
                dve(lambda e, k=k, RR=RR, off=off: e.tensor_scalar(out=RR[:, k], in0=DLI[:], scalar1=float(k), scalar2=off,
                                                                  op0=ALU.mult, op1=ALU.add), [bT["DLI"]], [bT[nm]])
                dve(lambda e, k=k, RR=RR: e.tensor_scalar(out=QA[:], in0=RR[:, k], scalar1=1.0 / TWO_PI, scalar2=None,
                                                          op0=ALU.mult), [bT[nm]], [bT["QA"]])
                dve(lambda e: e.tensor_copy(out=NI[:], in_=QA[:]), [bT["QA"]], [bT["QB"]])
                dve(lambda e: e.tensor_copy(out=QA[:], in_=NI[:]), [bT["QB"]], [bT["QA"]])
                dve(lambda e, k=k, RR=RR: e.scalar_tensor_tensor(out=RR[:, k], in0=QA[:], scalar=-TWO_PI, in1=RR[:, k],
                                                                 op0=ALU.mult, op1=ALU.add), [bT["QA"]], [bT[nm]])
                dve(lambda e, k=k, RR=RR: e.tensor_scalar(out=QA[:], in0=RR[:, k], scalar1=math.pi, scalar2=-TWO_PI,
                                                          op0=ALU.is_gt, op1=ALU.mult), [bT[nm]], [bT["QA"]])
                dve(lambda e, k=k, RR=RR: e.tensor_tensor(out=RR[:, k], in0=RR[:, k], in1=QA[:], op=ALU.add),
                    [bT["QA"]], [bT[nm]])
        for k in range(1, 9):
            act(lambda e, k=k: e.activation(out=SN[:, k], in_=R1[:, k], func=AF.Sin), [bT["R1"]], [bT["SN"]])
            act(lambda e, k=k: e.activation(out=CS[:, k], in_=R2[:, k], func=AF.Sin), [bT["R2"]], [bT["CS"]])
        PY = sb("s_py", [128, 64]); PY2 = sb("s_py2", [128, 64]); PP = sb("s_pp", [128, 64])
        PSn = sb("s_psn", [128, 64]); PCs = sb("s_pcs", [128, 64]); PT = sb("s_pt", [128, 64])
        bP = {n: Buf(n) for n in ["PY", "PY2", "PP", "PSn", "PCs", "PT"]}
        for k in (8, 4):
            dve(lambda e, k=k: e.tensor_scalar(out=PY[:], in0=R1[:, k], scalar1=1.0 / 16, scalar2=None, op0=ALU.mult),
                [bT["R1"]], [bP["PY"]])
            dve(lambda e: e.tensor_tensor(out=PY2[:], in0=PY[:], in1=PY[:], op=ALU.mult), [bP["PY"]], [bP["PY2"]])
            dve(lambda e: e.tensor_scalar(out=PP[:], in0=PY2[:], scalar1=-1.0 / 5040, scalar2=None, op0=ALU.mult),
                [bP["PY2"]], [bP["PP"]])
            for cc in (1.0 / 120, -1.0 / 6):
                dve(lambda e, cc=cc: e.scalar_tensor_tensor(out=PP[:], in0=PP[:], scalar=cc, in1=PY2[:], op0=ALU.add, op1=ALU.mult),
                    [bP["PY2"]], [bP["PP"]])
            dve(lambda e: e.scalar_tensor_tensor(out=PSn[:], in0=PP[:], scalar=1.0, in1=PY[:], op0=ALU.add, op1=ALU.mult),
                [bP["PP"], bP["PY"]], [bP["PSn"]])
            dve(lambda e: e.tensor_scalar(out=PP[:], in0=PY2[:], scalar1=1.0 / 40320, scalar2=None, op0=ALU.mult),
                [bP["PY2"]], [bP["PP"]])
            for cc in (-1.0 / 720, 1.0 / 24, -0.5):
                dve(lambda e, cc=cc: e.scalar_tensor_tensor(out=PP[:], in0=PP[:], scalar=cc, in1=PY2[:], op0=ALU.add, op1=ALU.mult),
                    [bP["PY2"]], [bP["PP"]])
            dve(lambda e: e.tensor_scalar(out=PCs[:], in0=PP[:], scalar1=1.0, scalar2=None, op0=ALU.add),
                [bP["PP"]], [bP["PCs"]])
            for _ in range(4):
                dve(lambda e: e.tensor_tensor(out=PT[:], in0=PSn[:], in1=PSn[:], op=ALU.mult), [bP["PSn"]], [bP["PT"]])
                dve(lambda e: e.tensor_tensor(out=PSn[:], in0=PSn[:], in1=PCs[:], op=ALU.mult), [bP["PCs"], bP["PT"]], [bP["PSn"]])
                dve(lambda e: e.tensor_scalar(out=PSn[:], in0=PSn[:], scalar1=2.0, scalar2=None, op0=ALU.mult), [], [bP["PSn"]])
                dve(lambda e: e.tensor_scalar(out=PCs[:], in0=PT[:], scalar1=-2.0, scalar2=1.0, op0=ALU.mult, op1=ALU.add),
                    [bP["PT"], bP["PSn"]], [bP["PCs"]])
            dve(lambda e, k=k: e.tensor_copy(out=SN[:, k], in_=PSn[:]), [bP["PSn"]], [bT["SN"]])
            dve(lambda e, k=k: e.tensor_copy(out=CS[:, k], in_=PCs[:]), [bP["PCs"]], [bT["CS"]])
        for k in range(1, 9):
            dve(lambda e, k=k: e.scalar_tensor_tensor(out=TR[:, k], in0=CS[:, k], scalar=1.0, in1=MAG[:, k],
                                                      op0=ALU.mult, op1=ALU.mult), [bT["CS"], bT["MAG"]], [bT["TR"]])
            dve(lambda e, k=k: e.scalar_tensor_tensor(out=TI[:, k], in0=SN[:, k], scalar=1.0, in1=MAG[:, k],
                                                      op0=ALU.mult, op1=ALU.mult), [bT["SN"], bT["MAG"]], [bT["TI"]])
            dve(lambda e, k=k: e.scalar_tensor_tensor(out=TRn[:, k], in0=CS[:, k], scalar=1.0, in1=MAGn[:, k],
                                                      op0=ALU.mult, op1=ALU.mult), [bT["CS"], bT["MAGn"]], [bT["TRn"]])
            dve(lambda e, k=k: e.scalar_tensor_tensor(out=TIn[:, k], in0=SN[:, k], scalar=-1.0, in1=MAGn[:, k],
                                                      op0=ALU.mult, op1=ALU.mult), [bT["SN"], bT["MAGn"]], [bT["TIn"]])
        dve(lambda e: e.tensor_scalar(out=QA[:], in0=TR[:, 1], scalar1=-1.0, scalar2=None, op0=ALU.add),
            [bT["TR"]], [bT["QA"]])
        dve(lambda e: e.tensor_tensor(out=QB[:], in0=LR[:], in1=LR[:], op=ALU.mult), C, [bT["QB"]])
        dve(lambda e: e.tensor_tensor(out=QC[:], in0=LI[:], in1=LI[:], op=ALU.mult), C, [bT["QC"]])
        dve(lambda e: e.tensor_tensor(out=QB[:], in0=QB[:], in1=QC[:], op=ALU.add), [bT["QB"], bT["QC"]], [bT["QB"]])
        dve(lambda e: e.reciprocal(out=QB[:], in_=QB[:]), [bT["QB"]], [bT["QB"]])
        dve(lambda e: e.tensor_tensor(out=QR[:], in0=QA[:], in1=LR[:], op=ALU.mult), [bT["QA"]] + C, [bT["QR"]])
        dve(lambda e: e.tensor_tensor(out=QC[:], in0=TI[:, 1], in1=LI[:], op=ALU.mult), [bT["TI"], bT["QC"]] + C, [bT["QC"]])
        dve(lambda e: e.tensor_tensor(out=QR[:], in0=QR[:], in1=QC[:], op=ALU.add), [bT["QR"], bT["QC"]], [bT["QR"]])
        dve(lambda e: e.tensor_tensor(out=QR[:], in0=QR[:], in1=QB[:], op=ALU.mult), [bT["QR"], bT["QB"]], [bT["QR"]])
        dve(lambda e: e.tensor_tensor(out=QI[:], in0=TI[:, 1], in1=LR[:], op=ALU.mult), [bT["TI"]] + C, [bT["QI"]])
        dve(lambda e: e.tensor_tensor(out=QC[:], in0=QA[:], in1=LI[:], op=ALU.mult), [bT["QA"], bT["QC"]] + C, [bT["QC"]])
        dve(lambda e: e.tensor_tensor(out=QI[:], in0=QI[:], in1=QC[:], op=ALU.subtract), [bT["QI"], bT["QC"]], [bT["QI"]])
        dve(lambda e: e.tensor_tensor(out=QI[:], in0=QI[:], in1=QB[:], op=ALU.mult), [bT["QI"], bT["QB"]], [bT["QI"]])

        def cmul(out_ap, ar, ai, s_ap, w_ap, sign, reads, wbuf):
            dve(lambda e: e.tensor_tensor(out=T16a[:], in0=s_ap, in1=bc_last(ar, 16), op=ALU.mult),
                reads + [bT["T16a"]], [bT["T16a"]])
            dve(lambda e: e.tensor_tensor(out=T16b[:], in0=w_ap, in1=bc_last(ai, 16), op=ALU.mult),
                reads + [bT["T16b"]], [bT["T16b"]])
            dve(lambda e: e.tensor_tensor(out=out_ap, in0=T16a[:], in1=T16b[:],
                                          op=(ALU.add if sign > 0 else ALU.subtract)),
                [bT["T16a"], bT["T16b"]], [wbuf])

        Cq = C + [bT["QR"], bT["QI"], bT["WB0"]]
        cmul(SBb[:], QR[:], QI[:], SB0[:], WB0[:], +1, Cq, bT["SBb"])
        cmul(WBb[:], QR[:], QI[:], WB0[:], SB0[:], -1, Cq, bT["WBb"])
        Cb = [bT["SBb"], bT["WBb"], bT["TR"], bT["TI"], bT["TRn"], bT["TIn"], bT["N1"], bT["N2"]]
        for s in range(8):
            cmul(BNp[:, :, s, :], TRn[:, s + 1], TIn[:, s + 1], SBb[:], WBb[:], +1, Cb, bT["BNp"])
            if s == 7:
                dve(lambda e: e.tensor_copy(out=BPs[:, :, 7, :], in_=SBb[:]), Cb, [bT["BPs"]])
                dve(lambda e: e.tensor_copy(out=WBPs[:, :, 7, :], in_=WBb[:]), Cb, [bT["WBPs"]])
            else:
                cmul(BPs[:, :, s, :], TR[:, 7 - s], TI[:, 7 - s], SBb[:], WBb[:], +1, Cb, bT["BPs"])
                cmul(WBPs[:, :, s, :], TR[:, 7 - s], TI[:, 7 - s], WBb[:], SBb[:], -1, Cb, bT["WBPs"])
            cmul(M3f[:, :, s, :], TR[:, s + 1], TI[:, s + 1], N1[:], N2[:], +1, Cb, bT["M3f"])
        cnt = 0
        for g0 in range(0, 64, 16):
            act(lambda e, g0=g0: e.activation(out=MO[3][:].rearrange("p g m -> p (g m)"),
                                       in_=M3f[:, g0:g0 + 16].rearrange("p g s c -> p (g s c)"), func=AF.Copy),
                [bT["M3f"]], [bMO[3]])
            for g in range(g0, g0 + 16):
                gl = g - g0
                pb = cnt % 8; cnt += 1
                S.add('pe', lambda e, g=g, pb=pb: e.matmul(PSs[pb][:, 0:128],
                                                          lhsT=BNp[:, g].rearrange("p s c -> p (s c)"),
                                                          rhs=M3f[:, g].rearrange("p s c -> p (s c)"),
                                                          start=True, stop=True),
                      reads=[bT["BNp"], bT["M3f"]], writes=[bPS[pb]])
                tm = g % 2
                dve(lambda e, pb=pb, tm=tm: e.tensor_tensor(out=TMPM[tm][:], in0=PSs[pb][:, 0:128], in1=MASK[:], op=ALU.mult),
                    C, [bPS[pb], bTMPM[tm]])
                dve(lambda e, g=g, gl=gl, tm=tm: e.scalar_tensor_tensor(out=MO[0][:, gl, :], in0=IDF[:], scalar=DDt[:, g:g + 1],
                                                                in1=TMPM[tm][:], op0=ALU.mult, op1=ALU.add),
                    C + [bTMPM[tm]], [bMO[0]])
                for (src_, bsrc, mi) in ((BPs, bT["BPs"], 1), (WBPs, bT["WBPs"], 2)):
                    pb = cnt % 8; cnt += 1
                    S.add('pe', lambda e, g=g, pb=pb, src_=src_: e.transpose(PSs[pb][:, 0:128],
                                                                          src_[:, g].rearrange("p s c -> p (s c)"), IDF[:]),
                          reads=[bsrc] + C, writes=[bPS[pb]])
                    act(lambda e, gl=gl, pb=pb, mi=mi: e.activation(out=MO[mi][:, gl, :], in_=PSs[pb][:, 0:128], func=AF.Copy),
                        [], [bPS[pb], bMO[mi]])
            for i in range(4):
                S.add('sp', lambda e, i=i, g0=g0: e.dma_start(out=ms[i][:, g0 * 128:(g0 + 16) * 128],
                                                          in_=MO[i][:].rearrange("p g m -> p (g m)")),
                      reads=[bMO[i]], dkey=f"mso{i}")
        for i, (tt, kk) in enumerate(((TR, 8), (TI, 8), (TRn, 4), (TIn, 4))):
            S.add('sp', lambda e, i=i, tt=tt, kk=kk: e.dma_start(out=tb[i], in_=tt[:, kk]),
                  reads=[bT["TR"], bT["TI"], bT["TRn"], bT["TIn"]], dkey="mso")
        S.emit(nc, "s", sem_es)

    if stage == "setup":
        return nc
    with ExitStack() as es:
        S = Sched()

        def sb(name, shape, dt=F32):
            return es.enter_context(nc.sbuf_tensor(name, list(shape), dt))

        X = sb("X", [128, 16, NT]); XN = sb("XN", [128, 16, NT], BF16)
        H = sb("H", [128, 11, NT], BF16)
        NSLAB = 7
        SL = sb("SL", [128, NSLAB, 4096], BF16)
        USS = sb("USS", [128, 8, NT], BF16)
        ST = sb("ST", [128, 8, NT], BF16)
        AO = sb("AO", [128, 8, NT], BF16)
        MG = sb("MG", [128, 16, NT], BF16)
        ZC = sb("ZC", [128, 8, 8, 16], BF16); ZS = sb("ZS", [128, 8, 8, 16], BF16)
        U = sb("U", [128, 64, NCOL], BF16); WG = sb("WG", [128, 64, NCOL], BF16)
        GB = sb("GB", [128, 64, NCOL + 1]); HB = sb("HB", [128, 64, NCOL], BF16)
        Wst = [sb(f"Wst{i}", [128, 64]) for i in range(2)]
        T1 = sb("T1", [128, 64]); T2 = sb("T2", [128, 64]); U1 = sb("U1", [128, 64]); U2 = sb("U2", [128, 64])
        SH0 = sb("SH0", [128, 64, NSL]); WH0 = sb("WH0", [128, 64, NSL])
        HPS = sb("HPS", [128, 64, NSL]); HPW = sb("HPW", [128, 64, NSL])
        T6a = sb("T6a", [128, 64, NSL]); T6b = sb("T6b", [128, 64, NSL])
        PW = SMP0 + NSL * 19
        UP = sb("UP", [128, 2, PW]); WA = sb("WA", [128, 2, PW]); WB = sb("WB", [128, 2, PW])
        Dm = sb("Dm", [128, 2, NT], BF16)
        SQ = sb("SQ", [128, 2, NT], BF16); RS = sb("RS", [128, NT])
        TMP = sb("TMP", [128, 2, NT])
        OST = sb("OST", [128, 2, 368])
        GE = sb("GE", [128, 2, 512]); GE2 = sb("GE2", [128, 2, 512])
        IDF = sb("IDF", [128, 128]); IDB = sb("IDB", [128, 128], BF16); ONES = sb("ONES", [128, 128], BF16)
        GN = sb("GN", [128, 4, 16]); BGt = sb("BGt", [128, 32]); GLBt = sb("GLBt", [128, 8]); PSC = sb("PSC", [128, 8])
        WSR = sb("WSR", [128, 8])
        TBL = sb("TBL", [128, 4, 64]); SPO = sb("SPO", [128, 64]); EPS = sb("EPS", [128, 2])
        PS = [es.enter_context(nc.psum_tensor(f"ps{i}", [128, 512], F32)) for i in range(8)]
        PSB = [PS[i][:, :].bitcast(BF16) for i in range(8)]

        bconst = Buf("const")
        bX = [Buf(f"X{i}") for i in range(16)]; bXN = [Buf(f"XN{i}") for i in range(16)]
        bH = [Buf(f"H{i}") for i in range(11)]
        bSL = [Buf(f"SL{i}") for i in range(NSLAB)]
        bUSS = [Buf(f"USS{i}") for i in range(8)]; bST = [Buf(f"ST{i}") for i in range(8)]
        bAO = [Buf(f"AO{i}") for i in range(8)]; bMG = [Buf(f"MG{i}") for i in range(16)]
        bZC = Buf("ZC"); bZS = Buf("ZS"); bU = Buf("U"); bWG = Buf("WG"); bGB = Buf("GB"); bHB = Buf("HB")
        bW = [Buf("W0"), Buf("W1")]; bT1 = Buf("T1"); bT2 = Buf("T2"); bU1 = Buf("U1"); bU2 = Buf("U2")
        bSH0 = Buf("SH0"); bHPS = Buf("HPS"); bHPW = Buf("HPW"); bT6a = Buf("T6a"); bT6b = Buf("T6b")
        bUP = Buf("UP"); bWA = Buf("WA"); bWB = Buf("WB"); bD = Buf("D")
        bSQ = [Buf("SQ0"), Buf("SQ1")]; bRS = Buf("RS"); bTMP = [Buf("TMP0"), Buf("TMP1")]
        bOST = [Buf("OST0"), Buf("OST1")]
        bG0 = Buf("G0"); bG1 = Buf("G1"); bG2 = Buf("G2"); bG3 = Buf("G3")
        bPS = [Buf(f"ps{i}") for i in range(8)]
        st = {"ps": 0, "slab": 0, "ge": 0}
        cw = {"a": 0, "b": NT}

        def nps():
            b = st["ps"] % 8; st["ps"] += 1
            return b

        def dve(fn, reads, writes):
            S.add('dve', fn, reads=reads, writes=writes)

        def act(fn, reads, writes):
            S.add('act', fn, reads=reads, writes=writes)

        def pe(fn, reads, writes):
            S.add('pe', fn, reads=reads, writes=writes)

        def ldc(dst, src):
            bconst.lw = S.add('sp', lambda e: e.dma_start(out=dst, in_=src), dkey="const")
        ldc(IDF[:], ident_in); ldc(GN[:], gains); ldc(BGt[:], bgate); ldc(GLBt[:], glub); ldc(PSC[:], pscale)
        ldc(TBL[:], tb.rearrange("i p g -> p i g"))
        S.finalize_key("const")
        C = [bconst]
        bIDB = Buf("idb")
        S.add('pool', lambda e: e.dma_start(out=IDB[:], in_=ident_in), writes=[bIDB], dkey="constb")
        C = [bconst, bIDB]
        bONES = Buf("ones2")
        dve(lambda e: e.memset(ONES[:], 1.0), [], [bONES])
        bEPS = Buf("eps")
        dve(lambda e: e.memset(EPS[:], 1e-6), [], [bEPS])
        dve(lambda e: e.memset(ZS[:], 0.0), [], [bZS])
        dve(lambda e: e.memset(GB[:], 0.0), [], [bGB])
        dve(lambda e: e.memset(Wst[0][:], 0.0), [], [bW[0]])
        dve(lambda e: e.memset(Dm[:], 0.0), [], [bD])
        A8r = TBL[:, 0]; A8i = TBL[:, 1]; Am4r = TBL[:, 2]; Am4i = TBL[:, 3]

        wcache = {}

        class SV:
            def __init__(self, ap, tiled):
                self.ap = ap; self.tiled = tiled

            def w(self, k, m):
                return self.ap[:, m, k, :] if self.tiled else self.ap[:, k, m * 128:(m + 1) * 128]

        def load_slab(src_ap, kt, ncols, ckey=None):
            si = st["slab"] % NSLAB; st["slab"] += 1
            n = kt * ncols
            nm = ncols // 128
            tiled = isinstance(src_ap, tuple)
            if tiled:
                view = SV(SL[:, si, 0:n].rearrange("p (m k c) -> p m k c", m=nm, k=kt), True)
            else:
                view = SV(SL[:, si, 0:n].rearrange("p (k m) -> p k m", k=kt), False)
            if ckey is not None and ckey in wcache:
                ci, bc = wcache[ckey]
                S.add('sp', lambda e: e.dma_start(out=SL[:, si, 0:n], in_=WC[ci][:, 0:n]),
                      reads=[bc], writes=[bSL[si]], dkey=f"slabh{si}")
                return view, bSL[si]
            if tiled:
                _, wt, mt0, k0 = src_ap
                S.add('pool', lambda e: e.dma_start(
                    out=SL[:, si, 0:n].rearrange("p (m r) -> p m r", m=nm),
                    in_=wt[mt0:mt0 + nm, :, k0 * 128:(k0 + kt) * 128].rearrange("m p r -> p m r")),
                    writes=[bSL[si]], dkey=f"slab{si}")
            else:
                S.add('pool', lambda e: e.dma_start(out=view.ap, in_=src_ap.rearrange("(k p) m -> p k m", p=128)),
                      writes=[bSL[si]], dkey=f"slab{si}")
            if ckey is not None:
                ci = len(wcache)
                bc = Buf(f"wc{ci}")
                wcache[ckey] = (ci, bc)
                S.add('sp', lambda e: e.dma_start(out=WC[ci][:, 0:n], in_=SL[:, si, 0:n]),
                      reads=[bSL[si]], writes=[bc], dkey=f"cw{ci % 8}")
            return view, bSL[si]

        def load_m(mi):
            halves = []
            for hf in range(2):
                si = st["slab"] % NSLAB; st["slab"] += 1
                S.add('sp', lambda e, mi=mi, si=si, hf=hf: e.dma_start(out=SL[:, si, :], in_=ms[mi][:, hf * 4096:(hf + 1) * 4096]),
                      writes=[bSL[si]], dkey=f"slabh{si}")
                halves.append((SL[:, si, :].rearrange("p (g m) -> p g m", g=32), bSL[si]))
            return halves

        def linear(src, bsrc, kt, w_ap, col0, n_mt, epi, krow0=0, cname=None, tiled=False):
            a, b = cw["a"], cw["b"]
            mt = 0
            while mt < n_mt:
                nm = min(2, n_mt - mt)
                view, bs = load_slab(("t", w_ap, col0 // 128 + mt, krow0 // 128) if tiled else
                                     w_ap[krow0:krow0 + kt * 128, col0 + mt * 128: col0 + (mt + nm) * 128], kt, nm * 128,
                                     ckey=(cname, krow0, col0 + mt * 128) if cname else None)
                for m in range(nm):
                    pb = nps()
                    for k in range(kt):
                        pe(lambda e, pb=pb, k=k, m=m, view=view: e.matmul(
                            PS[pb][:, a:b], lhsT=view.w(k, m), rhs=src[:, k, a:b],
                            start=(k == 0), stop=(k == kt - 1)),
                           [bs, bsrc[k]], [bPS[pb]])
                    epi(mt + m, pb)
                mt += nm

        def rmsnorm(gi):
            a, b = cw["a"], cw["b"]
            pb = nps()
            for k in range(16):
                q = k % 2
                act(lambda e, k=k, q=q: e.activation(out=SQ[:, q, a:b], in_=X[:, k, a:b], func=AF.Square),
                    [bX[k]], [bSQ[q]])
                pe(lambda e, k=k, q=q, pb=pb: e.matmul(PS[pb][:, a:b], lhsT=ONES[:], rhs=SQ[:, q, a:b],
                                                       start=(k == 0), stop=(k == 15)),
                   [bSQ[q], bONES], [bPS[pb]])
            act(lambda e, pb=pb: e.activation(out=RS[:, a:b], in_=PS[pb][:, a:b], func=AF.Sqrt, bias=EPS[:, 0:1], scale=1.0 / D),
                [bEPS], [bPS[pb], bRS])
            dve(lambda e: e.reciprocal(out=RS[:, a:b], in_=RS[:, a:b]), [], [bRS])
            return pb

        def norm_to_xn(gi):
            a, b = cw["a"], cw["b"]
            rmsnorm(gi)
            for k in range(16):
                dve(lambda e, k=k: e.scalar_tensor_tensor(out=XN[:, k, a:b], in0=X[:, k, a:b], scalar=GN[:, gi, k:k + 1],
                                                          in1=RS[:, a:b], op0=ALU.mult, op1=ALU.mult),
                    [bX[k], bRS] + C, [bXN[k]])

        def ffn(gi, wg, wu, wd, nm_):
            a, b = cw["a"], cw["b"]
            norm_to_xn(gi)
            for f0 in range(0, 44, 11):
                fl0 = 0
                while fl0 < 11:
                    nm = min(2, 11 - fl0)
                    f = f0 + fl0
                    vg, bg_ = load_slab(("t", wg, f, 0), 16, nm * 128, ckey=(nm_ + 'g', f))
                    vu, bu_ = load_slab(("t", wu, f, 0), 16, nm * 128, ckey=(nm_ + 'u', f))
                    for m in range(nm):
                        fl = fl0 + m
                        pg = nps(); pu = nps()
                        for k in range(16):
                            pe(lambda e, k=k, pg=pg, vg=vg, m=m: e.matmul(PS[pg][:, a:b], lhsT=vg.w(k, m),
                                                                          rhs=XN[:, k, a:b], start=(k == 0), stop=(k == 15)),
                               [bg_, bXN[k]], [bPS[pg]])
                        for k in range(16):
                            pe(lambda e, k=k, pu=pu, vu=vu, m=m: e.matmul(PS[pu][:, a:b], lhsT=vu.w(k, m),
                                                                          rhs=XN[:, k, a:b], start=(k == 0), stop=(k == 15)),
                               [bu_, bXN[k]], [bPS[pu]])
                        q = fl % 2
                        act(lambda e, pg=pg, q=q: e.activation(out=TMP[:, q, a:b], in_=PS[pg][:, a:b], func=AF.Silu),
                            [], [bPS[pg], bTMP[q]])
                        dve(lambda e, pu=pu, q=q, fl=fl: e.tensor_tensor(out=H[:, fl, a:b], in0=PS[pu][:, a:b],
                                                                         in1=TMP[:, q, a:b], op=ALU.mult),
                            [bTMP[q]], [bPS[pu], bH[fl]])
                    fl0 += nm

                def epi(mt, pb):
                    dve(lambda e: e.scalar_tensor_tensor(out=X[:, mt, a:b], in0=PS[pb][:, a:b], scalar=0.5,
                                                         in1=X[:, mt, a:b], op0=ALU.mult, op1=ALU.add),
                        [], [bPS[pb], bX[mt]])
                linear(H, bH, 11, wd, 0, 16, epi, krow0=f0 * 128, cname=nm_ + 'd', tiled=True)

        def s5_front(with_samples):
            dve(lambda e: e.memset(ZS[:, :, 0:4, :], 0.0), [], [bZS])
            for ft in range(8):
                pb = nps()
                for s in range(8):
                    pe(lambda e, ft=ft, s=s, pb=pb: e.transpose(
                        PSB[pb][0:NCH, s * 128:(s + 1) * 128],
                        USS[:, ft, OWN0 + s: OWN0 + s + 8 * (NCH - 1) + 1: 8], IDB[:]),
                       [bUSS[ft]] + C, [bPS[pb]])
                act(lambda e, pb=pb: e.activation(
                    out=ZC[0:NCH].rearrange("p g s c -> p s g c"),
                    in_=PSB[pb][0:NCH, 0:1024].rearrange("p (s g c) -> p s g c", s=8, g=8), func=AF.Copy),
                    [], [bPS[pb], bZC])
                if with_samples:
                    pb = nps()
                    for sq in range(4):
                        pe(lambda e, ft=ft, sq=sq, pb=pb: e.transpose(
                            PSB[pb][0:NSL, sq * 128:(sq + 1) * 128],
                            USS[:, ft, SMP0 + sq: SMP0 + sq + 4 * (NSL - 1) + 1: 4], IDB[:]),
                           [bUSS[ft]] + C, [bPS[pb]])
                    act(lambda e, pb=pb: e.activation(
                        out=ZS[0:NSL, :, 4:8, :].rearrange("p g s c -> p s g c"),
                        in_=PSB[pb][0:NSL, 0:512].rearrange("p (s g c) -> p s g c", s=4, g=8), func=AF.Copy),
                        [], [bPS[pb], bZS])
                pb = nps()
                for gl in range(8):
                    pe(lambda e, gl=gl, pb=pb: e.transpose(PSB[pb][:, gl * 52: gl * 52 + NCH],
                                                           ZC[0:NCH, gl].rearrange("p s c -> p (s c)"), IDB[0:NCH, 0:NCH]),
                       [bZC] + C, [bPS[pb]])
                    pe(lambda e, gl=gl, pb=pb: e.transpose(PSB[pb][:, gl * 52 + 44: gl * 52 + 44 + NSL],
                                                           ZS[0:NSL, gl].rearrange("p s c -> p (s c)"), IDB[0:NSL, 0:NSL]),
                       [bZS] + C, [bPS[pb]])
                act(lambda e, ft=ft, pb=pb: e.activation(
                    out=U[:, ft * 8:(ft + 1) * 8, 0:NCH],
                    in_=PSB[pb][:, 0:8 * 52].rearrange("p (g j) -> p g j", g=8)[:, :, 0:NCH], func=AF.Copy),
                    [], [bPS[pb], bU])
                act(lambda e, ft=ft, pb=pb: e.activation(
                    out=U[:, ft * 8:(ft + 1) * 8, NCH:NCOL],
                    in_=PSB[pb][:, 0:8 * 52].rearrange("p (g j) -> p g j", g=8)[:, :, 44:44 + NSL], func=AF.Copy),
                    [], [bPS[pb], bU])
            for (mi, dst_is_gb) in ((1, True), (2, False)):
                halves = load_m(mi)
                for g0 in range(0, 64, 8):
                    mv, bmv = halves[g0 // 32]
                    pb = nps()
                    for gl in range(8):
                        g = g0 + gl
                        pe(lambda e, g=g, gl=gl, pb=pb, mv=mv: e.matmul(PS[pb][:, gl * NCOL:(gl + 1) * NCOL],
                                                                       lhsT=mv[:, g % 32, :], rhs=U[:, g, :],
                                                                       start=True, stop=True),
                           [bmv, bU], [bPS[pb]])
                    if dst_is_gb:
                        act(lambda e, g0=g0, pb=pb: e.activation(
                            out=GB[:, g0:g0 + 8, 1:NCOL + 1],
                            in_=PS[pb][:, 0:8 * NCOL].rearrange("p (g j) -> p g j", g=8), func=AF.Copy),
                            [], [bPS[pb], bGB])
                    else:
                        act(lambda e, g0=g0, pb=pb: e.activation(
                            out=WG[:, g0:g0 + 8, :],
                            in_=PS[pb][:, 0:8 * NCOL].rearrange("p (g j) -> p g j", g=8), func=AF.Copy),
                            [], [bPS[pb], bWG])

        wcur = {"i": 0}

        def recurrence():
            for j in range(NCH):
                wc = wcur["i"]; wn = 1 - wc
                S.add('pool', lambda e, j=j: e.tensor_tensor(out=T1[:], in0=GB[:, :, j], in1=A8r, op=ALU.mult), [bGB] + C, [bT1])
                S.add('pool', lambda e, wc=wc: e.tensor_tensor(out=T2[:], in0=Wst[wc][:], in1=A8i, op=ALU.mult), [bW[wc]] + C, [bT2])
                S.add('pool', lambda e, wc=wc: e.tensor_tensor(out=U1[:], in0=Wst[wc][:], in1=A8r, op=ALU.mult), [bW[wc]] + C, [bU1])
                S.add('pool', lambda e, j=j: e.tensor_tensor(out=U2[:], in0=GB[:, :, j], in1=A8i, op=ALU.mult), [bGB] + C, [bU2])
                S.add('pool', lambda e: e.tensor_tensor(out=T1[:], in0=T1[:], in1=T2[:], op=ALU.add), [bT2], [bT1])
                S.add('pool', lambda e: e.tensor_tensor(out=U1[:], in0=U1[:], in1=U2[:], op=ALU.subtract), [bU2], [bU1])
                S.add('pool', lambda e, j=j: e.tensor_tensor(out=GB[:, :, j + 1], in0=GB[:, :, j + 1], in1=T1[:], op=ALU.add),
                    [bT1], [bGB])
                S.add('pool', lambda e, j=j, wn=wn: e.tensor_tensor(out=Wst[wn][:], in0=WG[:, :, j], in1=U1[:], op=ALU.add),
                    [bU1, bWG], [bW[wn]])
                wcur["i"] = wn

        def carry_state():
            S.add('pool', lambda e: e.tensor_copy(out=GB[:, :, 0], in_=GB[:, :, NCH]), [], [bGB])

        def load_x(ti):
            S.add('sp', lambda e: e.dma_start(out=X[:], in_=xin[ti].rearrange("(k p) n -> p k n", p=128)),
                  writes=bX, dkey="xin")

        def uss_from_win():
            a, b = cw["a"], cw["b"]
            def epi(mt, pb):
                act(lambda e: e.activation(out=USS[:, mt, a:b], in_=PS[pb][:, a:b], func=AF.Copy),
                    [], [bPS[pb], bUSS[mt]])
            linear(XN, bXN, 16, w_in, 1024, 8, epi, cname='win', tiled=True)

        import os as _os
        NOCC = bool(_os.environ.get("KSIM_NOCC"))
        bXS = [Buf(f"XS{i}") for i in range(3)]; bUS = [Buf(f"US{i}") for i in range(3)]
        bGS = [Buf(f"GS{i}") for i in range(3)]; bWS = [Buf(f"WS{i}") for i in range(3)]
        bCCI = Buf("cci"); bCCO = Buf("cco")

        def front(ti):
            load_x(ti)
            ffn(0, w_g1, w_u1, w_d1, 'f1')
            S.add('sp', lambda e: e.dma_start(out=XS[ti], in_=X[:].rearrange("p k n -> p (k n)")),
                  reads=bX, writes=[bXS[ti]], dkey="sx")
            norm_to_xn(1)
            uss_from_win()
            s5_front(True)
            S.add('sp', lambda e: e.dma_start(out=US[ti], in_=U[:].rearrange("p g j -> p (g j)")),
                  reads=[bU], writes=[bUS[ti]], dkey="su")
            S.add('sp', lambda e: e.dma_start(out=GS[ti].rearrange("p (g j) -> p g j", g=64), in_=GB[:, :, 1:NCOL + 1]),
                  reads=[bGB], writes=[bGS[ti]], dkey="sg")
            S.add('sp', lambda e: e.dma_start(out=WS_[ti], in_=WG[:].rearrange("p g j -> p (g j)")),
                  reads=[bWG], writes=[bWS[ti]], dkey="sw")
            recurrence()
            carry_state()

        def exchange():
            wc = wcur["i"]
            CBv = GE[:, 0, :].rearrange("p (s f) -> p s f", s=4)
            RBv = GE[:, 1, :].rearrange("p (s f) -> p s f", s=4)
            HI = T6a[:].rearrange("p g i -> p (g i)")[:, 0:128]
            for s in range(4):
                dve(lambda e, s=s: e.tensor_scalar(out=CBv[:, s, 0:64], in0=GB[:, :, 0], scalar1=WSR[:, s:s + 1], scalar2=None,
                                                  op0=ALU.mult), [bGB] + C, [bG0])
                dve(lambda e, s=s: e.tensor_scalar(out=CBv[:, s, 64:128], in0=Wst[wc][:], scalar1=WSR[:, s:s + 1], scalar2=None,
                                                  op0=ALU.mult), [bW[wc]] + C, [bG0])
            S.add('sp', lambda e: e.dma_start(out=cc_in[:, :], in_=GE[:, 0, :]), reads=[bG0], writes=[bCCI], dkey="cci")
            if NOCC:
                S.add('pool', lambda e: e.dma_start(out=cc_out[:, :], in_=cc_in[:, :]), reads=[bCCI], writes=[bCCO], dkey="cc")
            else:
                S.add('pool', lambda e: e.collective_compute("AllReduce", ALU.add, replica_groups=[list(range(8))],
                                                             ins=[cc_in.ap().opt()], outs=[cc_out.ap().opt()]),
                      reads=[bCCI], writes=[bCCO], dkey="cc", inc=1)
            S.add('sp', lambda e: e.dma_start(out=GE[:, 1, :], in_=cc_out[:, :]), reads=[bCCO], writes=[bG1], dkey="cco")
            dve(lambda e: e.tensor_scalar(out=HI, in0=RBv[:, 0, :], scalar1=WSR[:, 4:5], scalar2=None, op0=ALU.mult),
                [bG1] + C, [bT6a])
            for s in range(1, 4):
                dve(lambda e, s=s: e.scalar_tensor_tensor(out=HI, in0=RBv[:, s, :], scalar=WSR[:, 4 + s:5 + s], in1=HI,
                                                          op0=ALU.mult, op1=ALU.add), [bG1] + C, [bT6a])
            dve(lambda e: e.tensor_copy(out=GB[:, :, 0], in_=HI[:, 0:64]), [bT6a], [bGB])
            dve(lambda e: e.tensor_copy(out=Wst[wc][:], in_=HI[:, 64:128]), [bT6a], [bW[wc]])

        def prefix(ti):
            cw["a"], cw["b"] = OWN0, SMP0
            load_x(ti)
            ffn(0, w_g1, w_u1, w_d1, 'f1')
            norm_to_xn(1)
            uss_from_win()
            s5_front(False)
            recurrence()
            carry_state()
            cw["a"], cw["b"] = 0, NT

        for ti in range(3):
            prefix(ti)

        def back(ti):
            load_x(3 + ti)
            S.add('sp', lambda e: e.dma_start(out=SH0[:], in_=sh0_in[ti]), writes=[bSH0], dkey="h0")
            S.add('sp', lambda e: e.dma_start(out=WH0[:], in_=wh0_in[ti]), writes=[bSH0], dkey="h0")
            ffn(0, w_g1, w_u1, w_d1, 'f1')
            norm_to_xn(1)
            uss_from_win()
            s5_front(True)
            recurrence()
            for pg in range(4):
                w = (2, 4, 8, 16)[pg]

                def epi_up(mt, pb, pg=pg):
                    q = mt - 2 * pg
                    act(lambda e: e.activation(out=GE[:, q, 0:NT], in_=PS[pb][:, 0:NT], func=AF.Copy),
                        [], [bPS[pb], (bG0, bG1)[q]])
                    dve(lambda e: e.tensor_copy(out=UP[:, q, 0:SMP0], in_=GE[:, q, 0:SMP0]), [(bG0, bG1)[q]], [bUP])
                    dve(lambda e: e.tensor_copy(
                        out=UP[:, q, SMP0:PW].rearrange("p (i h) -> p i h", h=19)[:, :, 15:19],
                        in_=GE[:, q, SMP0:NT].rearrange("p (i t) -> p i t", t=4)), [(bG0, bG1)[q]], [bUP])
                def lin2():
                    view, bs = load_slab(("t", w_in, 2 * pg, 0), 16, 256, ckey=('winp', pg))
                    for m in range(2):
                        pb = nps()
                        for k in range(16):
                            pe(lambda e, pb=pb, k=k, m=m: e.matmul(PS[pb][:, 0:NT], lhsT=view.w(k, m),
                                                                   rhs=XN[:, k, :], start=(k == 0), stop=(k == 15)),
                               [bs, bXN[k]], [bPS[pb]])
                        epi_up(2 * pg + m, pb)
                lin2()
                for q in range(2):
                    S.add('sp', lambda e, ti=ti, pg=pg, q=q: e.dma_start(
                        out=UP[:, q, SMP0:PW].rearrange("p (i h) -> p i h", h=19)[:, :, 0:15],
                        in_=hist_in[ti, pg * 256 + q * 128: pg * 256 + (q + 1) * 128]),
                        writes=[bUP], dkey="hist")
                if pg == 0:
                    pass
                dve(lambda e: e.tensor_tensor(out=WA[:, :, 1:PW], in0=UP[:, :, 1:PW], in1=UP[:, :, 0:PW - 1], op=ALU.add),
                    [bUP], [bWA])
                cur, bcur, oth, both = WA, bWA, WB, bWB
                k = 2
                while k < w:
                    dve(lambda e, cur=cur, oth=oth, k=k: e.tensor_tensor(out=oth[:, :, 2 * k - 1:PW], in0=cur[:, :, 2 * k - 1:PW],
                                                                          in1=cur[:, :, k - 1:PW - k], op=ALU.add),
                        [bcur], [both])
                    cur, bcur, oth, both = oth, both, cur, bcur
                    k *= 2
                dve(lambda e, cur=cur, w=w: e.scalar_tensor_tensor(out=Dm[:, :, OWN0:SMP0], in0=cur[:, :, OWN0:SMP0],
                                                                   scalar=1.0 / w, in1=UP[:, :, OWN0:SMP0],
                                                                   op0=ALU.mult, op1=ALU.subtract), [bcur, bUP], [bD])
                for q in range(2):
                    dve(lambda e, cur=cur, w=w, q=q: e.scalar_tensor_tensor(
                        out=Dm[:, q, SMP0:NT].rearrange("p (i t) -> p i t", t=4),
                        in0=cur[:, q, SMP0:PW].rearrange("p (i h) -> p i h", h=19)[:, :, 15:19], scalar=1.0 / w,
                        in1=UP[:, q, SMP0:PW].rearrange("p (i h) -> p i h", h=19)[:, :, 15:19],
                        op0=ALU.mult, op1=ALU.subtract), [bcur, bUP], [bD])
                for q in range(2):
                    S.add('sp', lambda e, ti=ti, pg=pg, q=q: e.dma_start(
                        out=pool_s[ti, pg * 256 + q * 128: pg * 256 + (q + 1) * 128],
                        in_=UP[:, q, SMP0:PW].rearrange("p (i h) -> p i h", h=19)[:, :, 4:19]),
                        reads=[bUP], dkey="o_pools")
                if ti == 2:
                    S.add('sp', lambda e, pg=pg: e.dma_start(
                        out=pool_p[pg * 256:(pg + 1) * 256].rearrange("(q p) h -> p q h", p=128),
                        in_=UP[:, :, SMP0 - 15:SMP0]), reads=[bUP], dkey="o_poolp")
                vw, bw = load_slab(w_pool[pg], 2, 256, ckey=('pw', pg))
                for m in range(2):
                    pb = nps()
                    for k2 in range(2):
                        pe(lambda e, pb=pb, k2=k2, m=m, vw=vw: e.matmul(PS[pb][:, 0:NT], lhsT=vw.w(k2, m),
                                                                       rhs=Dm[:, k2, :], start=(k2 == 0), stop=(k2 == 1)),
                           [bw, bD], [bPS[pb]])
                    mt = 2 * pg + m
                    dve(lambda e, pb=pb, mt=mt: e.tensor_scalar(out=AO[:, mt, :], in0=PS[pb][:, 0:NT],
                                                                scalar1=PSC[:, mt:mt + 1], scalar2=None, op0=ALU.mult),
                        C, [bPS[pb], bAO[mt]])
            for dt_ in range(16):
                va, ba = load_slab(("t", w_in, 16 + dt_, 0), 16, 128, ckey=('ga', dt_))
                vwa, bwa = load_slab(("t", w_ba, dt_, 0), 8, 128, ckey=('ba', dt_))
                pga = nps(); pa = nps()
                for k in range(16):
                    pe(lambda e, k=k, pga=pga, va=va: e.matmul(PS[pga][:, 0:NT], lhsT=va.w(k, 0), rhs=XN[:, k, :],
                                                               start=(k == 0), stop=(k == 15)), [ba, bXN[k]], [bPS[pga]])
                for k in range(8):
                    pe(lambda e, k=k, pa=pa, vwa=vwa: e.matmul(PS[pa][:, 0:NT], lhsT=vwa.w(k, 0), rhs=AO[:, k, :],
                                                               start=(k == 0), stop=(k == 7)), [bwa, bAO[k]], [bPS[pa]])
                act(lambda e, pga=pga, dt_=dt_: e.activation(out=TMP[:, 0, :], in_=PS[pga][:, 0:NT], func=AF.Sigmoid,
                                                             bias=BGt[:, dt_:dt_ + 1], scale=1.0), C, [bPS[pga], bTMP[0]])
                dve(lambda e, pa=pa, dt_=dt_: e.tensor_tensor(out=MG[:, dt_, :], in0=PS[pa][:, 0:NT], in1=TMP[:, 0, :], op=ALU.mult),
                    [bTMP[0]], [bPS[pa], bMG[dt_]])
            dve(lambda e: e.tensor_scalar(out=WH0[0:64], in0=WH0[0:64], scalar1=-1.0, scalar2=None, op0=ALU.mult),
                [], [bSH0])

            def cm6(out_ap, ar, ai, s_ap, w_ap, sign, reads, wbuf):
                dve(lambda e: e.tensor_tensor(out=T6a[:], in0=s_ap, in1=bc_last(ar, NSL), op=ALU.mult), reads + C, [bT6a])
                dve(lambda e: e.tensor_tensor(out=T6b[:], in0=w_ap, in1=bc_last(ai, NSL), op=ALU.mult), reads + C, [bT6b])
                dve(lambda e: e.tensor_tensor(out=out_ap, in0=T6a[:], in1=T6b[:],
                                              op=(ALU.add if sign > 0 else ALU.subtract)), [bT6a, bT6b], [wbuf])
            cm6(HPS[:], Am4r, Am4i, SH0[:], WH0[:], +1, [bSH0], bHPS)
            cm6(HPW[:], Am4r, Am4i, WH0[:], SH0[:], -1, [bSH0], bHPW)
            cm6(T6a[:], A8r, A8i, HPS[:], HPW[:], +1, [bHPS, bHPW], bT6a)
            dve(lambda e: e.tensor_tensor(out=GB[:, :, NCH + 1:NCOL + 1], in0=GB[:, :, NCH + 1:NCOL + 1], in1=T6a[:],
                                          op=ALU.add), [bT6a], [bGB])
            S.add('sp', lambda e, ti=ti: e.dma_start(out=ssm_s[ti], in_=GB[:, :, NCH + 1:NCOL + 1]),
                  reads=[bGB], dkey="o_ssms")
            if ti == 2:
                bSPO = Buf("SPO")
                dve(lambda e: e.tensor_copy(out=SPO[:], in_=GB[:, :, NCH]), [bGB], [bSPO])
                S.add('sp', lambda e: e.dma_start(out=ssm_p, in_=SPO[:]), reads=[bSPO], dkey="o_ssmp")
            act(lambda e: e.activation(out=HB[:, :, 0:NCH], in_=GB[:, :, 0:NCH], func=AF.Copy), [bGB], [bHB])
            act(lambda e: e.activation(out=HB[:, :, NCH:NCOL], in_=HPS[:], func=AF.Copy), [bHPS], [bHB])
            m1h = load_m(0)
            m3h = load_m(3)
            zb = ZC[:].rearrange("p g s c -> p (g s c)").rearrange("p (t f) -> p t f", t=8)
            nr = NCOL
            for ft in range(8):
                for half in range(2):
                    pb = nps()
                    for gq in range(4):
                        g = ft * 8 + half * 4 + gq
                        m1v, bm1 = m1h[g // 32]
                        m3v, bm3 = m3h[g // 32]
                        pe(lambda e, g=g, gq=gq, pb=pb, m1v=m1v: e.matmul(
                            PS[pb][0:NCOL, gq * 128:(gq + 1) * 128], lhsT=U[:, g, 0:NCOL], rhs=m1v[:, g % 32, :],
                            start=True, stop=False), [bU, bm1], [bPS[pb]])
                        pe(lambda e, g=g, gq=gq, pb=pb, m3v=m3v: e.matmul(
                            PS[pb][0:NCOL, gq * 128:(gq + 1) * 128], lhsT=HB[:, g, 0:NCOL], rhs=m3v[:, g % 32, :],
                            start=False, stop=True), [bHB, bm3], [bPS[pb]])
                    gi_ = st["ge"] % 2; st["ge"] += 1
                    GX = (GE, GE2)[gi_]; bga, bgb = ((bG0, bG1), (bG2, bG3))[gi_]
                    act(lambda e, pb=pb, GX=GX: e.activation(out=GX[0:NCOL, 0, :], in_=PS[pb][0:NCOL, 0:512], func=AF.Square),
                        [], [bPS[pb], bga])
                    dve(lambda e, GX=GX: e.tensor_scalar(out=GX[0:NCOL, 0, :], in0=GX[0:NCOL, 0, :], scalar1=0.044715, scalar2=1.0,
                                                         op0=ALU.mult, op1=ALU.add), [], [bga])
                    dve(lambda e, pb=pb, GX=GX: e.tensor_tensor(out=GX[0:NCOL, 0, :], in0=PS[pb][0:NCOL, 0:512], in1=GX[0:NCOL, 0, :],
                                                                op=ALU.mult), [], [bPS[pb], bga])
                    act(lambda e, GX=GX: e.activation(out=GX[0:NCOL, 1, :], in_=GX[0:NCOL, 0, :], func=AF.Sigmoid, scale=1.5957691),
                        [bga], [bgb])
                    dve(lambda e, pb=pb, half=half, GX=GX: e.tensor_tensor(
                        out=zb[0:NCOL, :, half * 64:(half + 1) * 64].rearrange("p t (g c) -> p g t c", g=4),
                        in0=PS[pb][0:NCOL, 0:512].rearrange("p (g t c) -> p g t c", g=4, t=8),
                        in1=GX[0:NCOL, 1, :].rearrange("p (g t c) -> p g t c", g=4, t=8), op=ALU.mult),
                        [bgb], [bPS[pb], bZC])
                pb = nps()
                for t in range(8):
                    pe(lambda e, t=t, pb=pb: e.transpose(
                        PSB[pb][:, t * 64: t * 64 + NCOL], zb[0:NCOL, t, :], IDB[0:NCOL, 0:NCOL]),
                       [bZC] + C, [bPS[pb]])
                act(lambda e, ft=ft, pb=pb: e.activation(
                    out=ST[:, ft, OWN0:OWN0 + OWN].rearrange("p (j t) -> p t j", t=8),
                    in_=PSB[pb][:, 0:512].rearrange("p (t j) -> p t j", t=8)[:, :, 0:NCH], func=AF.Copy),
                    [], [bPS[pb], bST[ft]])
                act(lambda e, ft=ft, pb=pb: e.activation(
                    out=ST[:, ft, SMP0:SMP0 + 4 * NSL].rearrange("p (i t) -> p t i", t=4),
                    in_=PSB[pb][:, 256:512].rearrange("p (t j) -> p t j", t=4)[:, :, NCH:NCOL], func=AF.Copy),
                    [], [bPS[pb], bST[ft]])
            for ft in range(8):
                dve(lambda e, ft=ft: e.memset(ST[:, ft, 0:OWN0], 0.0), [], [bST[ft]])
            def epi_glu(mt, pb):
                act(lambda e: e.activation(out=AO[:, mt, :], in_=PS[pb][:, 0:NT], func=AF.Sigmoid,
                                           bias=GLBt[:, mt:mt + 1], scale=1.0), C, [bPS[pb], bAO[mt]])
            linear(ST, bST, 8, w_glu, 0, 8, epi_glu, cname='glu')
            for mt in range(8):
                dve(lambda e, mt=mt: e.tensor_tensor(out=ST[:, mt, :], in0=ST[:, mt, :], in1=AO[:, mt, :], op=ALU.mult),
                    [bAO[mt]], [bST[mt]])
            for dt_ in range(16):
                vb, bb_ = load_slab(("t", w_in, 32 + dt_, 0), 16, 128, ckey=('gb', dt_))
                vwb, bwb = load_slab(("t", w_bb, dt_, 0), 8, 128, ckey=('bb', dt_))
                pgb = nps(); pbb = nps()
                for k in range(16):
                    pe(lambda e, k=k, pgb=pgb, vb=vb: e.matmul(PS[pgb][:, 0:NT], lhsT=vb.w(k, 0), rhs=XN[:, k, :],
                                                               start=(k == 0), stop=(k == 15)), [bb_, bXN[k]], [bPS[pgb]])
                for k in range(8):
                    pe(lambda e, k=k, pbb=pbb, vwb=vwb: e.matmul(PS[pbb][:, 0:NT], lhsT=vwb.w(k, 0), rhs=ST[:, k, :],
                                                                 start=(k == 0), stop=(k == 7)), [bwb, bST[k]], [bPS[pbb]])
                act(lambda e, pgb=pgb, dt_=dt_: e.activation(out=TMP[:, 1, :], in_=PS[pgb][:, 0:NT], func=AF.Sigmoid,
                                                             bias=BGt[:, 16 + dt_:17 + dt_], scale=1.0), C, [bPS[pgb], bTMP[1]])
                dve(lambda e, pbb=pbb: e.tensor_tensor(out=TMP[:, 1, :], in0=PS[pbb][:, 0:NT], in1=TMP[:, 1, :], op=ALU.mult),
                    [], [bPS[pbb], bTMP[1]])
                dve(lambda e, dt_=dt_: e.tensor_tensor(out=MG[:, dt_, :], in0=MG[:, dt_, :], in1=TMP[:, 1, :], op=ALU.add),
                    [bTMP[1]], [bMG[dt_]])

            def epi_o(mt, pb):
                dve(lambda e: e.tensor_tensor(out=X[:, mt, :], in0=PS[pb][:, 0:NT], in1=X[:, mt, :], op=ALU.add),
                    [], [bPS[pb], bX[mt]])
            linear(MG, bMG, 16, w_o, 0, 16, epi_o, cname='wo')
            ffn(2, w_g2, w_u2, w_d2, 'f2')
            rmsnorm(3)
            for k in range(16):
                q = k % 2
                dve(lambda e, k=k, q=q: e.scalar_tensor_tensor(out=OST[:, q, :], in0=X[:, k, OWN0:NT], scalar=GN[:, 3, k:k + 1],
                                                               in1=RS[:, OWN0:NT], op0=ALU.mult, op1=ALU.mult),
                    [bX[k], bRS] + C, [bOST[q]])
                S.add('sp', lambda e, k=k, q=q, ti=ti: e.dma_start(out=yT[ti, k * 128:(k + 1) * 128, :], in_=OST[:, q, :]),
                      reads=[bOST[q]], dkey=f"oy{q}")
            carry_state()

        for ti in range(3):
            back(ti)
        S.emit(nc, "m", sem_es)
    return nc


_NC_CACHE = {}


def _tile_w(w):
    K, M = w.shape
    return np.ascontiguousarray(np.asarray(w, np.float32).reshape(K // 128, 128, M // 128, 128).transpose(2, 1, 0, 3)
                                ).reshape(M // 128, 128, K)


def _prep_shared(inp):
    f = np.float32
    sh = {}
    sh["w_g1"] = _tile_w(inp["ffn1_w_gate"][0]); sh["w_u1"] = _tile_w(inp["ffn1_w_up"][0])
    sh["w_d1"] = _tile_w(inp["ffn1_w_down"][0])
    sh["w_g2"] = _tile_w(inp["ffn2_w_gate"][0]); sh["w_u2"] = _tile_w(inp["ffn2_w_up"][0])
    sh["w_d2"] = _tile_w(inp["ffn2_w_down"][0])
    sh["w_in"] = _tile_w(inp["w_in"][0])
    sh["w_pool"] = np.ascontiguousarray(inp["pool_w"][0], f)
    sh["w_glu"] = np.ascontiguousarray(inp["glu_w"][0], f)
    sh["w_ba"] = _tile_w(inp["w_branch_a"][0]); sh["w_bb"] = _tile_w(inp["w_branch_b"][0])
    sh["w_o"] = np.ascontiguousarray(inp["w_out"][0], f)
    g = np.stack([inp["norm_ffn1"][0], inp["norm_mix"][0], inp["norm_ffn2"][0], inp["final_norm"]], 0)
    sh["gains"] = np.ascontiguousarray(g.reshape(4, 16, 128).transpose(2, 0, 1), f)
    sh["bgate"] = np.ascontiguousarray(inp["b_gate"][0].reshape(32, 128).T, f)
    sh["glub"] = np.ascontiguousarray(inp["glu_b"][0].reshape(8, 128).T, f)
    sh["pscale"] = np.ascontiguousarray(inp["pool_scale"][0].reshape(8, 128).T, f)
    lrT = inp["ssm_lambda_re"][0].T; liT = inp["ssm_lambda_im"][0].T
    sh["lr2"] = np.ascontiguousarray(np.concatenate([lrT, lrT], 0), f)
    sh["li2"] = np.ascontiguousarray(np.concatenate([liT, liT], 0), f)
    sh["ldt"] = np.ascontiguousarray(np.broadcast_to(inp["ssm_log_dt"][0][None, :], (128, 64)), f)
    br = inp["ssm_b_re"][0].transpose(1, 0, 2); bi = inp["ssm_b_im"][0].transpose(1, 0, 2)
    sh["sb_in"] = np.ascontiguousarray(np.concatenate([br, bi], 0), f)
    sh["wb_in"] = np.ascontiguousarray(np.concatenate([bi, br], 0), f)
    cr = inp["ssm_c_re"][0].transpose(2, 0, 1); ci = inp["ssm_c_im"][0].transpose(2, 0, 1)
    sh["n1_in"] = np.ascontiguousarray(np.concatenate([cr, ci], 0), f)
    sh["n2_in"] = np.ascontiguousarray(np.concatenate([ci, cr], 0), f)
    dd = inp["ssm_d"][0].reshape(64, 16)
    sh["dd_in"] = np.ascontiguousarray(np.tile(dd.T, (8, 1)), f)
    s_idx = np.arange(128) // 16
    sh["mask_in"] = (s_idx[None, :] >= s_idx[:, None]).astype(f)
    sh["ident_in"] = np.eye(128, dtype=f)
    return sh


def kernel(**inp):
    inp = {k: np.asarray(v) for k, v in inp.items()}
    f = np.float32
    if "nc" not in _NC_CACHE:
        _NC_CACHE["nc"] = build_program()
    nc = _NC_CACHE["nc"]
    sh = _prep_shared(inp)
    xp = inp["x_prompt"].astype(f); xs = inp["x_sample"].astype(f)
    meta = inp["meta_tokens"].astype(f)
    st_pool = inp["state_pool"][0]; st_re = inp["state_ssm_re"][0]; st_im = inp["state_ssm_im"][0]
    in_maps = []
    for c in range(8):
        b, r = c // 2, c % 2
        seq = np.concatenate([meta, xp[b]], 0)
        own = seq[r * 1032:(r + 1) * 1032]
        pre = seq[0:1032] if r == 1 else np.zeros((1032, D), f)
        halo = seq[1032 - 15:1032] if r == 1 else np.zeros((15, D), f)
        ownh = np.concatenate([halo, own], 0)
        xin = np.zeros((6, NT, D), f)
        hist = np.zeros((3, NSL, 15, 1024), f)
        h0r = np.zeros((3, NSL, 64, 64), f); h0i = np.zeros((3, NSL, 64, 64), f)
        for t in range(3):
            xin[t, OWN0:OWN0 + OWN] = pre[t * OWN:(t + 1) * OWN]
            xin[3 + t, 0:SMP0] = ownh[t * OWN:t * OWN + SMP0]
            for i in range(NSL):
                sl = t * NSL + i
                if sl < 16:
                    sq = 16 * c + sl
                    xin[3 + t, SMP0 + 4 * i:SMP0 + 4 * i + 4] = xs[sq]
                    hist[t, i] = st_pool[sq]
                    h0r[t, i] = st_re[sq]; h0i[t, i] = st_im[sq]
        m = dict(sh)
        m["xin"] = np.ascontiguousarray(xin.transpose(0, 2, 1))
        m["hist_in"] = np.ascontiguousarray(hist.transpose(0, 3, 1, 2))
        hr = h0r.transpose(0, 3, 2, 1); hi = h0i.transpose(0, 3, 2, 1)
        m["sh0_in"] = np.ascontiguousarray(np.concatenate([hr, hi], 1))
        m["wh0_in"] = np.ascontiguousarray(np.concatenate([hi, hr], 1))
        in_maps.append(m)
    res = run_bass_kernel_spmd(nc, in_maps, core_ids=list(range(8)))
    y_prompt = np.zeros((4, 2048, D), f); y_sample = np.zeros((128, 4, D), f)
    pool_pp = np.zeros((1, 4, 15, 1024), f); pool_ss = np.zeros((1, 128, 15, 1024), f)
    re_p = np.zeros((1, 4, 64, 64), f); im_p = np.zeros((1, 4, 64, 64), f)
    re_s = np.zeros((1, 128, 64, 64), f); im_s = np.zeros((1, 128, 64, 64), f)
    for c in range(8):
        b, r = c // 2, c % 2
        o = res.results[c]
        yT = np.asarray(o["yT"])
        yo = np.concatenate([yT[t, :, 0:OWN].T for t in range(3)], 0)
        if r == 0:
            y_prompt[b, 0:1016] = yo[16:]
        else:
            y_prompt[b, 1016:2048] = yo
            pool_pp[0, b] = np.asarray(o["pool_p"]).T
            sp = np.asarray(o["ssm_p"])
            re_p[0, b] = sp[0:64].T; im_p[0, b] = sp[64:128].T
        ps_ = np.asarray(o["pool_s"]); ss_ = np.asarray(o["ssm_s"])
        for t in range(3):
            for i in range(NSL):
                sl = t * NSL + i
                if sl < 16:
                    sq = 16 * c + sl
                    y_sample[sq] = yT[t, :, OWN + 4 * i:OWN + 4 * i + 4].T
                    pool_ss[0, sq] = ps_[t, :, i, :].T
                    re_s[0, sq] = ss_[t, 0:64, :, i].T; im_s[0, sq] = ss_[t, 64:128, :, i].T
    return (y_prompt, y_sample, pool_pp, pool_ss, re_p, im_p, re_s, im_s)
```

```python
import math
from contextlib import ExitStack
import numpy as np
import concourse.bass as bass
import concourse.mybir as mybir
from concourse.bass_utils import run_bass_kernel_spmd

F32 = mybir.dt.float32
BF16 = mybir.dt.bfloat16
AF = mybir.ActivationFunctionType
ALU = mybir.AluOpType

D = 2048
DFF = 5632
NT = 383
OWN0 = 15
OWN = 344
SMP0 = 359
NSL = 6
NCH = 43
NCOL = NCH + NSL
TWO_PI = 2.0 * math.pi


class Buf:
    def __init__(self, name):
        self.name = name
        self.lw = None
        self.rd = {}


class Op:
    pass


class Sched:
    ENG = ['pe', 'act', 'dve', 'pool', 'sp']

    def __init__(self):
        self.q = {e: [] for e in self.ENG}
        self.dma_cnt = {}
        self.dma_ops = {}

    def add(self, eng, fn, reads=(), writes=(), dkey=None, inc=16):
        o = Op()
        o.eng = eng
        o.fn = fn
        o.dkey = dkey
        o.inc = inc
        o.sig = False
        o.deps = []
        ds = []
        for b in reads:
            if b.lw is not None:
                ds.append(b.lw)
        for b in writes:
            if b.lw is not None:
                ds.append(b.lw)
            ds.extend(b.rd.values())
        rk = eng if dkey is None else ('d', dkey)
        for b in reads:
            b.rd[rk] = o
        for b in writes:
            b.lw = o
            b.rd = {}
        seen = set()
        for d in ds:
            if d is o or id(d) in seen:
                continue
            seen.add(id(d))
            if d.dkey is None and d.eng == 'pe' and eng == 'pe' and dkey is None:
                continue
            o.deps.append(d)
            if d.dkey is None:
                d.sig = True
        if dkey is not None:
            self.dma_cnt[dkey] = self.dma_cnt.get(dkey, 0) + inc
            o.dval = self.dma_cnt[dkey]
            self.dma_ops.setdefault(dkey, []).append(o)
        self.q[eng].append(o)
        return o

    def finalize_key(self, key):
        for o in self.dma_ops.get(key, []):
            o.dval = self.dma_cnt[key]

    def emit(self, nc, tag, sem_es):
        with ExitStack() as es:
            sems = {}
            for e in ['pe', 'act', 'dve', 'pool', 'sp']:
                sems[('e', e)] = sem_es.enter_context(nc.semaphore(f"{tag}_e_{e}"))
            for k in self.dma_cnt:
                sems[('d', k)] = sem_es.enter_context(nc.semaphore(f"{tag}_d_{k}"))
            final = {}
            for e in self.ENG:
                c = 0
                for o in self.q[e]:
                    if o.dkey is None and o.sig:
                        c += 1
                        o.sval = c
                final[('e', e)] = c
            for k, v in self.dma_cnt.items():
                final[('d', k)] = v
            block = es.enter_context(nc.Block())

            def run(engname, eng):
                waited = {}
                for o in self.q[engname]:
                    need = {}
                    for d in o.deps:
                        if d.dkey is None:
                            key = ('e', d.eng)
                            val = d.sval
                        else:
                            key = ('d', d.dkey)
                            val = d.dval
                        if val > need.get(key, 0):
                            need[key] = val
                    for key, val in need.items():
                        if waited.get(key, 0) >= val:
                            continue
                        waited[key] = val
                        eng.wait_ge(sems[key], val)
                    ins = o.fn(eng)
                    if o.dkey is not None:
                        ins.then_inc(sems[('d', o.dkey)], o.inc)
                    elif o.sig:
                        ins.then_inc(sems[('e', o.eng)], 1)
                for key, val in final.items():
                    if val > 0 and waited.get(key, 0) < val:
                        eng.wait_ge(sems[key], val)

            @block.tensor
            def _(eng):
                run('pe', eng)

            @block.scalar
            def _(eng):
                run('act', eng)

            @block.vector
            def _(eng):
                run('dve', eng)

            @block.gpsimd
            def _(eng):
                run('pool', eng)

            @block.sync
            def _(eng):
                run('sp', eng)


def bc_last(ap, n):
    return ap.unsqueeze(2).broadcast_to([ap.shape[0], ap.shape[1], n])


def build_program(stage="full"):
    nc = bass.Bass("TRN2", target_bir_lowering=False)
    sem_es = ExitStack()

    def din(name, shape):
        return nc.dram_tensor(name, list(shape), F32, kind="ExternalInput").ap()

    def dout(name, shape):
        return nc.dram_tensor(name, list(shape), F32, kind="ExternalOutput").ap()

    xin = din("xin", [6, D, NT])
    hist_in = din("hist_in", [3, 1024, NSL, 15])
    sh0_in = din("sh0_in", [3, 128, 64, NSL])
    wh0_in = din("wh0_in", [3, 128, 64, NSL])
    w_g1 = din("w_g1", [44, 128, D]); w_u1 = din("w_u1", [44, 128, D]); w_d1 = din("w_d1", [16, 128, DFF])
    w_g2 = din("w_g2", [44, 128, D]); w_u2 = din("w_u2", [44, 128, D]); w_d2 = din("w_d2", [16, 128, DFF])
    w_in = din("w_in", [48, 128, D])
    w_pool = din("w_pool", [4, 256, 256])
    w_glu = din("w_glu", [1024, 1024])
    w_ba = din("w_ba", [16, 128, 1024]); w_bb = din("w_bb", [16, 128, 1024]); w_o = din("w_o", [D, D])
    gains = din("gains", [128, 4, 16])
    bgate = din("bgate", [128, 32])
    glub = din("glub", [128, 8])
    pscale = din("pscale", [128, 8])
    lr2 = din("lr2", [128, 64]); li2 = din("li2", [128, 64]); ldt = din("ldt", [128, 64])
    sb_in = din("sb_in", [128, 64, 16]); wb_in = din("wb_in", [128, 64, 16])
    n1_in = din("n1_in", [128, 64, 16]); n2_in = din("n2_in", [128, 64, 16])
    dd_in = din("dd_in", [128, 64])
    mask_in = din("mask_in", [128, 128])
    ident_in = din("ident_in", [128, 128])

    yT = dout("yT", [3, D, 368])
    pool_p = dout("pool_p", [1024, 15])
    pool_s = dout("pool_s", [3, 1024, NSL, 15])
    ssm_p = dout("ssm_p", [128, 64])
    ssm_s = dout("ssm_s", [3, 128, 64, NSL])

    ms = nc.dram_tensor("ms_scr", [4, 128, 8192], BF16, kind="ExternalOutput").ap()
    tb = nc.dram_tensor("tb_scr", [4, 128, 64], F32, kind="ExternalOutput").ap()
    XS = nc.dram_tensor("xs_scr", [3, 128, 16 * NT], F32).ap()
    US = nc.dram_tensor("us_scr", [3, 128, 64 * NCOL], BF16).ap()
    GS = nc.dram_tensor("gs_scr", [3, 128, 64 * NCOL], F32).ap()
    WS_ = nc.dram_tensor("ws_scr", [3, 128, 64 * NCOL], BF16).ap()
    WC = nc.dram_tensor("wc_scr", [256, 128, 4096], BF16).ap()
    cc_in = nc.dram_tensor("cc_in", [128, 512], F32)
    cc_out = nc.dram_tensor("cc_out", [128, 512], F32)

    with ExitStack() as es:
        S = Sched()

        def sb(name, shape, dt=F32):
            return es.enter_context(nc.sbuf_tensor(name, list(shape), dt))

        LR = sb("s_lr", [128, 64]); LI = sb("s_li", [128, 64]); LDT = sb("s_ldt", [128, 64])
        SB0 = sb("s_sb", [128, 64, 16]); WB0 = sb("s_wb", [128, 64, 16])
        N1 = sb("s_n1", [128, 64, 16]); N2 = sb("s_n2", [128, 64, 16])
        SBb = sb("s_sbb", [128, 64, 16]); WBb = sb("s_wbb", [128, 64, 16])
        DDt = sb("s_dd", [128, 64]); MASK = sb("s_mask", [128, 128]); IDF = sb("s_id", [128, 128])
        DT = sb("s_dt", [128, 64]); DLR = sb("s_dlr", [128, 64]); DLI = sb("s_dli", [128, 64])
        TR = sb("s_tr", [128, 9, 64]); TI = sb("s_ti", [128, 9, 64])
        TRn = sb("s_trn", [128, 9, 64]); TIn = sb("s_tin", [128, 9, 64])
        MAG = sb("s_mag", [128, 9, 64]); MAGn = sb("s_magn", [128, 9, 64])
        SN = sb("s_sn", [128, 9, 64]); CS = sb("s_cs", [128, 9, 64])
        R1 = sb("s_r1", [128, 9, 64]); R2 = sb("s_r2", [128, 9, 64])
        NI = sb("s_ni", [128, 64], mybir.dt.int32)
        QA = sb("s_qa", [128, 64]); QB = sb("s_qb", [128, 64]); QC = sb("s_qc", [128, 64])
        QR = sb("s_qr", [128, 64]); QI = sb("s_qi", [128, 64])
        T16a = sb("s_t16a", [128, 64, 16]); T16b = sb("s_t16b", [128, 64, 16])
        BNp = sb("s_bnp", [128, 64, 8, 16]); BPs = sb("s_bps", [128, 64, 8, 16])
        WBPs = sb("s_wbps", [128, 64, 8, 16]); M3f = sb("s_m3f", [128, 64, 8, 16])
        MO = [sb(f"s_mo{i}", [128, 16, 128], BF16) for i in range(4)]
        TMPM = [sb(f"s_tmpm{i}", [128, 128]) for i in range(2)]
        PSs = [es.enter_context(nc.psum_tensor(f"s_ps{i}", [128, 512], F32)) for i in range(8)]

        b_const = Buf("const")
        bT = {n: Buf(n) for n in ["DT", "DLR", "DLI", "TR", "TI", "TRn", "TIn", "MAG", "MAGn", "SN", "CS",
                                  "R1", "R2", "QA", "QB", "QC", "QR", "QI", "T16a", "T16b", "SBb", "WBb",
                                  "BNp", "BPs", "WBPs", "M3f", "N1", "N2", "WB0"]}
        bMO = [Buf(f"MO{i}") for i in range(4)]
        bTMPM = [Buf(f"TMPM{i}") for i in range(2)]
        bPS = [Buf(f"sps{i}") for i in range(8)]

        def ld(dst, src):
            b_const.lw = S.add('sp', lambda e, dst=dst, src=src: e.dma_start(out=dst, in_=src), dkey="const")

        ld(LR[:], lr2); ld(LI[:], li2); ld(LDT[:], ldt)
        ld(SB0[:], sb_in); ld(WB0[:], wb_in); ld(N1[:], n1_in); ld(N2[:], n2_in)
        ld(DDt[:], dd_in); ld(MASK[:], mask_in); ld(IDF[:], ident_in)
        S.finalize_key("const")
        C = [b_const]

        def dve(fn, reads, writes):
            S.add('dve', fn, reads=reads, writes=writes)

        def act(fn, reads, writes):
            S.add('act', fn, reads=reads, writes=writes)

        dve(lambda e: e.tensor_scalar(out=WB0[0:64], in0=WB0[0:64], scalar1=-1.0, scalar2=None, op0=ALU.mult),
            C, [bT["WB0"]])
        dve(lambda e: e.tensor_scalar(out=N1[64:128], in0=N1[64:128], scalar1=-1.0, scalar2=None, op0=ALU.mult),
            C, [bT["N1"]])
        dve(lambda e: e.tensor_scalar(out=N2[:], in0=N2[:], scalar1=-1.0, scalar2=None, op0=ALU.mult),
            C, [bT["N2"]])
        act(lambda e: e.activation(out=DT[:], in_=LDT[:], func=AF.Exp), C, [bT["DT"]])
        dve(lambda e: e.tensor_tensor(out=DLR[:], in0=DT[:], in1=LR[:], op=ALU.mult), C + [bT["DT"]], [bT["DLR"]])
        dve(lambda e: e.tensor_tensor(out=DLI[:], in0=DT[:], in1=LI[:], op=ALU.mult), C + [bT["DT"]], [bT["DLI"]])
        for k in range(1, 9):
            act(lambda e, k=k: e.activation(out=MAG[:, k], in_=DLR[:], func=AF.Exp, scale=float(k)),
                [bT["DLR"]], [bT["MAG"]])
            act(lambda e, k=k: e.activation(out=MAGn[:, k], in_=DLR[:], func=AF.Exp, scale=float(-k)),
                [bT["DLR"]], [bT["MAGn"]])
            for (RR, off) in ((R1, 0.0), (R2, math.pi / 2)):
                nm = "R1" if RR is R1 else "R2"
                dve(lambda e, k=k, RR=RR, off=off: e.tensor_scalar(out=RR[:, k], in0=DLI[:], scalar1=float(k), scalar2=off,
                                                                  op0=ALU.mult, op1=ALU.add), [bT["DLI"]], [bT[nm]])
                dve(lambda e, k=k, RR=RR: e.tensor_scalar(out=QA[:], in0=RR[:, k], scalar1=1.0 / TWO_PI, scalar2=None,
                                                          op0=ALU.mult), [bT[nm]], [bT["QA"]])
                dve(lambda e: e.tensor_copy(out=NI[:], in_=QA[:]), [bT["QA"]], [bT["QB"]])
                dve(lambda e: e.tensor_copy(out=QA[:], in_=NI[:]), [bT["QB"]], [bT["QA"]])
                dve(lambda e, k=k, RR=RR: e.scalar_tensor_tensor(out=RR[:, k], in0=QA[:], scalar=-TWO_PI, in1=RR[:, k],
                                                                 op0=ALU.mult, op1=ALU.add), [bT["QA"]], [bT[nm]])
                dve(lambda e, k=k, RR=RR: e.tensor_scalar(out=QA[:], in0=RR[:, k], scalar1=math.pi, scalar2=-TWO_PI,
                                                          op0=ALU.is_gt, op1=ALU.mult), [bT[nm]], [bT["QA"]])
                dve(lambda e, k=k, RR=RR: e.tensor_tensor(out=RR[:, k], in0=RR[:, k], in1=QA[:], op=ALU.add),
                    [bT["QA"]], [bT[nm]])
        for k in range(1, 9):
            act(lambda e, k=k: e.activation(out=SN[:, k], in_=R1[:, k], func=AF.Sin), [bT["R1"]], [bT["SN"]])
            act(lambda e, k=k: e.activation(out=CS[:, k], in_=R2[:, k], func=AF.Sin), [bT["R2"]], [bT["CS"]])
        PY = sb("s_py", [128, 64]); PY2 = sb("s_py2", [128, 64]); PP = sb("s_pp", [128, 64])
        PSn = sb("s_psn", [128, 64]); PCs = sb("s_pcs", [128, 64]); PT = sb("s_pt", [128, 64])
        bP = {n: Buf(n) for n in ["PY", "PY2", "PP", "PSn", "PCs", "PT"]}
        for k in (8, 4):
            dve(lambda e, k=k: e.tensor_scalar(out=PY[:], in0=R1[:, k], scalar1=1.0 / 16, scalar2=None, op0=ALU.mult),
                [bT["R1"]], [bP["PY"]])
            dve(lambda e: e.tensor_tensor(out=PY2[:], in0=PY[:], in1=PY[:], op=ALU.mult), [bP["PY"]], [bP["PY2"]])
            dve(lambda e: e.tensor_scalar(out=PP[:], in0=PY2[:], scalar1=-1.0 / 5040, scalar2=None, op0=ALU.mult),
                [bP["PY2"]], [bP["PP"]])
            for cc in (1.0 / 120, -1.0 / 6):
                dve(lambda e, cc=cc: e.scalar_tensor_tensor(out=PP[:], in0=PP[:], scalar=cc, in1=PY2[:], op0=ALU.add, op1=ALU.mult),
                    [bP["PY2"]], [bP["PP"]])
            dve(lambda e: e.scalar_tensor_tensor(out=PSn[:], in0=PP[:], scalar=1.0, in1=PY[:], op0=ALU.add, op1=ALU.mult),
                [bP["PP"], bP["PY"]], [bP["PSn"]])
            dve(lambda e: e.tensor_scalar(out=PP[:], in0=PY2[:], scalar1=1.0 / 40320, scalar2=None, op0=ALU.mult),
                [bP["PY2"]], [bP["PP"]])
            for cc in (-1.0 / 720, 1.0 / 24, -0.5):
                dve(lambda e, cc=cc: e.scalar_tensor_tensor(out=PP[:], in0=PP[:], scalar=cc, in1=PY2[:], op0=ALU.add, op1=ALU.mult),
                    [bP["PY2"]], [bP["PP"]])
            dve(lambda e: e.tensor_scalar(out=PCs[:], in0=PP[:], scalar1=1.0, scalar2=None, op0=ALU.add),
                [bP["PP"]], [bP["PCs"]])
            for _ in range(4):
                dve(lambda e: e.tensor_tensor(out=PT[:], in0=PSn[:], in1=PSn[:], op=ALU.mult), [bP["PSn"]], [bP["PT"]])
                dve(lambda e: e.tensor_tensor(out=PSn[:], in0=PSn[:], in1=PCs[:], op=ALU.mult), [bP["PCs"], bP["PT"]], [bP["PSn"]])
                dve(lambda e: e.tensor_scalar(out=PSn[:], in0=PSn[:], scalar1=2.0, scalar2=None, op0=ALU.mult), [], [bP["PSn"]])
                dve(lambda e: e.tensor_scalar(out=PCs[:], in0=PT[:], scalar1=-2.0, scalar2=1.0, op0=ALU.mult, op1=ALU.add),
                    [bP["PT"], bP["PSn"]], [bP["PCs"]])
            dve(lambda e, k=k: e.tensor_copy(out=SN[:, k], in_=PSn[:]), [bP["PSn"]], [bT["SN"]])
            dve(lambda e, k=k: e.tensor_copy(out=CS[:, k], in_=PCs[:]), [bP["PCs"]], [bT["CS"]])
        for k in range(1, 9):
            dve(lambda e, k=k: e.scalar_tensor_tensor(out=TR[:, k], in0=CS[:, k], scalar=1.0, in1=MAG[:, k],
                                                      op0=ALU.mult, op1=ALU.mult), [bT["CS"], bT["MAG"]], [bT["TR"]])
            dve(lambda e, k=k: e.scalar_tensor_tensor(out=TI[:, k], in0=SN[:, k], scalar=1.0, in1=MAG[:, k],
                                                      op0=ALU.mult, op1=ALU.mult), [bT["SN"], bT["MAG"]], [bT["TI"]])
            dve(lambda e, k=k: e.scalar_tensor_tensor(out=TRn[:, k], in0=CS[:, k], scalar=1.0, in1=MAGn[:, k],
                                                      op0=ALU.mult, op1=ALU.mult), [bT["CS"], bT["MAGn"]], [bT["TRn"]])
            dve(lambda e, k=k: e.scalar_tensor_tensor(out=TIn[:, k], in0=SN[:, k], scalar=-1.0, in1=MAGn[:, k],
                                                      op0=ALU.mult, op1=ALU.mult), [bT["SN"], bT["MAGn"]], [bT["TIn"]])
        dve(lambda e: e.tensor_scalar(out=QA[:], in0=TR[:, 1], scalar1=-1.0, scalar2=None, op0=ALU.add),
            [bT["TR"]], [bT["QA"]])
        dve(lambda e: e.tensor_tensor(out=QB[:], in0=LR[:], in1=LR[:], op=ALU.mult), C, [bT["QB"]])
        dve(lambda e: e.tensor_tensor(out=QC[:], in0=LI[:], in1=LI[:], op=ALU.mult), C, [bT["QC"]])
        dve(lambda e: e.tensor_tensor(out=QB[:], in0=QB[:], in1=QC[:], op=ALU.add), [bT["QB"], bT["QC"]], [bT["QB"]])
        dve(lambda e: e.reciprocal(out=QB[:], in_=QB[:]), [bT["QB"]], [bT["QB"]])
        dve(lambda e: e.tensor_tensor(out=QR[:], in0=QA[:], in1=LR[:], op=ALU.mult), [bT["QA"]] + C, [bT["QR"]])
        dve(lambda e: e.tensor_tensor(out=QC[:], in0=TI[:, 1], in1=LI[:], op=ALU.mult), [bT["TI"], bT["QC"]] + C, [bT["QC"]])
        dve(lambda e: e.tensor_tensor(out=QR[:], in0=QR[:], in1=QC[:], op=ALU.add), [bT["QR"], bT["QC"]], [bT["QR"]])
        dve(lambda e: e.tensor_tensor(out=QR[:], in0=QR[:], in1=QB[:], op=ALU.mult), [bT["QR"], bT["QB"]], [bT["QR"]])
        dve(lambda e: e.tensor_tensor(out=QI[:], in0=TI[:, 1], in1=LR[:], op=ALU.mult), [bT["TI"]] + C, [bT["QI"]])
        dve(lambda e: e.tensor_tensor(out=QC[:], in0=QA[:], in1=LI[:], op=ALU.mult), [bT["QA"], bT["QC"]] + C, [bT["QC"]])
        dve(lambda e: e.tensor_tensor(out=QI[:], in0=QI[:], in1=QC[:], op=ALU.subtract), [bT["QI"], bT["QC"]], [bT["QI"]])
        dve(lambda e: e.tensor_tensor(out=QI[:], in0=QI[:], in1=QB[:], op=ALU.mult), [bT["QI"], bT["QB"]], [bT["QI"]])

        def cmul(out_ap, ar, ai, s_ap, w_ap, sign, reads, wbuf):
            dve(lambda e: e.tensor_tensor(out=T16a[:], in0=s_ap, in1=bc_last(ar, 16), op=ALU.mult),
                reads + [bT["T16a"]], [bT["T16a"]])
            dve(lambda e: e.tensor_tensor(out=T16b[:], in0=w_ap, in1=bc_last(ai, 16), op=ALU.mult),
                reads + [bT["T16b"]], [bT["T16b"]])
            dve(lambda e: e.tensor_tensor(out=out_ap, in0=T16a[:], in1=T16b[:],
                                          op=(ALU.add if sign > 0 else ALU.subtract)),
                [bT["T16a"], bT["T16b"]], [wbuf])

        Cq = C + [bT["QR"], bT["QI"], bT["WB0"]]
        cmul(SBb[:], QR[:], QI[:], SB0[:], WB0[:], +1, Cq, bT["SBb"])
        cmul(WBb[:], QR[:], QI[:], WB0[:], SB0[:], -1, Cq, bT["WBb"])
        Cb = [bT["SBb"], bT["WBb"], bT["TR"], bT["TI"], bT["TRn"], bT["TIn"], bT["N1"], bT["N2"]]
        for s in range(8):
            cmul(BNp[:, :, s, :], TRn[:, s + 1], TIn[:, s + 1], SBb[:], WBb[:], +1, Cb, bT["BNp"])
            if s == 7:
                dve(lambda e: e.tensor_copy(out=BPs[:, :, 7, :], in_=SBb[:]), Cb, [bT["BPs"]])
                dve(lambda e: e.tensor_copy(out=WBPs[:, :, 7, :], in_=WBb[:]), Cb, [bT["WBPs"]])
            else:
                cmul(BPs[:, :, s, :], TR[:, 7 - s], TI[:, 7 - s], SBb[:], WBb[:], +1, Cb, bT["BPs"])
                cmul(WBPs[:, :, s, :], TR[:, 7 - s], TI[:, 7 - s], WBb[:], SBb[:], -1, Cb, bT["WBPs"])
            cmul(M3f[:, :, s, :], TR[:, s + 1], TI[:, s + 1], N1[:], N2[:], +1, Cb, bT["M3f"])
        cnt = 0
        for g0 in range(0, 64, 16):
            act(lambda e, g0=g0: e.activation(out=MO[3][:].rearrange("p g m -> p (g m)"),
                                       in_=M3f[:, g0:g0 + 16].rearrange("p g s c -> p (g s c)"), func=AF.Copy),
                [bT["M3f"]], [bMO[3]])
            for g in range(g0, g0 + 16):
                gl = g - g0
                pb = cnt % 8; cnt += 1
                S.add('pe', lambda e, g=g, pb=pb: e.matmul(PSs[pb][:, 0:128],
                                                          lhsT=BNp[:, g].rearrange("p s c -> p (s c)"),
                                                          rhs=M3f[:, g].rearrange("p s c -> p (s c)"),
                                                          start=True, stop=True),
                      reads=[bT["BNp"], bT["M3f"]], writes=[bPS[pb]])
                tm = g % 2
                dve(lambda e, pb=pb, tm=tm: e.tensor_tensor(out=TMPM[tm][:], in0=PSs[pb][:, 0:128], in1=MASK[:], op=ALU.mult),
                    C, [bPS[pb], bTMPM[tm]])
                dve(lambda e, g=g, gl=gl, tm=tm: e.scalar_tensor_tensor(out=MO[0][:, gl, :], in0=IDF[:], scalar=DDt[:, g:g + 1],
                                                                in1=TMPM[tm][:], op0=ALU.mult, op1=ALU.add),
                    C + [bTMPM[tm]], [bMO[0]])
                for (src_, bsrc, mi) in ((BPs, bT["BPs"], 1), (WBPs, bT["WBPs"], 2)):
                    pb = cnt % 8; cnt += 1
                    S.add('pe', lambda e, g=g, pb=pb, src_=src_: e.transpose(PSs[pb][:, 0:128],
                                                                          src_[:, g].rearrange("p s c -> p (s c)"), IDF[:]),
                          reads=[bsrc] + C, writes=[bPS[pb]])
                    act(lambda e, gl=gl, pb=pb, mi=mi: e.activation(out=MO[mi][:, gl, :], in_=PSs[pb][:, 0:128], func=AF.Copy),
                        [], [bPS[pb], bMO[mi]])
            for i in range(4):
                S.add('sp', lambda e, i=i, g0=g0: e.dma_start(out=ms[i][:, g0 * 128:(g0 + 16) * 128],
                                                          in_=MO[i][:].rearrange("p g m -> p (g m)")),
                      reads=[bMO[i]], dkey=f"mso{i}")
        for i, (tt, kk) in enumerate(((TR, 8), (TI, 8), (TRn, 4), (TIn, 4))):
            S.add('sp', lambda e, i=i, tt=tt, kk=kk: e.dma_start(out=tb[i], in_=tt[:, kk]),
                  reads=[bT["TR"], bT["TI"], bT["TRn"], bT["TIn"]], dkey="mso")
        S.emit(nc, "s", sem_es)

    if stage == "setup":
        return nc
    with ExitStack() as es:
        S = Sched()

        def sb(name, shape, dt=F32):
            return es.enter_context(nc.sbuf_tensor(name, list(shape), dt))

        X = sb("X", [128, 16, NT]); XN = sb("XN", [128, 16, NT], BF16)
        H = sb("H", [128, 11, NT], BF16)
        NSLAB = 7
        SL = sb("SL", [128, NSLAB, 4096], BF16)
        USS = sb("USS", [128, 8, NT], BF16)
        ST = sb("ST", [128, 8, NT], BF16)
        AO = sb("AO", [128, 8, NT], BF16)
        MG = sb("MG", [128, 16, NT], BF16)
        ZC = sb("ZC", [128, 8, 8, 16], BF16); ZS = sb("ZS", [128, 8, 8, 16], BF16)
        U = sb("U", [128, 64, NCOL], BF16); WG = sb("WG", [128, 64, NCOL], BF16)
        GB = sb("GB", [128, 64, NCOL + 1]); HB = sb("HB", [128, 64, NCOL], BF16)
        Wst = [sb(f"Wst{i}", [128, 64]) for i in range(2)]
        T1 = sb("T1", [128, 64]); T2 = sb("T2", [128, 64]); U1 = sb("U1", [128, 64]); U2 = sb("U2", [128, 64])
        SH0 = sb("SH0", [128, 64, NSL]); WH0 = sb("WH0", [128, 64, NSL])
        HPS = sb("HPS", [128, 64, NSL]); HPW = sb("HPW", [128, 64, NSL])
        T6a = sb("T6a", [128, 64, NSL]); T6b = sb("T6b", [128, 64, NSL])
        PW = SMP0 + NSL * 19
        UP = sb("UP", [128, 2, PW]); WA = sb("WA", [128, 2, PW]); WB = sb("WB", [128, 2, PW])
        Dm = sb("Dm", [128, 2, NT], BF16)
        SQ = sb("SQ", [128, 2, NT], BF16); RS = sb("RS", [128, NT])
        TMP = sb("TMP", [128, 2, NT])
        OST = sb("OST", [128, 2, 368])
        GE = sb("GE", [128, 2, 512]); GE2 = sb("GE2", [128, 2, 512])
        IDF = sb("IDF", [128, 128]); IDB = sb("IDB", [128, 128], BF16); ONES = sb("ONES", [128, 128], BF16)
        GN = sb("GN", [128, 4, 16]); BGt = sb("BGt", [128, 32]); GLBt = sb("GLBt", [128, 8]); PSC = sb("PSC", [128, 8])
        WSR = sb("WSR", [128, 8])
        TBL = sb("TBL", [128, 4, 64]); SPO = sb("SPO", [128, 64]); EPS = sb("EPS", [128, 2])
        PS = [es.enter_context(nc.psum_tensor(f"ps{i}", [128, 512], F32)) for i in range(8)]
        PSB = [PS[i][:, :].bitcast(BF16) for i in range(8)]

        bconst = Buf("const")
        bX = [Buf(f"X{i}") for i in range(16)]; bXN = [Buf(f"XN{i}") for i in range(16)]
        bH = [Buf(f"H{i}") for i in range(11)]
        bSL = [Buf(f"SL{i}") for i in range(NSLAB)]
        bUSS = [Buf(f"USS{i}") for i in range(8)]; bST = [Buf(f"ST{i}") for i in range(8)]
        bAO = [Buf(f"AO{i}") for i in range(8)]; bMG = [Buf(f"MG{i}") for i in range(16)]
        bZC = Buf("ZC"); bZS = Buf("ZS"); bU = Buf("U"); bWG = Buf("WG"); bGB = Buf("GB"); bHB = Buf("HB")
        bW = [Buf("W0"), Buf("W1")]; bT1 = Buf("T1"); bT2 = Buf("T2"); bU1 = Buf("U1"); bU2 = Buf("U2")
        bSH0 = Buf("SH0"); bHPS = Buf("HPS"); bHPW = Buf("HPW"); bT6a = Buf("T6a"); bT6b = Buf("T6b")
        bUP = Buf("UP"); bWA = Buf("WA"); bWB = Buf("WB"); bD = Buf("D")
        bSQ = [Buf("SQ0"), Buf("SQ1")]; bRS = Buf("RS"); bTMP = [Buf("TMP0"), Buf("TMP1")]
        bOST = [Buf("OST0"), Buf("OST1")]
        bG0 = Buf("G0"); bG1 = Buf("G1"); bG2 = Buf("G2"); bG3 = Buf("G3")
        bPS = [Buf(f"ps{i}") for i in range(8)]
        st = {"ps": 0, "slab": 0, "ge": 0}
        cw = {"a": 0, "b": NT}

        def nps():
            b = st["ps"] % 8; st["ps"] += 1
            return b

        def dve(fn, reads, writes):
            S.add('dve', fn, reads=reads, writes=writes)

        def act(fn, reads, writes):
            S.add('act', fn, reads=reads, writes=writes)

        def pe(fn, reads, writes):
            S.add('pe', fn, reads=reads, writes=writes)

        def ldc(dst, src):
            bconst.lw = S.add('sp', lambda e: e.dma_start(out=dst, in_=src), dkey="const")
        ldc(IDF[:], ident_in); ldc(GN[:], gains); ldc(BGt[:], bgate); ldc(GLBt[:], glub); ldc(PSC[:], pscale)
        ldc(TBL[:], tb.rearrange("i p g -> p i g"))
        S.finalize_key("const")
        C = [bconst]
        bIDB = Buf("idb")
        S.add('pool', lambda e: e.dma_start(out=IDB[:], in_=ident_in), writes=[bIDB], dkey="constb")
        C = [bconst, bIDB]
        bONES = Buf("ones2")
        dve(lambda e: e.memset(ONES[:], 1.0), [], [bONES])
        bEPS = Buf("eps")
        dve(lambda e: e.memset(EPS[:], 1e-6), [], [bEPS])
        dve(lambda e: e.memset(ZS[:], 0.0), [], [bZS])
        dve(lambda e: e.memset(GB[:], 0.0), [], [bGB])
        dve(lambda e: e.memset(Wst[0][:], 0.0), [], [bW[0]])
        dve(lambda e: e.memset(Dm[:], 0.0), [], [bD])
        A8r = TBL[:, 0]; A8i = TBL[:, 1]; Am4r = TBL[:, 2]; Am4i = TBL[:, 3]

        wcache = {}

        class SV:
            def __init__(self, ap, tiled):
                self.ap = ap; self.tiled = tiled

            def w(self, k, m):
                return self.ap[:, m, k, :] if self.tiled else self.ap[:, k, m * 128:(m + 1) * 128]

        def load_slab(src_ap, kt, ncols, ckey=None):
            si = st["slab"] % NSLAB; st["slab"] += 1
            n = kt * ncols
            nm = ncols // 128
            tiled = isinstance(src_ap, tuple)
            if tiled:
                view = SV(SL[:, si, 0:n].rearrange("p (m k c) -> p m k c", m=nm, k=kt), True)
            else:
                view = SV(SL[:, si, 0:n].rearrange("p (k m) -> p k m", k=kt), False)
            if ckey is not None and ckey in wcache:
                ci, bc = wcache[ckey]
                S.add('sp', lambda e: e.dma_start(out=SL[:, si, 0:n], in_=WC[ci][:, 0:n]),
                      reads=[bc], writes=[bSL[si]], dkey=f"slabh{si}")
                return view, bSL[si]
            if tiled:
                _, wt, mt0, k0 = src_ap
                S.add('pool', lambda e: e.dma_start(
                    out=SL[:, si, 0:n].rearrange("p (m r) -> p m r", m=nm),
                    in_=wt[mt0:mt0 + nm, :, k0 * 128:(k0 + kt) * 128].rearrange("m p r -> p m r")),
                    writes=[bSL[si]], dkey=f"slab{si}")
            else:
                S.add('pool', lambda e: e.dma_start(out=view.ap, in_=src_ap.rearrange("(k p) m -> p k m", p=128)),
                      writes=[bSL[si]], dkey=f"slab{si}")
            if ckey is not None:
                ci = len(wcache)
                bc = Buf(f"wc{ci}")
                wcache[ckey] = (ci, bc)
                S.add('sp', lambda e: e.dma_start(out=WC[ci][:, 0:n], in_=SL[:, si, 0:n]),
                      reads=[bSL[si]], writes=[bc], dkey=f"cw{ci % 8}")
            return view, bSL[si]

        def load_m(mi):
            halves = []
            for hf in range(2):
                si = st["slab"] % NSLAB; st["slab"] += 1
                S.add('sp', lambda e, mi=mi, si=si, hf=hf: e.dma_start(out=SL[:, si, :], in_=ms[mi][:, hf * 4096:(hf + 1) * 4096]),
                      writes=[bSL[si]], dkey=f"slabh{si}")
                halves.append((SL[:, si, :].rearrange("p (g m) -> p g m", g=32), bSL[si]))
            return halves

        def linear(src, bsrc, kt, w_ap, col0, n_mt, epi, krow0=0, cname=None, tiled=False):
            a, b = cw["a"], cw["b"]
            mt = 0
            while mt < n_mt:
                nm = min(2, n_mt - mt)
                view, bs = load_slab(("t", w_ap, col0 // 128 + mt, krow0 // 128) if tiled else
                                     w_ap[krow0:krow0 + kt * 128, col0 + mt * 128: col0 + (mt + nm) * 128], kt, nm * 128,
                                     ckey=(cname, krow0, col0 + mt * 128) if cname else None)
                for m in range(nm):
                    pb = nps()
                    for k in range(kt):
                        pe(lambda e, pb=pb, k=k, m=m, view=view: e.matmul(
                            PS[pb][:, a:b], lhsT=view.w(k, m), rhs=src[:, k, a:b],
                            start=(k == 0), stop=(k == kt - 1)),
                           [bs, bsrc[k]], [bPS[pb]])
                    epi(mt + m, pb)
                mt += nm

        def rmsnorm(gi):
            a, b = cw["a"], cw["b"]
            pb = nps()
            for k in range(16):
                q = k % 2
                act(lambda e, k=k, q=q: e.activation(out=SQ[:, q, a:b], in_=X[:, k, a:b], func=AF.Square),
                    [bX[k]], [bSQ[q]])
                pe(lambda e, k=k, q=q, pb=pb: e.matmul(PS[pb][:, a:b], lhsT=ONES[:], rhs=SQ[:, q, a:b],
                                                       start=(k == 0), stop=(k == 15)),
                   [bSQ[q], bONES], [bPS[pb]])
            act(lambda e, pb=pb: e.activation(out=RS[:, a:b], in_=PS[pb][:, a:b], func=AF.Sqrt, bias=EPS[:, 0:1], scale=1.0 / D),
                [bEPS], [bPS[pb], bRS])
            dve(lambda e: e.reciprocal(out=RS[:, a:b], in_=RS[:, a:b]), [], [bRS])
            return pb

        def norm_to_xn(gi):
            a, b = cw["a"], cw["b"]
            rmsnorm(gi)
            for k in range(16):
                dve(lambda e, k=k: e.scalar_tensor_tensor(out=XN[:, k, a:b], in0=X[:, k, a:b], scalar=GN[:, gi, k:k + 1],
                                                          in1=RS[:, a:b], op0=ALU.mult, op1=ALU.mult),
                    [bX[k], bRS] + C, [bXN[k]])

        def ffn(gi, wg, wu, wd, nm_):
            a, b = cw["a"], cw["b"]
            norm_to_xn(gi)
            for f0 in range(0, 44, 11):
                fl0 = 0
                while fl0 < 11:
                    nm = min(2, 11 - fl0)
                    f = f0 + fl0
                    vg, bg_ = load_slab(("t", wg, f, 0), 16, nm * 128, ckey=(nm_ + 'g', f))
                    vu, bu_ = load_slab(("t", wu, f, 0), 16, nm * 128, ckey=(nm_ + 'u', f))
                    for m in range(nm):
                        fl = fl0 + m
                        pg = nps(); pu = nps()
                        for k in range(16):
                            pe(lambda e, k=k, pg=pg, vg=vg, m=m: e.matmul(PS[pg][:, a:b], lhsT=vg.w(k, m),
                                                                          rhs=XN[:, k, a:b], start=(k == 0), stop=(k == 15)),
                               [bg_, bXN[k]], [bPS[pg]])
                        for k in range(16):
                            pe(lambda e, k=k, pu=pu, vu=vu, m=m: e.matmul(PS[pu][:, a:b], lhsT=vu.w(k, m),
                                                                          rhs=XN[:, k, a:b], start=(k == 0), stop=(k == 15)),
                               [bu_, bXN[k]], [bPS[pu]])
                        q = fl % 2
                        act(lambda e, pg=pg, q=q: e.activation(out=TMP[:, q, a:b], in_=PS[pg][:, a:b], func=AF.Silu),
                            [], [bPS[pg], bTMP[q]])
                        dve(lambda e, pu=pu, q=q, fl=fl: e.tensor_tensor(out=H[:, fl, a:b], in0=PS[pu][:, a:b],
                                                                         in1=TMP[:, q, a:b], op=ALU.mult),
                            [bTMP[q]], [bPS[pu], bH[fl]])
                    fl0 += nm

                def epi(mt, pb):
                    dve(lambda e: e.scalar_tensor_tensor(out=X[:, mt, a:b], in0=PS[pb][:, a:b], scalar=0.5,
                                                         in1=X[:, mt, a:b], op0=ALU.mult, op1=ALU.add),
                        [], [bPS[pb], bX[mt]])
                linear(H, bH, 11, wd, 0, 16, epi, krow0=f0 * 128, cname=nm_ + 'd', tiled=True)

        def s5_front(with_samples):
            dve(lambda e: e.memset(ZS[:, :, 0:4, :], 0.0), [], [bZS])
            for ft in range(8):
                pb = nps()
                for s in range(8):
                    pe(lambda e, ft=ft, s=s, pb=pb: e.transpose(
                        PSB[pb][0:NCH, s * 128:(s + 1) * 128],
                        USS[:, ft, OWN0 + s: OWN0 + s + 8 * (NCH - 1) + 1: 8], IDB[:]),
                       [bUSS[ft]] + C, [bPS[pb]])
                act(lambda e, pb=pb: e.activation(
                    out=ZC[0:NCH].rearrange("p g s c -> p s g c"),
                    in_=PSB[pb][0:NCH, 0:1024].rearrange("p (s g c) -> p s g c", s=8, g=8), func=AF.Copy),
                    [], [bPS[pb], bZC])
                if with_samples:
                    pb = nps()
                    for sq in range(4):
                        pe(lambda e, ft=ft, sq=sq, pb=pb: e.transpose(
                            PSB[pb][0:NSL, sq * 128:(sq + 1) * 128],
                            USS[:, ft, SMP0 + sq: SMP0 + sq + 4 * (NSL - 1) + 1: 4], IDB[:]),
                           [bUSS[ft]] + C, [bPS[pb]])
                    act(lambda e, pb=pb: e.activation(
                        out=ZS[0:NSL, :, 4:8, :].rearrange("p g s c -> p s g c"),
                        in_=PSB[pb][0:NSL, 0:512].rearrange("p (s g c) -> p s g c", s=4, g=8), func=AF.Copy),
                        [], [bPS[pb], bZS])
                pb = nps()
                for gl in range(8):
                    pe(lambda e, gl=gl, pb=pb: e.transpose(PSB[pb][:, gl * 52: gl * 52 + NCH],
                                                           ZC[0:NCH, gl].rearrange("p s c -> p (s c)"), IDB[0:NCH, 0:NCH]),
                       [bZC] + C, [bPS[pb]])
                    pe(lambda e, gl=gl, pb=pb: e.transpose(PSB[pb][:, gl * 52 + 44: gl * 52 + 44 + NSL],
                                                           ZS[0:NSL, gl].rearrange("p s c -> p (s c)"), IDB[0:NSL, 0:NSL]),
                       [bZS] + C, [bPS[pb]])
                act(lambda e, ft=ft, pb=pb: e.activation(
                    out=U[:, ft * 8:(ft + 1) * 8, 0:NCH],
                    in_=PSB[pb][:, 0:8 * 52].rearrange("p (g j) -> p g j", g=8)[:, :, 0:NCH], func=AF.Copy),
                    [], [bPS[pb], bU])
                act(lambda e, ft=ft, pb=pb: e.activation(
                    out=U[:, ft * 8:(ft + 1) * 8, NCH:NCOL],
                    in_=PSB[pb][:, 0:8 * 52].rearrange("p (g j) -> p g j", g=8)[:, :, 44:44 + NSL], func=AF.Copy),
                    [], [bPS[pb], bU])
            for (mi, dst_is_gb) in ((1, True), (2, False)):
                halves = load_m(mi)
                for g0 in range(0, 64, 8):
                    mv, bmv = halves[g0 // 32]
                    pb = nps()
                    for gl in range(8):
                        g = g0 + gl
                        pe(lambda e, g=g, gl=gl, pb=pb, mv=mv: e.matmul(PS[pb][:, gl * NCOL:(gl + 1) * NCOL],
                                                                       lhsT=mv[:, g % 32, :], rhs=U[:, g, :],
                                                                       start=True, stop=True),
                           [bmv, bU], [bPS[pb]])
                    if dst_is_gb:
                        act(lambda e, g0=g0, pb=pb: e.activation(
                            out=GB[:, g0:g0 + 8, 1:NCOL + 1],
                            in_=PS[pb][:, 0:8 * NCOL].rearrange("p (g j) -> p g j", g=8), func=AF.Copy),
                            [], [bPS[pb], bGB])
                    else:
                        act(lambda e, g0=g0, pb=pb: e.activation(
                            out=WG[:, g0:g0 + 8, :],
                            in_=PS[pb][:, 0:8 * NCOL].rearrange("p (g j) -> p g j", g=8), func=AF.Copy),
                            [], [bPS[pb], bWG])

        wcur = {"i": 0}

        def recurrence():
            for j in range(NCH):
                wc = wcur["i"]; wn = 1 - wc
                S.add('pool', lambda e, j=j: e.tensor_tensor(out=T1[:], in0=GB[:, :, j], in1=A8r, op=ALU.mult), [bGB] + C, [bT1])
                S.add('pool', lambda e, wc=wc: e.tensor_tensor(out=T2[:], in0=Wst[wc][:], in1=A8i, op=ALU.mult), [bW[wc]] + C, [bT2])
                S.add('pool', lambda e, wc=wc: e.tensor_tensor(out=U1[:], in0=Wst[wc][:], in1=A8r, op=ALU.mult), [bW[wc]] + C, [bU1])
                S.add('pool', lambda e, j=j: e.tensor_tensor(out=U2[:], in0=GB[:, :, j], in1=A8i, op=ALU.mult), [bGB] + C, [bU2])
                S.add('pool', lambda e: e.tensor_tensor(out=T1[:], in0=T1[:], in1=T2[:], op=ALU.add), [bT2], [bT1])
                S.add('pool', lambda e: e.tensor_tensor(out=U1[:], in0=U1[:], in1=U2[:], op=ALU.subtract), [bU2], [bU1])
                S.add('pool', lambda e, j=j: e.tensor_tensor(out=GB[:, :, j + 1], in0=GB[:, :, j + 1], in1=T1[:], op=ALU.add),
                    [bT1], [bGB])
                S.add('pool', lambda e, j=j, wn=wn: e.tensor_tensor(out=Wst[wn][:], in0=WG[:, :, j], in1=U1[:], op=ALU.add),
                    [bU1, bWG], [bW[wn]])
                wcur["i"] = wn

        def carry_state():
            S.add('pool', lambda e: e.tensor_copy(out=GB[:, :, 0], in_=GB[:, :, NCH]), [], [bGB])

        def load_x(ti):
            S.add('sp', lambda e: e.dma_start(out=X[:], in_=xin[ti].rearrange("(k p) n -> p k n", p=128)),
                  writes=bX, dkey="xin")

        def uss_from_win():
            a, b = cw["a"], cw["b"]
            def epi(mt, pb):
                act(lambda e: e.activation(out=USS[:, mt, a:b], in_=PS[pb][:, a:b], func=AF.Copy),
                    [], [bPS[pb], bUSS[mt]])
            linear(XN, bXN, 16, w_in, 1024, 8, epi, cname='win', tiled=True)

        import os as _os
        NOCC = bool(_os.environ.get("KSIM_NOCC"))
        bXS = [Buf(f"XS{i}") for i in range(3)]; bUS = [Buf(f"US{i}") for i in range(3)]
        bGS = [Buf(f"GS{i}") for i in range(3)]; bWS = [Buf(f"WS{i}") for i in range(3)]
        bCCI = Buf("cci"); bCCO = Buf("cco")

        def front(ti):
            load_x(ti)
            ffn(0, w_g1, w_u1, w_d1, 'f1')
            S.add('sp', lambda e: e.dma_start(out=XS[ti], in_=X[:].rearrange("p k n -> p (k n)")),
                  reads=bX, writes=[bXS[ti]], dkey="sx")
            norm_to_xn(1)
            uss_from_win()
            s5_front(True)
            S.add('sp', lambda e: e.dma_start(out=US[ti], in_=U[:].rearrange("p g j -> p (g j)")),
                  reads=[bU], writes=[bUS[ti]], dkey="su")
            S.add('sp', lambda e: e.dma_start(out=GS[ti].rearrange("p (g j) -> p g j", g=64), in_=GB[:, :, 1:NCOL + 1]),
                  reads=[bGB], writes=[bGS[ti]], dkey="sg")
            S.add('sp', lambda e: e.dma_start(out=WS_[ti], in_=WG[:].rearrange("p g j -> p (g j)")),
                  reads=[bWG], writes=[bWS[ti]], dkey="sw")
            recurrence()
            carry_state()

        def exchange():
            wc = wcur["i"]
            CBv = GE[:, 0, :].rearrange("p (s f) -> p s f", s=4)
            RBv = GE[:, 1, :].rearrange("p (s f) -> p s f", s=4)
            HI = T6a[:].rearrange("p g i -> p (g i)")[:, 0:128]
            for s in range(4):
                dve(lambda e, s=s: e.tensor_scalar(out=CBv[:, s, 0:64], in0=GB[:, :, 0], scalar1=WSR[:, s:s + 1], scalar2=None,
                                                  op0=ALU.mult), [bGB] + C, [bG0])
                dve(lambda e, s=s: e.tensor_scalar(out=CBv[:, s, 64:128], in0=Wst[wc][:], scalar1=WSR[:, s:s + 1], scalar2=None,
                                                  op0=ALU.mult), [bW[wc]] + C, [bG0])
            S.add('sp', lambda e: e.dma_start(out=cc_in[:, :], in_=GE[:, 0, :]), reads=[bG0], writes=[bCCI], dkey="cci")
            if NOCC:
                S.add('pool', lambda e: e.dma_start(out=cc_out[:, :], in_=cc_in[:, :]), reads=[bCCI], writes=[bCCO], dkey="cc")
            else:
                S.add('pool', lambda e: e.collective_compute("AllReduce", ALU.add, replica_groups=[list(range(8))],
                                                             ins=[cc_in.ap().opt()], outs=[cc_out.ap().opt()]),
                      reads=[bCCI], writes=[bCCO], dkey="cc", inc=1)
            S.add('sp', lambda e: e.dma_start(out=GE[:, 1, :], in_=cc_out[:, :]), reads=[bCCO], writes=[bG1], dkey="cco")
            dve(lambda e: e.tensor_scalar(out=HI, in0=RBv[:, 0, :], scalar1=WSR[:, 4:5], scalar2=None, op0=ALU.mult),
                [bG1] + C, [bT6a])
            for s in range(1, 4):
                dve(lambda e, s=s: e.scalar_tensor_tensor(out=HI, in0=RBv[:, s, :], scalar=WSR[:, 4 + s:5 + s], in1=HI,
                                                          op0=ALU.mult, op1=ALU.add), [bG1] + C, [bT6a])
            dve(lambda e: e.tensor_copy(out=GB[:, :, 0], in_=HI[:, 0:64]), [bT6a], [bGB])
            dve(lambda e: e.tensor_copy(out=Wst[wc][:], in_=HI[:, 64:128]), [bT6a], [bW[wc]])

        def prefix(ti):
            cw["a"], cw["b"] = OWN0, SMP0
            load_x(ti)
            ffn(0, w_g1, w_u1, w_d1, 'f1')
            norm_to_xn(1)
            uss_from_win()
            s5_front(False)
            recurrence()
            carry_state()
            cw["a"], cw["b"] = 0, NT

        for ti in range(3):
            prefix(ti)

        def back(ti):
            load_x(3 + ti)
            S.add('sp', lambda e: e.dma_start(out=SH0[:], in_=sh0_in[ti]), writes=[bSH0], dkey="h0")
            S.add('sp', lambda e: e.dma_start(out=WH0[:], in_=wh0_in[ti]), writes=[bSH0], dkey="h0")
            ffn(0, w_g1, w_u1, w_d1, 'f1')
            norm_to_xn(1)
            uss_from_win()
            s5_front(True)
            recurrence()
            for pg in range(4):
                w = (2, 4, 8, 16)[pg]

                def epi_up(mt, pb, pg=pg):
                    q = mt - 2 * pg
                    act(lambda e: e.activation(out=GE[:, q, 0:NT], in_=PS[pb][:, 0:NT], func=AF.Copy),
                        [], [bPS[pb], (bG0, bG1)[q]])
                    dve(lambda e: e.tensor_copy(out=UP[:, q, 0:SMP0], in_=GE[:, q, 0:SMP0]), [(bG0, bG1)[q]], [bUP])
                    dve(lambda e: e.tensor_copy(
                        out=UP[:, q, SMP0:PW].rearrange("p (i h) -> p i h", h=19)[:, :, 15:19],
                        in_=GE[:, q, SMP0:NT].rearrange("p (i t) -> p i t", t=4)), [(bG0, bG1)[q]], [bUP])
                def lin2():
                    view, bs = load_slab(("t", w_in, 2 * pg, 0), 16, 256, ckey=('winp', pg))
                    for m in range(2):
                        pb = nps()
                        for k in range(16):
                            pe(lambda e, pb=pb, k=k, m=m: e.matmul(PS[pb][:, 0:NT], lhsT=view.w(k, m),
                                                                   rhs=XN[:, k, :], start=(k == 0), stop=(k == 15)),
                               [bs, bXN[k]], [bPS[pb]])
                        epi_up(2 * pg + m, pb)
                lin2()
                for q in range(2):
                    S.add('sp', lambda e, ti=ti, pg=pg, q=q: e.dma_start(
                        out=UP[:, q, SMP0:PW].rearrange("p (i h) -> p i h", h=19)[:, :, 0:15],
                        in_=hist_in[ti, pg * 256 + q * 128: pg * 256 + (q + 1) * 128]),
                        writes=[bUP], dkey="hist")
                if pg == 0:
                    pass
                dve(lambda e: e.tensor_tensor(out=WA[:, :, 1:PW], in0=UP[:, :, 1:PW], in1=UP[:, :, 0:PW - 1], op=ALU.add),
                    [bUP], [bWA])
                cur, bcur, oth, both = WA, bWA, WB, bWB
                k = 2
                while k < w:
                    dve(lambda e, cur=cur, oth=oth, k=k: e.tensor_tensor(out=oth[:, :, 2 * k - 1:PW], in0=cur[:, :, 2 * k - 1:PW],
                                                                          in1=cur[:, :, k - 1:PW - k], op=ALU.add),
                        [bcur], [both])
                    cur, bcur, oth, both = oth, both, cur, bcur
                    k *= 2
                dve(lambda e, cur=cur, w=w: e.scalar_tensor_tensor(out=Dm[:, :, OWN0:SMP0], in0=cur[:, :, OWN0:SMP0],
                                                                   scalar=1.0 / w, in1=UP[:, :, OWN0:SMP0],
                                                                   op0=ALU.mult, op1=ALU.subtract), [bcur, bUP], [bD])
                for q in range(2):
                    dve(lambda e, cur=cur, w=w, q=q: e.scalar_tensor_tensor(
                        out=Dm[:, q, SMP0:NT].rearrange("p (i t) -> p i t", t=4),
                        in0=cur[:, q, SMP0:PW].rearrange("p (i h) -> p i h", h=19)[:, :, 15:19], scalar=1.0 / w,
                        in1=UP[:, q, SMP0:PW].rearrange("p (i h) -> p i h", h=19)[:, :, 15:19],
                        op0=ALU.mult, op1=ALU.subtract), [bcur, bUP], [bD])
                for q in range(2):
                    S.add('sp', lambda e, ti=ti, pg=pg, q=q: e.dma_start(
                        out=pool_s[ti, pg * 256 + q * 128: pg * 256 + (q + 1) * 128],
                        in_=UP[:, q, SMP0:PW].rearrange("p (i h) -> p i h", h=19)[:, :, 4:19]),
                        reads=[bUP], dkey="o_pools")
                if ti == 2:
                    S.add('sp', lambda e, pg=pg: e.dma_start(
                        out=pool_p[pg * 256:(pg + 1) * 256].rearrange("(q p) h -> p q h", p=128),
                        in_=UP[:, :, SMP0 - 15:SMP0]), reads=[bUP], dkey="o_poolp")
                vw, bw = load_slab(w_pool[pg], 2, 256, ckey=('pw', pg))
                for m in range(2):
                    pb = nps()
                    for k2 in range(2):
                        pe(lambda e, pb=pb, k2=k2, m=m, vw=vw: e.matmul(PS[pb][:, 0:NT], lhsT=vw.w(k2, m),
                                                                       rhs=Dm[:, k2, :], start=(k2 == 0), stop=(k2 == 1)),
                           [bw, bD], [bPS[pb]])
                    mt = 2 * pg + m
                    dve(lambda e, pb=pb, mt=mt: e.tensor_scalar(out=AO[:, mt, :], in0=PS[pb][:, 0:NT],
                                                                scalar1=PSC[:, mt:mt + 1], scalar2=None, op0=ALU.mult),
                        C, [bPS[pb], bAO[mt]])
            for dt_ in range(16):
                va, ba = load_slab(("t", w_in, 16 + dt_, 0), 16, 128, ckey=('ga', dt_))
                vwa, bwa = load_slab(("t", w_ba, dt_, 0), 8, 128, ckey=('ba', dt_))
                pga = nps(); pa = nps()
                for k in range(16):
                    pe(lambda e, k=k, pga=pga, va=va: e.matmul(PS[pga][:, 0:NT], lhsT=va.w(k, 0), rhs=XN[:, k, :],
                                                               start=(k == 0), stop=(k == 15)), [ba, bXN[k]], [bPS[pga]])
                for k in range(8):
                    pe(lambda e, k=k, pa=pa, vwa=vwa: e.matmul(PS[pa][:, 0:NT], lhsT=vwa.w(k, 0), rhs=AO[:, k, :],
                                                               start=(k == 0), stop=(k == 7)), [bwa, bAO[k]], [bPS[pa]])
                act(lambda e, pga=pga, dt_=dt_: e.activation(out=TMP[:, 0, :], in_=PS[pga][:, 0:NT], func=AF.Sigmoid,
                                                             bias=BGt[:, dt_:dt_ + 1], scale=1.0), C, [bPS[pga], bTMP[0]])
                dve(lambda e, pa=pa, dt_=dt_: e.tensor_tensor(out=MG[:, dt_, :], in0=PS[pa][:, 0:NT], in1=TMP[:, 0, :], op=ALU.mult),
                    [bTMP[0]], [bPS[pa], bMG[dt_]])
                if dt_ < 11:
                    vb, bb_ = load_slab(("t", w_in, 32 + dt_, 0), 16, 128, ckey=('gb', dt_))
                    pgb = nps()
                    for k in range(16):
                        pe(lambda e, k=k, pgb=pgb, vb=vb: e.matmul(PS[pgb][:, 0:NT], lhsT=vb.w(k, 0), rhs=XN[:, k, :],
                                                                   start=(k == 0), stop=(k == 15)), [bb_, bXN[k]], [bPS[pgb]])
                    act(lambda e, pgb=pgb, dt_=dt_: e.activation(out=H[:, dt_, :], in_=PS[pgb][:, 0:NT], func=AF.Sigmoid,
                                                                 bias=BGt[:, 16 + dt_:17 + dt_], scale=1.0), C, [bPS[pgb], bH[dt_]])
            dve(lambda e: e.tensor_scalar(out=WH0[0:64], in0=WH0[0:64], scalar1=-1.0, scalar2=None, op0=ALU.mult),
                [], [bSH0])

            def cm6(out_ap, ar, ai, s_ap, w_ap, sign, reads, wbuf):
                dve(lambda e: e.tensor_tensor(out=T6a[:], in0=s_ap, in1=bc_last(ar, NSL), op=ALU.mult), reads + C, [bT6a])
                dve(lambda e: e.tensor_tensor(out=T6b[:], in0=w_ap, in1=bc_last(ai, NSL), op=ALU.mult), reads + C, [bT6b])
                dve(lambda e: e.tensor_tensor(out=out_ap, in0=T6a[:], in1=T6b[:],
                                              op=(ALU.add if sign > 0 else ALU.subtract)), [bT6a, bT6b], [wbuf])
            cm6(HPS[:], Am4r, Am4i, SH0[:], WH0[:], +1, [bSH0], bHPS)
            cm6(HPW[:], Am4r, Am4i, WH0[:], SH0[:], -1, [bSH0], bHPW)
            cm6(T6a[:], A8r, A8i, HPS[:], HPW[:], +1, [bHPS, bHPW], bT6a)
            dve(lambda e: e.tensor_tensor(out=GB[:, :, NCH + 1:NCOL + 1], in0=GB[:, :, NCH + 1:NCOL + 1], in1=T6a[:],
                                          op=ALU.add), [bT6a], [bGB])
            S.add('sp', lambda e, ti=ti: e.dma_start(out=ssm_s[ti], in_=GB[:, :, NCH + 1:NCOL + 1]),
                  reads=[bGB], dkey="o_ssms")
            if ti == 2:
                bSPO = Buf("SPO")
                dve(lambda e: e.tensor_copy(out=SPO[:], in_=GB[:, :, NCH]), [bGB], [bSPO])
                S.add('sp', lambda e: e.dma_start(out=ssm_p, in_=SPO[:]), reads=[bSPO], dkey="o_ssmp")
            act(lambda e: e.activation(out=HB[:, :, 0:NCH], in_=GB[:, :, 0:NCH], func=AF.Copy), [bGB], [bHB])
            act(lambda e: e.activation(out=HB[:, :, NCH:NCOL], in_=HPS[:], func=AF.Copy), [bHPS], [bHB])
            m1h = load_m(0)
            m3h = load_m(3)
            zb = ZC[:].rearrange("p g s c -> p (g s c)").rearrange("p (t f) -> p t f", t=8)
            nr = NCOL
            for ft in range(8):
                for half in range(2):
                    pb = nps()
                    for gq in range(4):
                        g = ft * 8 + half * 4 + gq
                        m1v, bm1 = m1h[g // 32]
                        m3v, bm3 = m3h[g // 32]
                        pe(lambda e, g=g, gq=gq, pb=pb, m1v=m1v: e.matmul(
                            PS[pb][0:NCOL, gq * 128:(gq + 1) * 128], lhsT=U[:, g, 0:NCOL], rhs=m1v[:, g % 32, :],
                            start=True, stop=False), [bU, bm1], [bPS[pb]])
                        pe(lambda e, g=g, gq=gq, pb=pb, m3v=m3v: e.matmul(
                            PS[pb][0:NCOL, gq * 128:(gq + 1) * 128], lhsT=HB[:, g, 0:NCOL], rhs=m3v[:, g % 32, :],
                            start=False, stop=True), [bHB, bm3], [bPS[pb]])
                    gi_ = st["ge"] % 2; st["ge"] += 1
                    GX = (GE, GE2)[gi_]; bga, bgb = ((bG0, bG1), (bG2, bG3))[gi_]
                    act(lambda e, pb=pb, GX=GX: e.activation(out=GX[0:NCOL, 0, :], in_=PS[pb][0:NCOL, 0:512], func=AF.Square),
                        [], [bPS[pb], bga])
                    dve(lambda e, GX=GX: e.tensor_scalar(out=GX[0:NCOL, 0, :], in0=GX[0:NCOL, 0, :], scalar1=0.044715, scalar2=1.0,
                                                         op0=ALU.mult, op1=ALU.add), [], [bga])
                    dve(lambda e, pb=pb, GX=GX: e.tensor_tensor(out=GX[0:NCOL, 0, :], in0=PS[pb][0:NCOL, 0:512], in1=GX[0:NCOL, 0, :],
                                                                op=ALU.mult), [], [bPS[pb], bga])
                    act(lambda e, GX=GX: e.activation(out=GX[0:NCOL, 1, :], in_=GX[0:NCOL, 0, :], func=AF.Sigmoid, scale=1.5957691),
                        [bga], [bgb])
                    dve(lambda e, pb=pb, half=half, GX=GX: e.tensor_tensor(
                        out=zb[0:NCOL, :, half * 64:(half + 1) * 64].rearrange("p t (g c) -> p g t c", g=4),
                        in0=PS[pb][0:NCOL, 0:512].rearrange("p (g t c) -> p g t c", g=4, t=8),
                        in1=GX[0:NCOL, 1, :].rearrange("p (g t c) -> p g t c", g=4, t=8), op=ALU.mult),
                        [bgb], [bPS[pb], bZC])
                pb = nps()
                for t in range(8):
                    pe(lambda e, t=t, pb=pb: e.transpose(
                        PSB[pb][:, t * 64: t * 64 + NCOL], zb[0:NCOL, t, :], IDB[0:NCOL, 0:NCOL]),
                       [bZC] + C, [bPS[pb]])
                act(lambda e, ft=ft, pb=pb: e.activation(
                    out=ST[:, ft, OWN0:OWN0 + OWN].rearrange("p (j t) -> p t j", t=8),
                    in_=PSB[pb][:, 0:512].rearrange("p (t j) -> p t j", t=8)[:, :, 0:NCH], func=AF.Copy),
                    [], [bPS[pb], bST[ft]])
                act(lambda e, ft=ft, pb=pb: e.activation(
                    out=ST[:, ft, SMP0:SMP0 + 4 * NSL].rearrange("p (i t) -> p t i", t=4),
                    in_=PSB[pb][:, 256:512].rearrange("p (t j) -> p t j", t=4)[:, :, NCH:NCOL], func=AF.Copy),
                    [], [bPS[pb], bST[ft]])
            for ft in range(8):
                dve(lambda e, ft=ft: e.memset(ST[:, ft, 0:OWN0], 0.0), [], [bST[ft]])
            def epi_glu(mt, pb):
                act(lambda e: e.activation(out=AO[:, mt, :], in_=PS[pb][:, 0:NT], func=AF.Sigmoid,
                                           bias=GLBt[:, mt:mt + 1], scale=1.0), C, [bPS[pb], bAO[mt]])
            linear(ST, bST, 8, w_glu, 0, 8, epi_glu, cname='glu')
            for mt in range(8):
                dve(lambda e, mt=mt: e.tensor_tensor(out=ST[:, mt, :], in0=ST[:, mt, :], in1=AO[:, mt, :], op=ALU.mult),
                    [bAO[mt]], [bST[mt]])
            for dt_ in range(16):
                vwb, bwb = load_slab(("t", w_bb, dt_, 0), 8, 128, ckey=('bb', dt_))
                pbb = nps()
                if dt_ >= 11:
                    vb, bb_ = load_slab(("t", w_in, 32 + dt_, 0), 16, 128, ckey=('gb', dt_))
                    pgb = nps()
                    for k in range(16):
                        pe(lambda e, k=k, pgb=pgb, vb=vb: e.matmul(PS[pgb][:, 0:NT], lhsT=vb.w(k, 0), rhs=XN[:, k, :],
                                                                   start=(k == 0), stop=(k == 15)), [bb_, bXN[k]], [bPS[pgb]])
                for k in range(8):
                    pe(lambda e, k=k, pbb=pbb, vwb=vwb: e.matmul(PS[pbb][:, 0:NT], lhsT=vwb.w(k, 0), rhs=ST[:, k, :],
                                                                 start=(k == 0), stop=(k == 7)), [bwb, bST[k]], [bPS[pbb]])
                if dt_ >= 11:
                    act(lambda e, pgb=pgb, dt_=dt_: e.activation(out=TMP[:, 1, :], in_=PS[pgb][:, 0:NT], func=AF.Sigmoid,
                                                                 bias=BGt[:, 16 + dt_:17 + dt_], scale=1.0), C, [bPS[pgb], bTMP[1]])
                    dve(lambda e, pbb=pbb: e.tensor_tensor(out=TMP[:, 1, :], in0=PS[pbb][:, 0:NT], in1=TMP[:, 1, :], op=ALU.mult),
                        [], [bPS[pbb], bTMP[1]])
                else:
                    dve(lambda e, pbb=pbb, dt_=dt_: e.tensor_tensor(out=TMP[:, 1, :], in0=PS[pbb][:, 0:NT], in1=H[:, dt_, :], op=ALU.mult),
                        [bH[dt_]], [bPS[pbb], bTMP[1]])
                dve(lambda e, dt_=dt_: e.tensor_tensor(out=MG[:, dt_, :], in0=MG[:, dt_, :], in1=TMP[:, 1, :], op=ALU.add),
                    [bTMP[1]], [bMG[dt_]])

            def epi_o(mt, pb):
                dve(lambda e: e.tensor_tensor(out=X[:, mt, :], in0=PS[pb][:, 0:NT], in1=X[:, mt, :], op=ALU.add),
                    [], [bPS[pb], bX[mt]])
            linear(MG, bMG, 16, w_o, 0, 16, epi_o, cname='wo')
            ffn(2, w_g2, w_u2, w_d2, 'f2')
            rmsnorm(3)
            for k in range(16):
                q = k % 2
                dve(lambda e, k=k, q=q: e.scalar_tensor_tensor(out=OST[:, q, :], in0=X[:, k, OWN0:NT], scalar=GN[:, 3, k:k + 1],
                                                               in1=RS[:, OWN0:NT], op0=ALU.mult, op1=ALU.mult),
                    [bX[k], bRS] + C, [bOST[q]])
                S.add('sp', lambda e, k=k, q=q, ti=ti: e.dma_start(out=yT[ti, k * 128:(k + 1) * 128, :], in_=OST[:, q, :]),
                      reads=[bOST[q]], dkey=f"oy{q}")
            carry_state()

        for ti in range(3):
            back(ti)
        S.emit(nc, "m", sem_es)
    return nc


_NC_CACHE = {}


def _tile_w(w):
    K, M = w.shape
    return np.ascontiguousarray(np.asarray(w, np.float32).reshape(K // 128, 128, M // 128, 128).transpose(2, 1, 0, 3)
                                ).reshape(M // 128, 128, K)


def _prep_shared(inp):
    f = np.float32
    sh = {}
    sh["w_g1"] = _tile_w(inp["ffn1_w_gate"][0]); sh["w_u1"] = _tile_w(inp["ffn1_w_up"][0])
    sh["w_d1"] = _tile_w(inp["ffn1_w_down"][0])
    sh["w_g2"] = _tile_w(inp["ffn2_w_gate"][0]); sh["w_u2"] = _tile_w(inp["ffn2_w_up"][0])
    sh["w_d2"] = _tile_w(inp["ffn2_w_down"][0])
    sh["w_in"] = _tile_w(inp["w_in"][0])
    sh["w_pool"] = np.ascontiguousarray(inp["pool_w"][0], f)
    sh["w_glu"] = np.ascontiguousarray(inp["glu_w"][0], f)
    sh["w_ba"] = _tile_w(inp["w_branch_a"][0]); sh["w_bb"] = _tile_w(inp["w_branch_b"][0])
    sh["w_o"] = np.ascontiguousarray(inp["w_out"][0], f)
    g = np.stack([inp["norm_ffn1"][0], inp["norm_mix"][0], inp["norm_ffn2"][0], inp["final_norm"]], 0)
    sh["gains"] = np.ascontiguousarray(g.reshape(4, 16, 128).transpose(2, 0, 1), f)
    sh["bgate"] = np.ascontiguousarray(inp["b_gate"][0].reshape(32, 128).T, f)
    sh["glub"] = np.ascontiguousarray(inp["glu_b"][0].reshape(8, 128).T, f)
    sh["pscale"] = np.ascontiguousarray(inp["pool_scale"][0].reshape(8, 128).T, f)
    lrT = inp["ssm_lambda_re"][0].T; liT = inp["ssm_lambda_im"][0].T
    sh["lr2"] = np.ascontiguousarray(np.concatenate([lrT, lrT], 0), f)
    sh["li2"] = np.ascontiguousarray(np.concatenate([liT, liT], 0), f)
    sh["ldt"] = np.ascontiguousarray(np.broadcast_to(inp["ssm_log_dt"][0][None, :], (128, 64)), f)
    br = inp["ssm_b_re"][0].transpose(1, 0, 2); bi = inp["ssm_b_im"][0].transpose(1, 0, 2)
    sh["sb_in"] = np.ascontiguousarray(np.concatenate([br, bi], 0), f)
    sh["wb_in"] = np.ascontiguousarray(np.concatenate([bi, br], 0), f)
    cr = inp["ssm_c_re"][0].transpose(2, 0, 1); ci = inp["ssm_c_im"][0].transpose(2, 0, 1)
    sh["n1_in"] = np.ascontiguousarray(np.concatenate([cr, ci], 0), f)
    sh["n2_in"] = np.ascontiguousarray(np.concatenate([ci, cr], 0), f)
    dd = inp["ssm_d"][0].reshape(64, 16)
    sh["dd_in"] = np.ascontiguousarray(np.tile(dd.T, (8, 1)), f)
    s_idx = np.arange(128) // 16
    sh["mask_in"] = (s_idx[None, :] >= s_idx[:, None]).astype(f)
    sh["ident_in"] = np.eye(128, dtype=f)
    return sh


def kernel(**inp):
    inp = {k: np.asarray(v) for k, v in inp.items()}
    f = np.float32
    if "nc" not in _NC_CACHE:
        _NC_CACHE["nc"] = build_program()
    nc = _NC_CACHE["nc"]
    sh = _prep_shared(inp)
    xp = inp["x_prompt"].astype(f); xs = inp["x_sample"].astype(f)
    meta = inp["meta_tokens"].astype(f)
    st_pool = inp["state_pool"][0]; st_re = inp["state_ssm_re"][0]; st_im = inp["state_ssm_im"][0]
    in_maps = []
    for c in range(8):
        b, r = c // 2, c % 2
        seq = np.concatenate([meta, xp[b]], 0)
        own = seq[r * 1032:(r + 1) * 1032]
        pre = seq[0:1032] if r == 1 else np.zeros((1032, D), f)
        halo = seq[1032 - 15:1032] if r == 1 else np.zeros((15, D), f)
        ownh = np.concatenate([halo, own], 0)
        xin = np.zeros((6, NT, D), f)
        hist = np.zeros((3, NSL, 15, 1024), f)
        h0r = np.zeros((3, NSL, 64, 64), f); h0i = np.zeros((3, NSL, 64, 64), f)
        for t in range(3):
            xin[t, OWN0:OWN0 + OWN] = pre[t * OWN:(t + 1) * OWN]
            xin[3 + t, 0:SMP0] = ownh[t * OWN:t * OWN + SMP0]
            for i in range(NSL):
                sl = t * NSL + i
                if sl < 16:
                    sq = 16 * c + sl
                    xin[3 + t, SMP0 + 4 * i:SMP0 + 4 * i + 4] = xs[sq]
                    hist[t, i] = st_pool[sq]
                    h0r[t, i] = st_re[sq]; h0i[t, i] = st_im[sq]
        m = dict(sh)
        m["xin"] = np.ascontiguousarray(xin.transpose(0, 2, 1))
        m["hist_in"] = np.ascontiguousarray(hist.transpose(0, 3, 1, 2))
        hr = h0r.transpose(0, 3, 2, 1); hi = h0i.transpose(0, 3, 2, 1)
        m["sh0_in"] = np.ascontiguousarray(np.concatenate([hr, hi], 1))
        m["wh0_in"] = np.ascontiguousarray(np.concatenate([hi, hr], 1))
        in_maps.append(m)
    res = run_bass_kernel_spmd(nc, in_maps, core_ids=list(range(8)))
    y_prompt = np.zeros((4, 2048, D), f); y_sample = np.zeros((128, 4, D), f)
    pool_pp = np.zeros((1, 4, 15, 1024), f); pool_ss = np.zeros((1, 128, 15, 1024), f)
    re_p = np.zeros((1, 4, 64, 64), f); im_p = np.zeros((1, 4, 64, 64), f)
    re_s = np.zeros((1, 128, 64, 64), f); im_s = np.zeros((1, 128, 64, 64), f)
    for c in range(8):
        b, r = c // 2, c % 2
        o = res.results[c]
        yT = np.asarray(o["yT"])
        yo = np.concatenate([yT[t, :, 0:OWN].T for t in range(3)], 0)
        if r == 0:
            y_prompt[b, 0:1016] = yo[16:]
        else:
            y_prompt[b, 1016:2048] = yo
            pool_pp[0, b] = np.asarray(o["pool_p"]).T
            sp = np.asarray(o["ssm_p"])
            re_p[0, b] = sp[0:64].T; im_p[0, b] = sp[64:128].T
        ps_ = np.asarray(o["pool_s"]); ss_ = np.asarray(o["ssm_s"])
        for t in range(3):
            for i in range(NSL):
                sl = t * NSL + i
                if sl < 16:
                    sq = 16 * c + sl
                    y_sample[sq] = yT[t, :, OWN + 4 * i:OWN + 4 * i + 4].T
                    pool_ss[0, sq] = ps_[t, :, i, :].T
                    re_s[0, sq] = ss_[t, 0:64, :, i].T; im_s[0, sq] = ss_[t, 64:128, :, i].T
    return (y_prompt, y_sample, pool_pp, pool_ss, re_p, im_p, re_s, im_s)
```

```python
import math
from contextlib import ExitStack
import numpy as np
import concourse.bass as bass
import concourse.mybir as mybir
from concourse.bass_utils import run_bass_kernel_spmd

F32 = mybir.dt.float32
BF16 = mybir.dt.bfloat16
AF = mybir.ActivationFunctionType
ALU = mybir.AluOpType

D = 2048
DFF = 5632
NT = 383
OWN0 = 15
OWN = 344
SMP0 = 359
NSL = 6
NCH = 43
NCOL = NCH + NSL
TWO_PI = 2.0 * math.pi


class Buf:
    def __init__(self, name):
        self.name = name
        self.lw = None
        self.rd = {}


class Op:
    pass


class Sched:
    ENG = ['pe', 'act', 'dve', 'pool', 'sp']

    def __init__(self):
        self.q = {e: [] for e in self.ENG}
        self.dma_cnt = {}
        self.dma_ops = {}

    def add(self, eng, fn, reads=(), writes=(), dkey=None, inc=16):
        o = Op()
        o.eng = eng
        o.fn = fn
        o.dkey = dkey
        o.inc = inc
        o.sig = False
        o.deps = []
        ds = []
        for b in reads:
            if b.lw is not None:
                ds.append(b.lw)
        for b in writes:
            if b.lw is not None:
                ds.append(b.lw)
            ds.extend(b.rd.values())
        rk = eng if dkey is None else ('d', dkey)
        for b in reads:
            b.rd[rk] = o
        for b in writes:
            b.lw = o
            b.rd = {}
        seen = set()
        for d in ds:
            if d is o or id(d) in seen:
                continue
            seen.add(id(d))
            if d.dkey is None and d.eng == 'pe' and eng == 'pe' and dkey is None:
                continue
            o.deps.append(d)
            if d.dkey is None:
                d.sig = True
        if dkey is not None:
            self.dma_cnt[dkey] = self.dma_cnt.get(dkey, 0) + inc
            o.dval = self.dma_cnt[dkey]
            self.dma_ops.setdefault(dkey, []).append(o)
        self.q[eng].append(o)
        return o

    def finalize_key(self, key):
        for o in self.dma_ops.get(key, []):
            o.dval = self.dma_cnt[key]

    def emit(self, nc, tag, sem_es):
        with ExitStack() as es:
            sems = {}
            for e in ['pe', 'act', 'dve', 'pool', 'sp']:
                sems[('e', e)] = sem_es.enter_context(nc.semaphore(f"{tag}_e_{e}"))
            for k in self.dma_cnt:
                sems[('d', k)] = sem_es.enter_context(nc.semaphore(f"{tag}_d_{k}"))
            final = {}
            for e in self.ENG:
                c = 0
                for o in self.q[e]:
                    if o.dkey is None and o.sig:
                        c += 1
                        o.sval = c
                final[('e', e)] = c
            for k, v in self.dma_cnt.items():
                final[('d', k)] = v
            block = es.enter_context(nc.Block())

            def run(engname, eng):
                waited = {}
                for o in self.q[engname]:
                    need = {}
                    for d in o.deps:
                        if d.dkey is None:
                            key = ('e', d.eng)
                            val = d.sval
                        else:
                            key = ('d', d.dkey)
                            val = d.dval
                        if val > need.get(key, 0):
                            need[key] = val
                    for key, val in need.items():
                        if waited.get(key, 0) >= val:
                            continue
                        waited[key] = val
                        eng.wait_ge(sems[key], val)
                    ins = o.fn(eng)
                    if o.dkey is not None:
                        ins.then_inc(sems[('d', o.dkey)], o.inc)
                    elif o.sig:
                        ins.then_inc(sems[('e', o.eng)], 1)
                for key, val in final.items():
                    if val > 0 and waited.get(key, 0) < val:
                        eng.wait_ge(sems[key], val)

            @block.tensor
            def _(eng):
                run('pe', eng)

            @block.scalar
            def _(eng):
                run('act', eng)

            @block.vector
            def _(eng):
                run('dve', eng)

            @block.gpsimd
            def _(eng):
                run('pool', eng)

            @block.sync
            def _(eng):
                run('sp', eng)


def bc_last(ap, n):
    return ap.unsqueeze(2).broadcast_to([ap.shape[0], ap.shape[1], n])


def build_program(stage="full"):
    nc = bass.Bass("TRN2", target_bir_lowering=False)
    sem_es = ExitStack()

    def din(name, shape):
        return nc.dram_tensor(name, list(shape), F32, kind="ExternalInput").ap()

    def dout(name, shape):
        return nc.dram_tensor(name, list(shape), F32, kind="ExternalOutput").ap()

    xin = din("xin", [6, D, NT])
    hist_in = din("hist_in", [3, 1024, NSL, 15])
    sh0_in = din("sh0_in", [3, 128, 64, NSL])
    wh0_in = din("wh0_in", [3, 128, 64, NSL])
    w_g1 = din("w_g1", [44, 128, D]); w_u1 = din("w_u1", [44, 128, D]); w_d1 = din("w_d1", [16, 128, DFF])
    w_g2 = din("w_g2", [44, 128, D]); w_u2 = din("w_u2", [44, 128, D]); w_d2 = din("w_d2", [16, 128, DFF])
    w_in = din("w_in", [48, 128, D])
    w_pool = din("w_pool", [4, 256, 256])
    w_glu = din("w_glu", [1024, 1024])
    w_ba = din("w_ba", [16, 128, 1024]); w_bb = din("w_bb", [16, 128, 1024]); w_o = din("w_o", [D, D])
    gains = din("gains", [128, 4, 16])
    bgate = din("bgate", [128, 32])
    glub = din("glub", [128, 8])
    pscale = din("pscale", [128, 8])
    lr2 = din("lr2", [128, 64]); li2 = din("li2", [128, 64]); ldt = din("ldt", [128, 64])
    sb_in = din("sb_in", [128, 64, 16]); wb_in = din("wb_in", [128, 64, 16])
    n1_in = din("n1_in", [128, 64, 16]); n2_in = din("n2_in", [128, 64, 16])
    dd_in = din("dd_in", [128, 64])
    mask_in = din("mask_in", [128, 128])
    ident_in = din("ident_in", [128, 128])

    yT = dout("yT", [3, D, 368])
    pool_p = dout("pool_p", [1024, 15])
    pool_s = dout("pool_s", [3, 1024, NSL, 15])
    ssm_p = dout("ssm_p", [128, 64])
    ssm_s = dout("ssm_s", [3, 128, 64, NSL])

    ms = nc.dram_tensor("ms_scr", [4, 128, 8192], BF16, kind="ExternalOutput").ap()
    tb = nc.dram_tensor("tb_scr", [4, 128, 64], F32, kind="ExternalOutput").ap()
    XS = nc.dram_tensor("xs_scr", [3, 128, 16 * NT], F32).ap()
    US = nc.dram_tensor("us_scr", [3, 128, 64 * NCOL], BF16).ap()
    GS = nc.dram_tensor("gs_scr", [3, 128, 64 * NCOL], F32).ap()
    WS_ = nc.dram_tensor("ws_scr", [3, 128, 64 * NCOL], BF16).ap()
    WC = nc.dram_tensor("wc_scr", [256, 128, 4096], BF16).ap()
    cc_in = nc.dram_tensor("cc_in", [128, 512], F32)
    cc_out = nc.dram_tensor("cc_out", [128, 512], F32)

    with ExitStack() as es:
        S = Sched()

        def sb(name, shape, dt=F32):
            return es.enter_context(nc.sbuf_tensor(name, list(shape), dt))

        LR = sb("s_lr", [128, 64]); LI = sb("s_li", [128, 64]); LDT = sb("s_ldt", [128, 64])
        SB0 = sb("s_sb", [128, 64, 16]); WB0 = sb("s_wb", [128, 64, 16])
        N1 = sb("s_n1", [128, 64, 16]); N2 = sb("s_n2", [128, 64, 16])
        SBb = sb("s_sbb", [128, 64, 16]); WBb = sb("s_wbb", [128, 64, 16])
        DDt = sb("s_dd", [128, 64]); MASK = sb("s_mask", [128, 128]); IDF = sb("s_id", [128, 128])
        DT = sb("s_dt", [128, 64]); DLR = sb("s_dlr", [128, 64]); DLI = sb("s_dli", [128, 64])
        TR = sb("s_tr", [128, 9, 64]); TI = sb("s_ti", [128, 9, 64])
        TRn = sb("s_trn", [128, 9, 64]); TIn = sb("s_tin", [128, 9, 64])
        MAG = sb("s_mag", [128, 9, 64]); MAGn = sb("s_magn", [128, 9, 64])
        SN = sb("s_sn", [128, 9, 64]); CS = sb("s_cs", [128, 9, 64])
        R1 = sb("s_r1", [128, 9, 64]); R2 = sb("s_r2", [128, 9, 64])
        NI = sb("s_ni", [128, 64], mybir.dt.int32)
        QA = sb("s_qa", [128, 64]); QB = sb("s_qb", [128, 64]); QC = sb("s_qc", [128, 64])
        QR = sb("s_qr", [128, 64]); QI = sb("s_qi", [128, 64])
        T16a = sb("s_t16a", [128, 64, 16]); T16b = sb("s_t16b", [128, 64, 16])
        BNp = sb("s_bnp", [128, 64, 8, 16]); BPs = sb("s_bps", [128, 64, 8, 16])
        WBPs = sb("s_wbps", [128, 64, 8, 16]); M3f = sb("s_m3f", [128, 64, 8, 16])
        MO = [sb(f"s_mo{i}", [128, 16, 128], BF16) for i in range(4)]
        TMPM = [sb(f"s_tmpm{i}", [128, 128]) for i in range(2)]
        PSs = [es.enter_context(nc.psum_tensor(f"s_ps{i}", [128, 512], F32)) for i in range(8)]

        b_const = Buf("const")
        bT = {n: Buf(n) for n in ["DT", "DLR", "DLI", "TR", "TI", "TRn", "TIn", "MAG", "MAGn", "SN", "CS",
                                  "R1", "R2", "QA", "QB", "QC", "QR", "QI", "T16a", "T16b", "SBb", "WBb",
                                  "BNp", "BPs", "WBPs", "M3f", "N1", "N2", "WB0"]}
        bMO = [Buf(f"MO{i}") for i in range(4)]
        bTMPM = [Buf(f"TMPM{i}") for i in range(2)]
        bPS = [Buf(f"sps{i}") for i in range(8)]

        def ld(dst, src):
            b_const.lw = S.add('sp', lambda e, dst=dst, src=src: e.dma_start(out=dst, in_=src), dkey="const")

        ld(LR[:], lr2); ld(LI[:], li2); ld(LDT[:], ldt)
        ld(SB0[:], sb_in); ld(WB0[:], wb_in); ld(N1[:], n1_in); ld(N2[:], n2_in)
        ld(DDt[:], dd_in); ld(MASK[:], mask_in); ld(IDF[:], ident_in)
        S.finalize_key("const")
        C = [b_const]

        def dve(fn, reads, writes):
            S.add('dve', fn, reads=reads, writes=writes)

        def act(fn, reads, writes):
            S.add('act', fn, reads=reads, writes=writes)

        dve(lambda e: e.tensor_scalar(out=WB0[0:64], in0=WB0[0:64], scalar1=-1.0, scalar2=None, op0=ALU.mult),
            C, [bT["WB0"]])
        dve(lambda e: e.tensor_scalar(out=N1[64:128], in0=N1[64:128], scalar1=-1.0, scalar2=None, op0=ALU.mult),
            C, [bT["N1"]])
        dve(lambda e: e.tensor_scalar(out=N2[:], in0=N2[:], scalar1=-1.0, scalar2=None, op0=ALU.mult),
            C, [bT["N2"]])
        act(lambda e: e.activation(out=DT[:], in_=LDT[:], func=AF.Exp), C, [bT["DT"]])
        dve(lambda e: e.tensor_tensor(out=DLR[:], in0=DT[:], in1=LR[:], op=ALU.mult), C + [bT["DT"]], [bT["DLR"]])
        dve(lambda e: e.tensor_tensor(out=DLI[:], in0=DT[:], in1=LI[:], op=ALU.mult), C + [bT["DT"]], [bT["DLI"]])
        for k in range(1, 9):
            act(lambda e, k=k: e.activation(out=MAG[:, k], in_=DLR[:], func=AF.Exp, scale=float(k)),
                [bT["DLR"]], [bT["MAG"]])
            act(lambda e, k=k: e.activation(out=MAGn[:, k], in_=DLR[:], func=AF.Exp, scale=float(-k)),
                [bT["DLR"]], [bT["MAGn"]])
            for (RR, off) in ((R1, 0.0), (R2, math.pi / 2)):
                nm = "R1" if RR is R1 else "R2"
                dve(lambda e, k=k, RR=RR, off=off: e.tensor_scalar(out=RR[:, k], in0=DLI[:], scalar1=float(k), scalar2=off,
                                                                  op0=ALU.mult, op1=ALU.add), [bT["DLI"]], [bT[nm]])
                dve(lambda e, k=k, RR=RR: e.tensor_scalar(out=QA[:], in0=RR[:, k], scalar1=1.0 / TWO_PI, scalar2=None,
                                                          op0=ALU.mult), [bT[nm]], [bT["QA"]])
                dve(lambda e: e.tensor_copy(out=NI[:], in_=QA[:]), [bT["QA"]], [bT["QB"]])
                dve(lambda e: e.tensor_copy(out=QA[:], in_=NI[:]), [bT["QB"]], [bT["QA"]])
                dve(lambda e, k=k, RR=RR: e.scalar_tensor_tensor(out=RR[:, k], in0=QA[:], scalar=-TWO_PI, in1=RR[:, k],
                                                                 op0=ALU.mult, op1=ALU.add), [bT["QA"]], [bT[nm]])
                dve(lambda e, k=k, RR=RR: e.tensor_scalar(out=QA[:], in0=RR[:, k], scalar1=math.pi, scalar2=-TWO_PI,
                                                          op0=ALU.is_gt, op1=ALU.mult), [bT[nm]], [bT["QA"]])
                dve(lambda e, k=k, RR=RR: e.tensor_tensor(out=RR[:, k], in0=RR[:, k], in1=QA[:], op=ALU.add),
                    [bT["QA"]], [bT[nm]])
        for k in range(1, 9):
            act(lambda e, k=k: e.activation(out=SN[:, k], in_=R1[:, k], func=AF.Sin), [bT["R1"]], [bT["SN"]])
            act(lambda e, k=k: e.activation(out=CS[:, k], in_=R2[:, k], func=AF.Sin), [bT["R2"]], [bT["CS"]])
        PY = sb("s_py", [128, 64]); PY2 = sb("s_py2", [128, 64]); PP = sb("s_pp", [128, 64])
        PSn = sb("s_psn", [128, 64]); PCs = sb("s_pcs", [128, 64]); PT = sb("s_pt", [128, 64])
        bP = {n: Buf(n) for n in ["PY", "PY2", "PP", "PSn", "PCs", "PT"]}
        for k in (8, 4):
            dve(lambda e, k=k: e.tensor_scalar(out=PY[:], in0=R1[:, k], scalar1=1.0 / 16, scalar2=None, op0=ALU.mult),
                [bT["R1"]], [bP["PY"]])
            dve(lambda e: e.tensor_tensor(out=PY2[:], in0=PY[:], in1=PY[:], op=ALU.mult), [bP["PY"]], [bP["PY2"]])
            dve(lambda e: e.tensor_scalar(out=PP[:], in0=PY2[:], scalar1=-1.0 / 5040, scalar2=None, op0=ALU.mult),
                [bP["PY2"]], [bP["PP"]])
            for cc in (1.0 / 120, -1.0 / 6):
                dve(lambda e, cc=cc: e.scalar_tensor_tensor(out=PP[:], in0=PP[:], scalar=cc, in1=PY2[:], op0=ALU.add, op1=ALU.mult),
                    [bP["PY2"]], [bP["PP"]])
            dve(lambda e: e.scalar_tensor_tensor(out=PSn[:], in0=PP[:], scalar=1.0, in1=PY[:], op0=ALU.add, op1=ALU.mult),
                [bP["PP"], bP["PY"]], [bP["PSn"]])
            dve(lambda e: e.tensor_scalar(out=PP[:], in0=PY2[:], scalar1=1.0 / 40320, scalar2=None, op0=ALU.mult),
                [bP["PY2"]], [bP["PP"]])
            for cc in (-1.0 / 720, 1.0 / 24, -0.5):
                dve(lambda e, cc=cc: e.scalar_tensor_tensor(out=PP[:], in0=PP[:], scalar=cc, in1=PY2[:], op0=ALU.add, op1=ALU.mult),
                    [bP["PY2"]], [bP["PP"]])
            dve(lambda e: e.tensor_scalar(out=PCs[:], in0=PP[:], scalar1=1.0, scalar2=None, op0=ALU.add),
                [bP["PP"]], [bP["PCs"]])
            for _ in range(4):
                dve(lambda e: e.tensor_tensor(out=PT[:], in0=PSn[:], in1=PSn[:], op=ALU.mult), [bP["PSn"]], [bP["PT"]])
                dve(lambda e: e.tensor_tensor(out=PSn[:], in0=PSn[:], in1=PCs[:], op=ALU.mult), [bP["PCs"], bP["PT"]], [bP["PSn"]])
                dve(lambda e: e.tensor_scalar(out=PSn[:], in0=PSn[:], scalar1=2.0, scalar2=None, op0=ALU.mult), [], [bP["PSn"]])
                dve(lambda e: e.tensor_scalar(out=PCs[:], in0=PT[:], scalar1=-2.0, scalar2=1.0, op0=ALU.mult, op1=ALU.add),
                    [bP["PT"], bP["PSn"]], [bP["PCs"]])
            dve(lambda e, k=k: e.tensor_copy(out=SN[:, k], in_=PSn[:]), [bP["PSn"]], [bT["SN"]])
            dve(lambda e, k=k: e.tensor_copy(out=CS[:, k], in_=PCs[:]), [bP["PCs"]], [bT["CS"]])
        for k in range(1, 9):
            dve(lambda e, k=k: e.scalar_tensor_tensor(out=TR[:, k], in0=CS[:, k], scalar=1.0, in1=MAG[:, k],
                                                      op0=ALU.mult, op1=ALU.mult), [bT["CS"], bT["MAG"]], [bT["TR"]])
            dve(lambda e, k=k: e.scalar_tensor_tensor(out=TI[:, k], in0=SN[:, k], scalar=1.0, in1=MAG[:, k],
                                                      op0=ALU.mult, op1=ALU.mult), [bT["SN"], bT["MAG"]], [bT["TI"]])
            dve(lambda e, k=k: e.scalar_tensor_tensor(out=TRn[:, k], in0=CS[:, k], scalar=1.0, in1=MAGn[:, k],
                                                      op0=ALU.mult, op1=ALU.mult), [bT["CS"], bT["MAGn"]], [bT["TRn"]])
            dve(lambda e, k=k: e.scalar_tensor_tensor(out=TIn[:, k], in0=SN[:, k], scalar=-1.0, in1=MAGn[:, k],
                                                      op0=ALU.mult, op1=ALU.mult), [bT["SN"], bT["MAGn"]], [bT["TIn"]])
        dve(lambda e: e.tensor_scalar(out=QA[:], in0=TR[:, 1], scalar1=-1.0, scalar2=None, op0=ALU.add),
            [bT["TR"]], [bT["QA"]])
        dve(lambda e: e.tensor_tensor(out=QB[:], in0=LR[:], in1=LR[:], op=ALU.mult), C, [bT["QB"]])
        dve(lambda e: e.tensor_tensor(out=QC[:], in0=LI[:], in1=LI[:], op=ALU.mult), C, [bT["QC"]])
        dve(lambda e: e.tensor_tensor(out=QB[:], in0=QB[:], in1=QC[:], op=ALU.add), [bT["QB"], bT["QC"]], [bT["QB"]])
        dve(lambda e: e.reciprocal(out=QB[:], in_=QB[:]), [bT["QB"]], [bT["QB"]])
        dve(lambda e: e.tensor_tensor(out=QR[:], in0=QA[:], in1=LR[:], op=ALU.mult), [bT["QA"]] + C, [bT["QR"]])
        dve(lambda e: e.tensor_tensor(out=QC[:], in0=TI[:, 1], in1=LI[:], op=ALU.mult), [bT["TI"], bT["QC"]] + C, [bT["QC"]])
        dve(lambda e: e.tensor_tensor(out=QR[:], in0=QR[:], in1=QC[:], op=ALU.add), [bT["QR"], bT["QC"]], [bT["QR"]])
        dve(lambda e: e.tensor_tensor(out=QR[:], in0=QR[:], in1=QB[:], op=ALU.mult), [bT["QR"], bT["QB"]], [bT["QR"]])
        dve(lambda e: e.tensor_tensor(out=QI[:], in0=TI[:, 1], in1=LR[:], op=ALU.mult), [bT["TI"]] + C, [bT["QI"]])
        dve(lambda e: e.tensor_tensor(out=QC[:], in0=QA[:], in1=LI[:], op=ALU.mult), [bT["QA"], bT["QC"]] + C, [bT["QC"]])
        dve(lambda e: e.tensor_tensor(out=QI[:], in0=QI[:], in1=QC[:], op=ALU.subtract), [bT["QI"], bT["QC"]], [bT["QI"]])
        dve(lambda e: e.tensor_tensor(out=QI[:], in0=QI[:], in1=QB[:], op=ALU.mult), [bT["QI"], bT["QB"]], [bT["QI"]])

        def cmul(out_ap, ar, ai, s_ap, w_ap, sign, reads, wbuf):
            dve(lambda e: e.tensor_tensor(out=T16a[:], in0=s_ap, in1=bc_last(ar, 16), op=ALU.mult),
                reads + [bT["T16a"]], [bT["T16a"]])
            dve(lambda e: e.tensor_tensor(out=T16b[:], in0=w_ap, in1=bc_last(ai, 16), op=ALU.mult),
                reads + [bT["T16b"]], [bT["T16b"]])
            dve(lambda e: e.tensor_tensor(out=out_ap, in0=T16a[:], in1=T16b[:],
                                          op=(ALU.add if sign > 0 else ALU.subtract)),
                [bT["T16a"], bT["T16b"]], [wbuf])

        Cq = C + [bT["QR"], bT["QI"], bT["WB0"]]
        cmul(SBb[:], QR[:], QI[:], SB0[:], WB0[:], +1, Cq, bT["SBb"])
        cmul(WBb[:], QR[:], QI[:], WB0[:], SB0[:], -1, Cq, bT["WBb"])
        Cb = [bT["SBb"], bT["WBb"], bT["TR"], bT["TI"], bT["TRn"], bT["TIn"], bT["N1"], bT["N2"]]
        for s in range(8):
            cmul(BNp[:, :, s, :], TRn[:, s + 1], TIn[:, s + 1], SBb[:], WBb[:], +1, Cb, bT["BNp"])
            if s == 7:
                dve(lambda e: e.tensor_copy(out=BPs[:, :, 7, :], in_=SBb[:]), Cb, [bT["BPs"]])
                dve(lambda e: e.tensor_copy(out=WBPs[:, :, 7, :], in_=WBb[:]), Cb, [bT["WBPs"]])
            else:
                cmul(BPs[:, :, s, :], TR[:, 7 - s], TI[:, 7 - s], SBb[:], WBb[:], +1, Cb, bT["BPs"])
                cmul(WBPs[:, :, s, :], TR[:, 7 - s], TI[:, 7 - s], WBb[:], SBb[:], -1, Cb, bT["WBPs"])
            cmul(M3f[:, :, s, :], TR[:, s + 1], TI[:, s + 1], N1[:], N2[:], +1, Cb, bT["M3f"])
        cnt = 0
        for g0 in range(0, 64, 16):
            act(lambda e, g0=g0: e.activation(out=MO[3][:].rearrange("p g m -> p (g m)"),
                                       in_=M3f[:, g0:g0 + 16].rearrange("p g s c -> p (g s c)"), func=AF.Copy),
                [bT["M3f"]], [bMO[3]])
            for g in range(g0, g0 + 16):
                gl = g - g0
                pb = cnt % 8; cnt += 1
                S.add('pe', lambda e, g=g, pb=pb: e.matmul(PSs[pb][:, 0:128],
                                                          lhsT=BNp[:, g].rearrange("p s c -> p (s c)"),
                                                          rhs=M3f[:, g].rearrange("p s c -> p (s c)"),
                                                          start=True, stop=True),
                      reads=[bT["BNp"], bT["M3f"]], writes=[bPS[pb]])
                tm = g % 2
                dve(lambda e, pb=pb, tm=tm: e.tensor_tensor(out=TMPM[tm][:], in0=PSs[pb][:, 0:128], in1=MASK[:], op=ALU.mult),
                    C, [bPS[pb], bTMPM[tm]])
                dve(lambda e, g=g, gl=gl, tm=tm: e.scalar_tensor_tensor(out=MO[0][:, gl, :], in0=IDF[:], scalar=DDt[:, g:g + 1],
                                                                in1=TMPM[tm][:], op0=ALU.mult, op1=ALU.add),
                    C + [bTMPM[tm]], [bMO[0]])
                for (src_, bsrc, mi) in ((BPs, bT["BPs"], 1), (WBPs, bT["WBPs"], 2)):
                    pb = cnt % 8; cnt += 1
                    S.add('pe', lambda e, g=g, pb=pb, src_=src_: e.transpose(PSs[pb][:, 0:128],
                                                                          src_[:, g].rearrange("p s c -> p (s c)"), IDF[:]),
                          reads=[bsrc] + C, writes=[bPS[pb]])
                    act(lambda e, gl=gl, pb=pb, mi=mi: e.activation(out=MO[mi][:, gl, :], in_=PSs[pb][:, 0:128], func=AF.Copy),
                        [], [bPS[pb], bMO[mi]])
            for i in range(4):
                S.add('sp', lambda e, i=i, g0=g0: e.dma_start(out=ms[i][:, g0 * 128:(g0 + 16) * 128],
                                                          in_=MO[i][:].rearrange("p g m -> p (g m)")),
                      reads=[bMO[i]], dkey=f"mso{i}")
        for i, (tt, kk) in enumerate(((TR, 8), (TI, 8), (TRn, 4), (TIn, 4))):
            S.add('sp', lambda e, i=i, tt=tt, kk=kk: e.dma_start(out=tb[i], in_=tt[:, kk]),
                  reads=[bT["TR"], bT["TI"], bT["TRn"], bT["TIn"]], dkey="mso")
        S.emit(nc, "s", sem_es)

    if stage == "setup":
        return nc
    with ExitStack() as es:
        S = Sched()

        def sb(name, shape, dt=F32):
            return es.enter_context(nc.sbuf_tensor(name, list(shape), dt))

        X = sb("X", [128, 16, NT]); XN = sb("XN", [128, 16, NT], BF16)
        H = sb("H", [128, 11, NT], BF16)
        NSLAB = 7
        SL = sb("SL", [128, NSLAB, 4096], BF16)
        USS = sb("USS", [128, 8, NT], BF16)
        ST = sb("ST", [128, 8, NT], BF16)
        AO = sb("AO", [128, 8, NT], BF16)
        MG = sb("MG", [128, 16, NT], BF16)
        ZC = sb("ZC", [128, 8, 8, 16], BF16); ZS = sb("ZS", [128, 8, 8, 16], BF16)
        U = sb("U", [128, 64, NCOL], BF16); WG = sb("WG", [128, 64, NCOL], BF16)
        GB = sb("GB", [128, 64, NCOL + 1]); HB = sb("HB", [128, 64, NCOL], BF16)
        Wst = [sb(f"Wst{i}", [128, 64]) for i in range(2)]
        T1 = sb("T1", [128, 64]); T2 = sb("T2", [128, 64]); U1 = sb("U1", [128, 64]); U2 = sb("U2", [128, 64])
        SH0 = sb("SH0", [128, 64, NSL]); WH0 = sb("WH0", [128, 64, NSL])
        HPS = sb("HPS", [128, 64, NSL]); HPW = sb("HPW", [128, 64, NSL])
        T6a = sb("T6a", [128, 64, NSL]); T6b = sb("T6b", [128, 64, NSL])
        PW = SMP0 + NSL * 19
        UP = sb("UP", [128, 2, PW]); WA = sb("WA", [128, 2, PW]); WB = sb("WB", [128, 2, PW])
        Dm = sb("Dm", [128, 2, NT], BF16)
        SQ = sb("SQ", [128, 2, NT], BF16); RS = sb("RS", [128, NT])
        TMP = sb("TMP", [128, 2, NT])
        OST = sb("OST", [128, 2, 368])
        GE = sb("GE", [128, 2, 512]); GE2 = sb("GE2", [128, 2, 512])
        IDF = sb("IDF", [128, 128]); IDB = sb("IDB", [128, 128], BF16); ONES = sb("ONES", [128, 128], BF16)
        GN = sb("GN", [128, 4, 16]); BGt = sb("BGt", [128, 32]); GLBt = sb("GLBt", [128, 8]); PSC = sb("PSC", [128, 8])
        WSR = sb("WSR", [128, 8])
        TBL = sb("TBL", [128, 4, 64]); SPO = sb("SPO", [128, 64]); EPS = sb("EPS", [128, 2])
        PS = [es.enter_context(nc.psum_tensor(f"ps{i}", [128, 512], F32)) for i in range(8)]
        PSB = [PS[i][:, :].bitcast(BF16) for i in range(8)]

        bconst = Buf("const")
        bX = [Buf(f"X{i}") for i in range(16)]; bXN = [Buf(f"XN{i}") for i in range(16)]
        bH = [Buf(f"H{i}") for i in range(11)]
        bSL = [Buf(f"SL{i}") for i in range(NSLAB)]
        bUSS = [Buf(f"USS{i}") for i in range(8)]; bST = [Buf(f"ST{i}") for i in range(8)]
        bAO = [Buf(f"AO{i}") for i in range(8)]; bMG = [Buf(f"MG{i}") for i in range(16)]
        bZC = Buf("ZC"); bZS = Buf("ZS"); bU = Buf("U"); bWG = Buf("WG"); bGB = Buf("GB"); bHB = Buf("HB")
        bW = [Buf("W0"), Buf("W1")]; bT1 = Buf("T1"); bT2 = Buf("T2"); bU1 = Buf("U1"); bU2 = Buf("U2")
        bSH0 = Buf("SH0"); bHPS = Buf("HPS"); bHPW = Buf("HPW"); bT6a = Buf("T6a"); bT6b = Buf("T6b")
        bUP = Buf("UP"); bWA = Buf("WA"); bWB = Buf("WB"); bD = Buf("D")
        bSQ = [Buf("SQ0"), Buf("SQ1")]; bRS = Buf("RS"); bTMP = [Buf("TMP0"), Buf("TMP1")]
        bOST = [Buf("OST0"), Buf("OST1")]
        bG0 = Buf("G0"); bG1 = Buf("G1"); bG2 = Buf("G2"); bG3 = Buf("G3")
        bPS = [Buf(f"ps{i}") for i in range(8)]
        st = {"ps": 0, "slab": 0, "ge": 0, "cv": 0}
        cw = {"a": 0, "b": NT}

        def nps():
            b = st["ps"] % 8; st["ps"] += 1
            return b

        def dve(fn, reads, writes):
            S.add('dve', fn, reads=reads, writes=writes)

        def act(fn, reads, writes):
            S.add('act', fn, reads=reads, writes=writes)

        def pe(fn, reads, writes):
            S.add('pe', fn, reads=reads, writes=writes)

        def ldc(dst, src):
            bconst.lw = S.add('sp', lambda e: e.dma_start(out=dst, in_=src), dkey="const")
        ldc(IDF[:], ident_in); ldc(GN[:], gains); ldc(BGt[:], bgate); ldc(GLBt[:], glub); ldc(PSC[:], pscale)
        ldc(TBL[:], tb.rearrange("i p g -> p i g"))
        S.finalize_key("const")
        C = [bconst]
        bIDB = Buf("idb")
        S.add('pool', lambda e: e.dma_start(out=IDB[:], in_=ident_in), writes=[bIDB], dkey="constb")
        C = [bconst, bIDB]
        bONES = Buf("ones2")
        dve(lambda e: e.memset(ONES[:], 1.0), [], [bONES])
        bEPS = Buf("eps")
        dve(lambda e: e.memset(EPS[:], 1e-6), [], [bEPS])
        dve(lambda e: e.memset(ZS[:], 0.0), [], [bZS])
        dve(lambda e: e.memset(GB[:], 0.0), [], [bGB])
        dve(lambda e: e.memset(Wst[0][:], 0.0), [], [bW[0]])
        dve(lambda e: e.memset(Dm[:], 0.0), [], [bD])
        A8r = TBL[:, 0]; A8i = TBL[:, 1]; Am4r = TBL[:, 2]; Am4i = TBL[:, 3]

        wcache = {}

        class SV:
            def __init__(self, ap, tiled):
                self.ap = ap; self.tiled = tiled

            def w(self, k, m):
                return self.ap[:, m, k, :] if self.tiled else self.ap[:, k, m * 128:(m + 1) * 128]

        def load_slab(src_ap, kt, ncols, ckey=None):
            si = st["slab"] % 5; st["slab"] += 1
            n = kt * ncols
            nm = ncols // 128
            tiled = isinstance(src_ap, tuple)
            if tiled:
                view = SV(SL[:, si, 0:n].rearrange("p (m k c) -> p m k c", m=nm, k=kt), True)
            else:
                view = SV(SL[:, si, 0:n].rearrange("p (k m) -> p k m", k=kt), False)
            if ckey is not None and ckey in wcache:
                ci, bc = wcache[ckey]
                S.add('sp', lambda e: e.dma_start(out=SL[:, si, 0:n], in_=WC[ci][:, 0:n]),
                      reads=[bc], writes=[bSL[si]], dkey=f"slabh{si}")
                return view, bSL[si]
            if tiled:
                _, wt, mt0, k0 = src_ap
                S.add('pool', lambda e: e.dma_start(
                    out=SL[:, si, 0:n].rearrange("p (m r) -> p m r", m=nm),
                    in_=wt[mt0:mt0 + nm, :, k0 * 128:(k0 + kt) * 128].rearrange("m p r -> p m r")),
                    writes=[bSL[si]], dkey=f"slab{si}")
            else:
                S.add('pool', lambda e: e.dma_start(out=view.ap, in_=src_ap.rearrange("(k p) m -> p k m", p=128)),
                      writes=[bSL[si]], dkey=f"slab{si}")
            if ckey is not None:
                ci = len(wcache)
                bc = Buf(f"wc{ci}")
                wcache[ckey] = (ci, bc)
                S.add('sp', lambda e: e.dma_start(out=WC[ci][:, 0:n], in_=SL[:, si, 0:n]),
                      reads=[bSL[si]], writes=[bc], dkey=f"cw{ci % 8}")
            return view, bSL[si]

        def convert(wt, mt0, nm, kt, ckey):
            if ckey in wcache:
                return
            si = 5 + (st["cv"] % 2); st["cv"] += 1
            n = kt * nm * 128
            S.add('pool', lambda e: e.dma_start(
                out=SL[:, si, 0:n].rearrange("p (m r) -> p m r", m=nm),
                in_=wt[mt0:mt0 + nm, :, 0:kt * 128].rearrange("m p r -> p m r")),
                writes=[bSL[si]], dkey=f"slab{si}")
            ci = len(wcache)
            bc = Buf(f"wc{ci}")
            wcache[ckey] = (ci, bc)
            S.add('pool', lambda e: e.dma_start(out=WC[ci][:, 0:n], in_=SL[:, si, 0:n]),
                  reads=[bSL[si]], writes=[bc], dkey=f"cvw{si}")

        def convert_chunk(part):
            ents = []
            for f0 in range(0, 44, 11):
                fl0 = 0
                while fl0 < 11:
                    nm = min(2, 11 - fl0)
                    f = f0 + fl0
                    ents.append((w_g2, f, nm, ('f2g', f)))
                    ents.append((w_u2, f, nm, ('f2u', f)))
                    fl0 += nm
            n3 = (len(ents) + 2) // 3
            for (wt, f, nm, key) in ents[part * n3:(part + 1) * n3]:
                convert(wt, f, nm, 16, key)

        def load_m(mi):
            halves = []
            for hf in range(2):
                si = st["slab"] % 5; st["slab"] += 1
                S.add('sp', lambda e, mi=mi, si=si, hf=hf: e.dma_start(out=SL[:, si, :], in_=ms[mi][:, hf * 4096:(hf + 1) * 4096]),
                      writes=[bSL[si]], dkey=f"slabh{si}")
                halves.append((SL[:, si, :].rearrange("p (g m) -> p g m", g=32), bSL[si]))
            return halves

        def linear(src, bsrc, kt, w_ap, col0, n_mt, epi, krow0=0, cname=None, tiled=False):
            a, b = cw["a"], cw["b"]
            mt = 0
            while mt < n_mt:
                nm = min(2, n_mt - mt)
                view, bs = load_slab(("t", w_ap, col0 // 128 + mt, krow0 // 128) if tiled else
                                     w_ap[krow0:krow0 + kt * 128, col0 + mt * 128: col0 + (mt + nm) * 128], kt, nm * 128,
                                     ckey=(cname, krow0, col0 + mt * 128) if cname else None)
                for m in range(nm):
                    pb = nps()
                    for k in range(kt):
                        pe(lambda e, pb=pb, k=k, m=m, view=view: e.matmul(
                            PS[pb][:, a:b], lhsT=view.w(k, m), rhs=src[:, k, a:b],
                            start=(k == 0), stop=(k == kt - 1)),
                           [bs, bsrc[k]], [bPS[pb]])
                    epi(mt + m, pb)
                mt += nm

        def rmsnorm(gi):
            a, b = cw["a"], cw["b"]
            pb = nps()
            for k in range(16):
                q = k % 2
                act(lambda e, k=k, q=q: e.activation(out=SQ[:, q, a:b], in_=X[:, k, a:b], func=AF.Square),
                    [bX[k]], [bSQ[q]])
                pe(lambda e, k=k, q=q, pb=pb: e.matmul(PS[pb][:, a:b], lhsT=ONES[:], rhs=SQ[:, q, a:b],
                                                       start=(k == 0), stop=(k == 15)),
                   [bSQ[q], bONES], [bPS[pb]])
            act(lambda e, pb=pb: e.activation(out=RS[:, a:b], in_=PS[pb][:, a:b], func=AF.Sqrt, bias=EPS[:, 0:1], scale=1.0 / D),
                [bEPS], [bPS[pb], bRS])
            dve(lambda e: e.reciprocal(out=RS[:, a:b], in_=RS[:, a:b]), [], [bRS])
            return pb

        def norm_to_xn(gi):
            a, b = cw["a"], cw["b"]
            rmsnorm(gi)
            for k in range(16):
                dve(lambda e, k=k: e.scalar_tensor_tensor(out=XN[:, k, a:b], in0=X[:, k, a:b], scalar=GN[:, gi, k:k + 1],
                                                          in1=RS[:, a:b], op0=ALU.mult, op1=ALU.mult),
                    [bX[k], bRS] + C, [bXN[k]])

        def ffn(gi, wg, wu, wd, nm_):
            a, b = cw["a"], cw["b"]
            norm_to_xn(gi)
            for f0 in range(0, 44, 11):
                fl0 = 0
                while fl0 < 11:
                    nm = min(2, 11 - fl0)
                    f = f0 + fl0
                    vg, bg_ = load_slab(("t", wg, f, 0), 16, nm * 128, ckey=(nm_ + 'g', f))
                    vu, bu_ = load_slab(("t", wu, f, 0), 16, nm * 128, ckey=(nm_ + 'u', f))
                    for m in range(nm):
                        fl = fl0 + m
                        pg = nps(); pu = nps()
                        for k in range(16):
                            pe(lambda e, k=k, pg=pg, vg=vg, m=m: e.matmul(PS[pg][:, a:b], lhsT=vg.w(k, m),
                                                                          rhs=XN[:, k, a:b], start=(k == 0), stop=(k == 15)),
                               [bg_, bXN[k]], [bPS[pg]])
                        for k in range(16):
                            pe(lambda e, k=k, pu=pu, vu=vu, m=m: e.matmul(PS[pu][:, a:b], lhsT=vu.w(k, m),
                                                                          rhs=XN[:, k, a:b], start=(k == 0), stop=(k == 15)),
                               [bu_, bXN[k]], [bPS[pu]])
                        q = fl % 2
                        act(lambda e, pg=pg, q=q: e.activation(out=TMP[:, q, a:b], in_=PS[pg][:, a:b], func=AF.Silu),
                            [], [bPS[pg], bTMP[q]])
                        dve(lambda e, pu=pu, q=q, fl=fl: e.tensor_tensor(out=H[:, fl, a:b], in0=PS[pu][:, a:b],
                                                                         in1=TMP[:, q, a:b], op=ALU.mult),
                            [bTMP[q]], [bPS[pu], bH[fl]])
                    fl0 += nm

                def epi(mt, pb):
                    dve(lambda e: e.scalar_tensor_tensor(out=X[:, mt, a:b], in0=PS[pb][:, a:b], scalar=0.5,
                                                         in1=X[:, mt, a:b], op0=ALU.mult, op1=ALU.add),
                        [], [bPS[pb], bX[mt]])
                linear(H, bH, 11, wd, 0, 16, epi, krow0=f0 * 128, cname=nm_ + 'd', tiled=True)

        def s5_front(with_samples):
            dve(lambda e: e.memset(ZS[:, :, 0:4, :], 0.0), [], [bZS])
            for ft in range(8):
                pb = nps()
                for s in range(8):
                    pe(lambda e, ft=ft, s=s, pb=pb: e.transpose(
                        PSB[pb][0:NCH, s * 128:(s + 1) * 128],
                        USS[:, ft, OWN0 + s: OWN0 + s + 8 * (NCH - 1) + 1: 8], IDB[:]),
                       [bUSS[ft]] + C, [bPS[pb]])
                act(lambda e, pb=pb: e.activation(
                    out=ZC[0:NCH].rearrange("p g s c -> p s g c"),
                    in_=PSB[pb][0:NCH, 0:1024].rearrange("p (s g c) -> p s g c", s=8, g=8), func=AF.Copy),
                    [], [bPS[pb], bZC])
                if with_samples:
                    pb = nps()
                    for sq in range(4):
                        pe(lambda e, ft=ft, sq=sq, pb=pb: e.transpose(
                            PSB[pb][0:NSL, sq * 128:(sq + 1) * 128],
                            USS[:, ft, SMP0 + sq: SMP0 + sq + 4 * (NSL - 1) + 1: 4], IDB[:]),
                           [bUSS[ft]] + C, [bPS[pb]])
                    act(lambda e, pb=pb: e.activation(
                        out=ZS[0:NSL, :, 4:8, :].rearrange("p g s c -> p s g c"),
                        in_=PSB[pb][0:NSL, 0:512].rearrange("p (s g c) -> p s g c", s=4, g=8), func=AF.Copy),
                        [], [bPS[pb], bZS])
                pb = nps()
                for gl in range(8):
                    pe(lambda e, gl=gl, pb=pb: e.transpose(PSB[pb][:, gl * 52: gl * 52 + NCH],
                                                           ZC[0:NCH, gl].rearrange("p s c -> p (s c)"), IDB[0:NCH, 0:NCH]),
                       [bZC] + C, [bPS[pb]])
                    pe(lambda e, gl=gl, pb=pb: e.transpose(PSB[pb][:, gl * 52 + 44: gl * 52 + 44 + NSL],
                                                           ZS[0:NSL, gl].rearrange("p s c -> p (s c)"), IDB[0:NSL, 0:NSL]),
                       [bZS] + C, [bPS[pb]])
                act(lambda e, ft=ft, pb=pb: e.activation(
                    out=U[:, ft * 8:(ft + 1) * 8, 0:NCH],
                    in_=PSB[pb][:, 0:8 * 52].rearrange("p (g j) -> p g j", g=8)[:, :, 0:NCH], func=AF.Copy),
                    [], [bPS[pb], bU])
                act(lambda e, ft=ft, pb=pb: e.activation(
                    out=U[:, ft * 8:(ft + 1) * 8, NCH:NCOL],
                    in_=PSB[pb][:, 0:8 * 52].rearrange("p (g j) -> p g j", g=8)[:, :, 44:44 + NSL], func=AF.Copy),
                    [], [bPS[pb], bU])
            for (mi, dst_is_gb) in ((1, True), (2, False)):
                halves = load_m(mi)
                for g0 in range(0, 64, 8):
                    mv, bmv = halves[g0 // 32]
                    pb = nps()
                    for gl in range(8):
                        g = g0 + gl
                        pe(lambda e, g=g, gl=gl, pb=pb, mv=mv: e.matmul(PS[pb][:, gl * NCOL:(gl + 1) * NCOL],
                                                                       lhsT=mv[:, g % 32, :], rhs=U[:, g, :],
                                                                       start=True, stop=True),
                           [bmv, bU], [bPS[pb]])
                    if dst_is_gb:
                        act(lambda e, g0=g0, pb=pb: e.activation(
                            out=GB[:, g0:g0 + 8, 1:NCOL + 1],
                            in_=PS[pb][:, 0:8 * NCOL].rearrange("p (g j) -> p g j", g=8), func=AF.Copy),
                            [], [bPS[pb], bGB])
                    else:
                        act(lambda e, g0=g0, pb=pb: e.activation(
                            out=WG[:, g0:g0 + 8, :],
                            in_=PS[pb][:, 0:8 * NCOL].rearrange("p (g j) -> p g j", g=8), func=AF.Copy),
                            [], [bPS[pb], bWG])

        wcur = {"i": 0}

        def recurrence():
            for j in range(NCH):
                wc = wcur["i"]; wn = 1 - wc
                S.add('pool', lambda e, j=j: e.tensor_tensor(out=T1[:], in0=GB[:, :, j], in1=A8r, op=ALU.mult), [bGB] + C, [bT1])
                S.add('pool', lambda e, wc=wc: e.tensor_tensor(out=T2[:], in0=Wst[wc][:], in1=A8i, op=ALU.mult), [bW[wc]] + C, [bT2])
                S.add('pool', lambda e, wc=wc: e.tensor_tensor(out=U1[:], in0=Wst[wc][:], in1=A8r, op=ALU.mult), [bW[wc]] + C, [bU1])
                S.add('pool', lambda e, j=j: e.tensor_tensor(out=U2[:], in0=GB[:, :, j], in1=A8i, op=ALU.mult), [bGB] + C, [bU2])
                S.add('pool', lambda e: e.tensor_tensor(out=T1[:], in0=T1[:], in1=T2[:], op=ALU.add), [bT2], [bT1])
                S.add('pool', lambda e: e.tensor_tensor(out=U1[:], in0=U1[:], in1=U2[:], op=ALU.subtract), [bU2], [bU1])
                S.add('pool', lambda e, j=j: e.tensor_tensor(out=GB[:, :, j + 1], in0=GB[:, :, j + 1], in1=T1[:], op=ALU.add),
                    [bT1], [bGB])
                S.add('pool', lambda e, j=j, wn=wn: e.tensor_tensor(out=Wst[wn][:], in0=WG[:, :, j], in1=U1[:], op=ALU.add),
                    [bU1, bWG], [bW[wn]])
                wcur["i"] = wn

        def carry_state():
            S.add('pool', lambda e: e.tensor_copy(out=GB[:, :, 0], in_=GB[:, :, NCH]), [], [bGB])

        def load_x(ti):
            S.add('sp', lambda e: e.dma_start(out=X[:], in_=xin[ti].rearrange("(k p) n -> p k n", p=128)),
                  writes=bX, dkey="xin")

        def uss_from_win():
            a, b = cw["a"], cw["b"]
            def epi(mt, pb):
                act(lambda e: e.activation(out=USS[:, mt, a:b], in_=PS[pb][:, a:b], func=AF.Copy),
                    [], [bPS[pb], bUSS[mt]])
            linear(XN, bXN, 16, w_in, 1024, 8, epi, cname='win', tiled=True)

        import os as _os
        NOCC = bool(_os.environ.get("KSIM_NOCC"))
        bXS = [Buf(f"XS{i}") for i in range(3)]; bUS = [Buf(f"US{i}") for i in range(3)]
        bGS = [Buf(f"GS{i}") for i in range(3)]; bWS = [Buf(f"WS{i}") for i in range(3)]
        bCCI = Buf("cci"); bCCO = Buf("cco")

        def front(ti):
            load_x(ti)
            ffn(0, w_g1, w_u1, w_d1, 'f1')
            S.add('sp', lambda e: e.dma_start(out=XS[ti], in_=X[:].rearrange("p k n -> p (k n)")),
                  reads=bX, writes=[bXS[ti]], dkey="sx")
            norm_to_xn(1)
            uss_from_win()
            s5_front(True)
            S.add('sp', lambda e: e.dma_start(out=US[ti], in_=U[:].rearrange("p g j -> p (g j)")),
                  reads=[bU], writes=[bUS[ti]], dkey="su")
            S.add('sp', lambda e: e.dma_start(out=GS[ti].rearrange("p (g j) -> p g j", g=64), in_=GB[:, :, 1:NCOL + 1]),
                  reads=[bGB], writes=[bGS[ti]], dkey="sg")
            S.add('sp', lambda e: e.dma_start(out=WS_[ti], in_=WG[:].rearrange("p g j -> p (g j)")),
                  reads=[bWG], writes=[bWS[ti]], dkey="sw")
            recurrence()
            carry_state()

        def exchange():
            wc = wcur["i"]
            CBv = GE[:, 0, :].rearrange("p (s f) -> p s f", s=4)
            RBv = GE[:, 1, :].rearrange("p (s f) -> p s f", s=4)
            HI = T6a[:].rearrange("p g i -> p (g i)")[:, 0:128]
            for s in range(4):
                dve(lambda e, s=s: e.tensor_scalar(out=CBv[:, s, 0:64], in0=GB[:, :, 0], scalar1=WSR[:, s:s + 1], scalar2=None,
                                                  op0=ALU.mult), [bGB] + C, [bG0])
                dve(lambda e, s=s: e.tensor_scalar(out=CBv[:, s, 64:128], in0=Wst[wc][:], scalar1=WSR[:, s:s + 1], scalar2=None,
                                                  op0=ALU.mult), [bW[wc]] + C, [bG0])
            S.add('sp', lambda e: e.dma_start(out=cc_in[:, :], in_=GE[:, 0, :]), reads=[bG0], writes=[bCCI], dkey="cci")
            if NOCC:
                S.add('pool', lambda e: e.dma_start(out=cc_out[:, :], in_=cc_in[:, :]), reads=[bCCI], writes=[bCCO], dkey="cc")
            else:
                S.add('pool', lambda e: e.collective_compute("AllReduce", ALU.add, replica_groups=[list(range(8))],
                                                             ins=[cc_in.ap().opt()], outs=[cc_out.ap().opt()]),
                      reads=[bCCI], writes=[bCCO], dkey="cc", inc=1)
            S.add('sp', lambda e: e.dma_start(out=GE[:, 1, :], in_=cc_out[:, :]), reads=[bCCO], writes=[bG1], dkey="cco")
            dve(lambda e: e.tensor_scalar(out=HI, in0=RBv[:, 0, :], scalar1=WSR[:, 4:5], scalar2=None, op0=ALU.mult),
                [bG1] + C, [bT6a])
            for s in range(1, 4):
                dve(lambda e, s=s: e.scalar_tensor_tensor(out=HI, in0=RBv[:, s, :], scalar=WSR[:, 4 + s:5 + s], in1=HI,
                                                          op0=ALU.mult, op1=ALU.add), [bG1] + C, [bT6a])
            dve(lambda e: e.tensor_copy(out=GB[:, :, 0], in_=HI[:, 0:64]), [bT6a], [bGB])
            dve(lambda e: e.tensor_copy(out=Wst[wc][:], in_=HI[:, 64:128]), [bT6a], [bW[wc]])

        def prefix(ti):
            cw["a"], cw["b"] = OWN0, SMP0
            load_x(ti)
            ffn(0, w_g1, w_u1, w_d1, 'f1')
            norm_to_xn(1)
            uss_from_win()
            s5_front(False)
            recurrence()
            carry_state()
            cw["a"], cw["b"] = 0, NT
            convert_chunk(ti)

        for ti in range(3):
            prefix(ti)

        def back(ti):
            load_x(3 + ti)
            S.add('sp', lambda e: e.dma_start(out=SH0[:], in_=sh0_in[ti]), writes=[bSH0], dkey="h0")
            S.add('sp', lambda e: e.dma_start(out=WH0[:], in_=wh0_in[ti]), writes=[bSH0], dkey="h0")
            ffn(0, w_g1, w_u1, w_d1, 'f1')
            norm_to_xn(1)
            uss_from_win()
            s5_front(True)
            recurrence()
            for pg in range(4):
                w = (2, 4, 8, 16)[pg]

                def epi_up(mt, pb, pg=pg):
                    q = mt - 2 * pg
                    act(lambda e: e.activation(out=GE[:, q, 0:NT], in_=PS[pb][:, 0:NT], func=AF.Copy),
                        [], [bPS[pb], (bG0, bG1)[q]])
                    dve(lambda e: e.tensor_copy(out=UP[:, q, 0:SMP0], in_=GE[:, q, 0:SMP0]), [(bG0, bG1)[q]], [bUP])
                    dve(lambda e: e.tensor_copy(
                        out=UP[:, q, SMP0:PW].rearrange("p (i h) -> p i h", h=19)[:, :, 15:19],
                        in_=GE[:, q, SMP0:NT].rearrange("p (i t) -> p i t", t=4)), [(bG0, bG1)[q]], [bUP])
                def lin2():
                    view, bs = load_slab(("t", w_in, 2 * pg, 0), 16, 256, ckey=('winp', pg))
                    for m in range(2):
                        pb = nps()
                        for k in range(16):
                            pe(lambda e, pb=pb, k=k, m=m: e.matmul(PS[pb][:, 0:NT], lhsT=view.w(k, m),
                                                                   rhs=XN[:, k, :], start=(k == 0), stop=(k == 15)),
                               [bs, bXN[k]], [bPS[pb]])
                        epi_up(2 * pg + m, pb)
                lin2()
                for q in range(2):
                    S.add('sp', lambda e, ti=ti, pg=pg, q=q: e.dma_start(
                        out=UP[:, q, SMP0:PW].rearrange("p (i h) -> p i h", h=19)[:, :, 0:15],
                        in_=hist_in[ti, pg * 256 + q * 128: pg * 256 + (q + 1) * 128]),
                        writes=[bUP], dkey="hist")
                if pg == 0:
                    pass
                dve(lambda e: e.tensor_tensor(out=WA[:, :, 1:PW], in0=UP[:, :, 1:PW], in1=UP[:, :, 0:PW - 1], op=ALU.add),
                    [bUP], [bWA])
                cur, bcur, oth, both = WA, bWA, WB, bWB
                k = 2
                while k < w:
                    dve(lambda e, cur=cur, oth=oth, k=k: e.tensor_tensor(out=oth[:, :, 2 * k - 1:PW], in0=cur[:, :, 2 * k - 1:PW],
                                                                          in1=cur[:, :, k - 1:PW - k], op=ALU.add),
                        [bcur], [both])
                    cur, bcur, oth, both = oth, both, cur, bcur
                    k *= 2
                dve(lambda e, cur=cur, w=w: e.scalar_tensor_tensor(out=Dm[:, :, OWN0:SMP0], in0=cur[:, :, OWN0:SMP0],
                                                                   scalar=1.0 / w, in1=UP[:, :, OWN0:SMP0],
                                                                   op0=ALU.mult, op1=ALU.subtract), [bcur, bUP], [bD])
                for q in range(2):
                    dve(lambda e, cur=cur, w=w, q=q: e.scalar_tensor_tensor(
                        out=Dm[:, q, SMP0:NT].rearrange("p (i t) -> p i t", t=4),
                        in0=cur[:, q, SMP0:PW].rearrange("p (i h) -> p i h", h=19)[:, :, 15:19], scalar=1.0 / w,
                        in1=UP[:, q, SMP0:PW].rearrange("p (i h) -> p i h", h=19)[:, :, 15:19],
                        op0=ALU.mult, op1=ALU.subtract), [bcur, bUP], [bD])
                for q in range(2):
                    S.add('sp', lambda e, ti=ti, pg=pg, q=q: e.dma_start(
                        out=pool_s[ti, pg * 256 + q * 128: pg * 256 + (q + 1) * 128],
                        in_=UP[:, q, SMP0:PW].rearrange("p (i h) -> p i h", h=19)[:, :, 4:19]),
                        reads=[bUP], dkey="o_pools")
                if ti == 2:
                    S.add('sp', lambda e, pg=pg: e.dma_start(
                        out=pool_p[pg * 256:(pg + 1) * 256].rearrange("(q p) h -> p q h", p=128),
                        in_=UP[:, :, SMP0 - 15:SMP0]), reads=[bUP], dkey="o_poolp")
                vw, bw = load_slab(w_pool[pg], 2, 256, ckey=('pw', pg))
                for m in range(2):
                    pb = nps()
                    for k2 in range(2):
                        pe(lambda e, pb=pb, k2=k2, m=m, vw=vw: e.matmul(PS[pb][:, 0:NT], lhsT=vw.w(k2, m),
                                                                       rhs=Dm[:, k2, :], start=(k2 == 0), stop=(k2 == 1)),
                           [bw, bD], [bPS[pb]])
                    mt = 2 * pg + m
                    dve(lambda e, pb=pb, mt=mt: e.tensor_scalar(out=AO[:, mt, :], in0=PS[pb][:, 0:NT],
                                                                scalar1=PSC[:, mt:mt + 1], scalar2=None, op0=ALU.mult),
                        C, [bPS[pb], bAO[mt]])
            for dt_ in range(16):
                va, ba = load_slab(("t", w_in, 16 + dt_, 0), 16, 128, ckey=('ga', dt_))
                vwa, bwa = load_slab(("t", w_ba, dt_, 0), 8, 128, ckey=('ba', dt_))
                pga = nps(); pa = nps()
                for k in range(16):
                    pe(lambda e, k=k, pga=pga, va=va: e.matmul(PS[pga][:, 0:NT], lhsT=va.w(k, 0), rhs=XN[:, k, :],
                                                               start=(k == 0), stop=(k == 15)), [ba, bXN[k]], [bPS[pga]])
                for k in range(8):
                    pe(lambda e, k=k, pa=pa, vwa=vwa: e.matmul(PS[pa][:, 0:NT], lhsT=vwa.w(k, 0), rhs=AO[:, k, :],
                                                               start=(k == 0), stop=(k == 7)), [bwa, bAO[k]], [bPS[pa]])
                act(lambda e, pga=pga, dt_=dt_: e.activation(out=TMP[:, 0, :], in_=PS[pga][:, 0:NT], func=AF.Sigmoid,
                                                             bias=BGt[:, dt_:dt_ + 1], scale=1.0), C, [bPS[pga], bTMP[0]])
                dve(lambda e, pa=pa, dt_=dt_: e.tensor_tensor(out=MG[:, dt_, :], in0=PS[pa][:, 0:NT], in1=TMP[:, 0, :], op=ALU.mult),
                    [bTMP[0]], [bPS[pa], bMG[dt_]])
                if dt_ < 11:
                    vb, bb_ = load_slab(("t", w_in, 32 + dt_, 0), 16, 128, ckey=('gb', dt_))
                    pgb = nps()
                    for k in range(16):
                        pe(lambda e, k=k, pgb=pgb, vb=vb: e.matmul(PS[pgb][:, 0:NT], lhsT=vb.w(k, 0), rhs=XN[:, k, :],
                                                                   start=(k == 0), stop=(k == 15)), [bb_, bXN[k]], [bPS[pgb]])
                    act(lambda e, pgb=pgb, dt_=dt_: e.activation(out=H[:, dt_, :], in_=PS[pgb][:, 0:NT], func=AF.Sigmoid,
                                                                 bias=BGt[:, 16 + dt_:17 + dt_], scale=1.0), C, [bPS[pgb], bH[dt_]])
            dve(lambda e: e.tensor_scalar(out=WH0[0:64], in0=WH0[0:64], scalar1=-1.0, scalar2=None, op0=ALU.mult),
                [], [bSH0])

            def cm6(out_ap, ar, ai, s_ap, w_ap, sign, reads, wbuf):
                dve(lambda e: e.tensor_tensor(out=T6a[:], in0=s_ap, in1=bc_last(ar, NSL), op=ALU.mult), reads + C, [bT6a])
                dve(lambda e: e.tensor_tensor(out=T6b[:], in0=w_ap, in1=bc_last(ai, NSL), op=ALU.mult), reads + C, [bT6b])
                dve(lambda e: e.tensor_tensor(out=out_ap, in0=T6a[:], in1=T6b[:],
                                              op=(ALU.add if sign > 0 else ALU.subtract)), [bT6a, bT6b], [wbuf])
            cm6(HPS[:], Am4r, Am4i, SH0[:], WH0[:], +1, [bSH0], bHPS)
            cm6(HPW[:], Am4r, Am4i, WH0[:], SH0[:], -1, [bSH0], bHPW)
            cm6(T6a[:], A8r, A8i, HPS[:], HPW[:], +1, [bHPS, bHPW], bT6a)
            dve(lambda e: e.tensor_tensor(out=GB[:, :, NCH + 1:NCOL + 1], in0=GB[:, :, NCH + 1:NCOL + 1], in1=T6a[:],
                                          op=ALU.add), [bT6a], [bGB])
            S.add('sp', lambda e, ti=ti: e.dma_start(out=ssm_s[ti], in_=GB[:, :, NCH + 1:NCOL + 1]),
                  reads=[bGB], dkey="o_ssms")
            if ti == 2:
                bSPO = Buf("SPO")
                dve(lambda e: e.tensor_copy(out=SPO[:], in_=GB[:, :, NCH]), [bGB], [bSPO])
                S.add('sp', lambda e: e.dma_start(out=ssm_p, in_=SPO[:]), reads=[bSPO], dkey="o_ssmp")
            act(lambda e: e.activation(out=HB[:, :, 0:NCH], in_=GB[:, :, 0:NCH], func=AF.Copy), [bGB], [bHB])
            act(lambda e: e.activation(out=HB[:, :, NCH:NCOL], in_=HPS[:], func=AF.Copy), [bHPS], [bHB])
            m1h = load_m(0)
            m3h = load_m(3)
            zb = ZC[:].rearrange("p g s c -> p (g s c)").rearrange("p (t f) -> p t f", t=8)
            nr = NCOL
            for ft in range(8):
                for half in range(2):
                    pb = nps()
                    for gq in range(4):
                        g = ft * 8 + half * 4 + gq
                        m1v, bm1 = m1h[g // 32]
                        m3v, bm3 = m3h[g // 32]
                        pe(lambda e, g=g, gq=gq, pb=pb, m1v=m1v: e.matmul(
                            PS[pb][0:NCOL, gq * 128:(gq + 1) * 128], lhsT=U[:, g, 0:NCOL], rhs=m1v[:, g % 32, :],
                            start=True, stop=False), [bU, bm1], [bPS[pb]])
                        pe(lambda e, g=g, gq=gq, pb=pb, m3v=m3v: e.matmul(
                            PS[pb][0:NCOL, gq * 128:(gq + 1) * 128], lhsT=HB[:, g, 0:NCOL], rhs=m3v[:, g % 32, :],
                            start=False, stop=True), [bHB, bm3], [bPS[pb]])
                    gi_ = st["ge"] % 2; st["ge"] += 1
                    GX = (GE, GE2)[gi_]; bga, bgb = ((bG0, bG1), (bG2, bG3))[gi_]
                    act(lambda e, pb=pb, GX=GX: e.activation(out=GX[0:NCOL, 0, :], in_=PS[pb][0:NCOL, 0:512], func=AF.Square),
                        [], [bPS[pb], bga])
                    dve(lambda e, GX=GX: e.tensor_scalar(out=GX[0:NCOL, 0, :], in0=GX[0:NCOL, 0, :], scalar1=0.044715, scalar2=1.0,
                                                         op0=ALU.mult, op1=ALU.add), [], [bga])
                    dve(lambda e, pb=pb, GX=GX: e.tensor_tensor(out=GX[0:NCOL, 0, :], in0=PS[pb][0:NCOL, 0:512], in1=GX[0:NCOL, 0, :],
                                                                op=ALU.mult), [], [bPS[pb], bga])
                    act(lambda e, GX=GX: e.activation(out=GX[0:NCOL, 1, :], in_=GX[0:NCOL, 0, :], func=AF.Sigmoid, scale=1.5957691),
                        [bga], [bgb])
                    dve(lambda e, pb=pb, half=half, GX=GX: e.tensor_tensor(
                        out=zb[0:NCOL, :, half * 64:(half + 1) * 64].rearrange("p t (g c) -> p g t c", g=4),
                        in0=PS[pb][0:NCOL, 0:512].rearrange("p (g t c) -> p g t c", g=4, t=8),
                        in1=GX[0:NCOL, 1, :].rearrange("p (g t c) -> p g t c", g=4, t=8), op=ALU.mult),
                        [bgb], [bPS[pb], bZC])
                pb = nps()
                for t in range(8):
                    pe(lambda e, t=t, pb=pb: e.transpose(
                        PSB[pb][:, t * 64: t * 64 + NCOL], zb[0:NCOL, t, :], IDB[0:NCOL, 0:NCOL]),
                       [bZC] + C, [bPS[pb]])
                act(lambda e, ft=ft, pb=pb: e.activation(
                    out=ST[:, ft, OWN0:OWN0 + OWN].rearrange("p (j t) -> p t j", t=8),
                    in_=PSB[pb][:, 0:512].rearrange("p (t j) -> p t j", t=8)[:, :, 0:NCH], func=AF.Copy),
                    [], [bPS[pb], bST[ft]])
                act(lambda e, ft=ft, pb=pb: e.activation(
                    out=ST[:, ft, SMP0:SMP0 + 4 * NSL].rearrange("p (i t) -> p t i", t=4),
                    in_=PSB[pb][:, 256:512].rearrange("p (t j) -> p t j", t=4)[:, :, NCH:NCOL], func=AF.Copy),
                    [], [bPS[pb], bST[ft]])
            for ft in range(8):
                dve(lambda e, ft=ft: e.memset(ST[:, ft, 0:OWN0], 0.0), [], [bST[ft]])
            def epi_glu(mt, pb):
                act(lambda e: e.activation(out=AO[:, mt, :], in_=PS[pb][:, 0:NT], func=AF.Sigmoid,
                                           bias=GLBt[:, mt:mt + 1], scale=1.0), C, [bPS[pb], bAO[mt]])
            linear(ST, bST, 8, w_glu, 0, 8, epi_glu, cname='glu')
            for mt in range(8):
                dve(lambda e, mt=mt: e.tensor_tensor(out=ST[:, mt, :], in0=ST[:, mt, :], in1=AO[:, mt, :], op=ALU.mult),
                    [bAO[mt]], [bST[mt]])
            for dt_ in range(16):
                vwb, bwb = load_slab(("t", w_bb, dt_, 0), 8, 128, ckey=('bb', dt_))
                pbb = nps()
                if dt_ >= 11:
                    vb, bb_ = load_slab(("t", w_in, 32 + dt_, 0), 16, 128, ckey=('gb', dt_))
                    pgb = nps()
                    for k in range(16):
                        pe(lambda e, k=k, pgb=pgb, vb=vb: e.matmul(PS[pgb][:, 0:NT], lhsT=vb.w(k, 0), rhs=XN[:, k, :],
                                                                   start=(k == 0), stop=(k == 15)), [bb_, bXN[k]], [bPS[pgb]])
                for k in range(8):
                    pe(lambda e, k=k, pbb=pbb, vwb=vwb: e.matmul(PS[pbb][:, 0:NT], lhsT=vwb.w(k, 0), rhs=ST[:, k, :],
                                                                 start=(k == 0), stop=(k == 7)), [bwb, bST[k]], [bPS[pbb]])
                if dt_ >= 11:
                    act(lambda e, pgb=pgb, dt_=dt_: e.activation(out=TMP[:, 1, :], in_=PS[pgb][:, 0:NT], func=AF.Sigmoid,
                                                                 bias=BGt[:, 16 + dt_:17 + dt_], scale=1.0), C, [bPS[pgb], bTMP[1]])
                    dve(lambda e, pbb=pbb: e.tensor_tensor(out=TMP[:, 1, :], in0=PS[pbb][:, 0:NT], in1=TMP[:, 1, :], op=ALU.mult),
                        [], [bPS[pbb], bTMP[1]])
                else:
                    dve(lambda e, pbb=pbb, dt_=dt_: e.tensor_tensor(out=TMP[:, 1, :], in0=PS[pbb][:, 0:NT], in1=H[:, dt_, :], op=ALU.mult),
                        [bH[dt_]], [bPS[pbb], bTMP[1]])
                dve(lambda e, dt_=dt_: e.tensor_tensor(out=MG[:, dt_, :], in0=MG[:, dt_, :], in1=TMP[:, 1, :], op=ALU.add),
                    [bTMP[1]], [bMG[dt_]])

            def epi_o(mt, pb):
                dve(lambda e: e.tensor_tensor(out=X[:, mt, :], in0=PS[pb][:, 0:NT], in1=X[:, mt, :], op=ALU.add),
                    [], [bPS[pb], bX[mt]])
            linear(MG, bMG, 16, w_o, 0, 16, epi_o, cname='wo')
            ffn(2, w_g2, w_u2, w_d2, 'f2')
            rmsnorm(3)
            for k in range(16):
                q = k % 2
                dve(lambda e, k=k, q=q: e.scalar_tensor_tensor(out=OST[:, q, :], in0=X[:, k, OWN0:NT], scalar=GN[:, 3, k:k + 1],
                                                               in1=RS[:, OWN0:NT], op0=ALU.mult, op1=ALU.mult),
                    [bX[k], bRS] + C, [bOST[q]])
                S.add('sp', lambda e, k=k, q=q, ti=ti: e.dma_start(out=yT[ti, k * 128:(k + 1) * 128, :], in_=OST[:, q, :]),
                      reads=[bOST[q]], dkey=f"oy{q}")
            carry_state()

        for ti in range(3):
            back(ti)
        S.emit(nc, "m", sem_es)
    return nc


_NC_CACHE = {}


def _tile_w(w):
    K, M = w.shape
    return np.ascontiguousarray(np.asarray(w, np.float32).reshape(K // 128, 128, M // 128, 128).transpose(2, 1, 0, 3)
                                ).reshape(M // 128, 128, K)


def _prep_shared(inp):
    f = np.float32
    sh = {}
    sh["w_g1"] = _tile_w(inp["ffn1_w_gate"][0]); sh["w_u1"] = _tile_w(inp["ffn1_w_up"][0])
    sh["w_d1"] = _tile_w(inp["ffn1_w_down"][0])
    sh["w_g2"] = _tile_w(inp["ffn2_w_gate"][0]); sh["w_u2"] = _tile_w(inp["ffn2_w_up"][0])
    sh["w_d2"] = _tile_w(inp["ffn2_w_down"][0])
    sh["w_in"] = _tile_w(inp["w_in"][0])
    sh["w_pool"] = np.ascontiguousarray(inp["pool_w"][0], f)
    sh["w_glu"] = np.ascontiguousarray(inp["glu_w"][0], f)
    sh["w_ba"] = _tile_w(inp["w_branch_a"][0]); sh["w_bb"] = _tile_w(inp["w_branch_b"][0])
    sh["w_o"] = np.ascontiguousarray(inp["w_out"][0], f)
    g = np.stack([inp["norm_ffn1"][0], inp["norm_mix"][0], inp["norm_ffn2"][0], inp["final_norm"]], 0)
    sh["gains"] = np.ascontiguousarray(g.reshape(4, 16, 128).transpose(2, 0, 1), f)
    sh["bgate"] = np.ascontiguousarray(inp["b_gate"][0].reshape(32, 128).T, f)
    sh["glub"] = np.ascontiguousarray(inp["glu_b"][0].reshape(8, 128).T, f)
    sh["pscale"] = np.ascontiguousarray(inp["pool_scale"][0].reshape(8, 128).T, f)
    lrT = inp["ssm_lambda_re"][0].T; liT = inp["ssm_lambda_im"][0].T
    sh["lr2"] = np.ascontiguousarray(np.concatenate([lrT, lrT], 0), f)
    sh["li2"] = np.ascontiguousarray(np.concatenate([liT, liT], 0), f)
    sh["ldt"] = np.ascontiguousarray(np.broadcast_to(inp["ssm_log_dt"][0][None, :], (128, 64)), f)
    br = inp["ssm_b_re"][0].transpose(1, 0, 2); bi = inp["ssm_b_im"][0].transpose(1, 0, 2)
    sh["sb_in"] = np.ascontiguousarray(np.concatenate([br, bi], 0), f)
    sh["wb_in"] = np.ascontiguousarray(np.concatenate([bi, br], 0), f)
    cr = inp["ssm_c_re"][0].transpose(2, 0, 1); ci = inp["ssm_c_im"][0].transpose(2, 0, 1)
    sh["n1_in"] = np.ascontiguousarray(np.concatenate([cr, ci], 0), f)
    sh["n2_in"] = np.ascontiguousarray(np.concatenate([ci, cr], 0), f)
    dd = inp["ssm_d"][0].reshape(64, 16)
    sh["dd_in"] = np.ascontiguousarray(np.tile(dd.T, (8, 1)), f)
    s_idx = np.arange(128) // 16
    sh["mask_in"] = (s_idx[None, :] >= s_idx[:, None]).astype(f)
    sh["ident_in"] = np.eye(128, dtype=f)
    return sh


def kernel(**inp):
    inp = {k: np.asarray(v) for k, v in inp.items()}
    f = np.float32
    if "nc" not in _NC_CACHE:
        _NC_CACHE["nc"] = build_program()
    nc = _NC_CACHE["nc"]
    sh = _prep_shared(inp)
    xp = inp["x_prompt"].astype(f); xs = inp["x_sample"].astype(f)
    meta = inp["meta_tokens"].astype(f)
    st_pool = inp["state_pool"][0]; st_re = inp["state_ssm_re"][0]; st_im = inp["state_ssm_im"][0]
    in_maps = []
    for c in range(8):
        b, r = c // 2, c % 2
        seq = np.concatenate([meta, xp[b]], 0)
        own = seq[r * 1032:(r + 1) * 1032]
        pre = seq[0:1032] if r == 1 else np.zeros((1032, D), f)
        halo = seq[1032 - 15:1032] if r == 1 else np.zeros((15, D), f)
        ownh = np.concatenate([halo, own], 0)
        xin = np.zeros((6, NT, D), f)
        hist = np.zeros((3, NSL, 15, 1024), f)
        h0r = np.zeros((3, NSL, 64, 64), f); h0i = np.zeros((3, NSL, 64, 64), f)
        for t in range(3):
            xin[t, OWN0:OWN0 + OWN] = pre[t * OWN:(t + 1) * OWN]
            xin[3 + t, 0:SMP0] = ownh[t * OWN:t * OWN + SMP0]
            for i in range(NSL):
                sl = t * NSL + i
                if sl < 16:
                    sq = 16 * c + sl
                    xin[3 + t, SMP0 + 4 * i:SMP0 + 4 * i + 4] = xs[sq]
                    hist[t, i] = st_pool[sq]
                    h0r[t, i] = st_re[sq]; h0i[t, i] = st_im[sq]
        m = dict(sh)
        m["xin"] = np.ascontiguousarray(xin.transpose(0, 2, 1))
        m["hist_in"] = np.ascontiguousarray(hist.transpose(0, 3, 1, 2))
        hr = h0r.transpose(0, 3, 2, 1); hi = h0i.transpose(0, 3, 2, 1)
        m["sh0_in"] = np.ascontiguousarray(np.concatenate([hr, hi], 1))
        m["wh0_in"] = np.ascontiguousarray(np.concatenate([hi, hr], 1))
        in_maps.append(m)
    res = run_bass_kernel_spmd(nc, in_maps, core_ids=list(range(8)))
    y_prompt = np.zeros((4, 2048, D), f); y_sample = np.zeros((128, 4, D), f)
    pool_pp = np.zeros((1, 4, 15, 1024), f); pool_ss = np.zeros((1, 128, 15, 1024), f)
    re_p = np.zeros((1, 4, 64, 64), f); im_p = np.zeros((1, 4, 64, 64), f)
    re_s = np.zeros((1, 128, 64, 64), f); im_s = np.zeros((1, 128, 64, 64), f)
    for c in range(8):
        b, r = c // 2, c % 2
        o = res.results[c]
        yT = np.asarray(o["yT"])
        yo = np.concatenate([yT[t, :, 0:OWN].T for t in range(3)], 0)
        if r == 0:
            y_prompt[b, 0:1016] = yo[16:]
        else:
            y_prompt[b, 1016:2048] = yo
            pool_pp[0, b] = np.asarray(o["pool_p"]).T
            sp = np.asarray(o["ssm_p"])
            re_p[0, b] = sp[0:64].T; im_p[0, b] = sp[64:128].T
        ps_ = np.asarray(o["pool_s"]); ss_ = np.asarray(o["ssm_s"])
        for t in range(3):
            for i in range(NSL):
                sl = t * NSL + i
                if sl < 16:
                    sq = 16 * c + sl
                    y_sample[sq] = yT[t, :, OWN + 4 * i:OWN + 4 * i + 4].T
                    pool_ss[0, sq] = ps_[t, :, i, :].T
                    re_s[0, sq] = ss_[t, 0:64, :, i].T; im_s[0, sq] = ss_[t, 64:128, :, i].T
    return (y_prompt, y_sample, pool_pp, pool_ss, re_p, im_p, re_s, im_s)
```
